# Optimizing a Trainium2 kernel written in Bass

```python
import math
import jax, jax.numpy as jnp
from jax import lax
import numpy as np

D_MODEL = 1024
BATCH = 16
SEQ = 2048
DEPTH = 2

MEM_LEN = 256
BRANCH_WIDTH = D_MODEL // 2
N_BRANCH = 3
DA_QK_DIM = 64
DA_V_DIM = 2 * DA_QK_DIM
DA_WIDTH = BRANCH_WIDTH
DA_HEADS = DA_WIDTH // DA_V_DIM
DA_QK_WIDTH = 2 * DA_HEADS * DA_QK_DIM
RW_HEAD = 64
RW_WIDTH = BRANCH_WIDTH
RW_HEADS = RW_WIDTH // RW_HEAD
RW_DECAY_LORA = 64
RW_AAA_LORA = 64
RW_SHIFT_WIDTH = 3 * RW_WIDTH + RW_DECAY_LORA + RW_AAA_LORA
MU_SPLITS = (RW_WIDTH, 2 * RW_WIDTH, 3 * RW_WIDTH, 3 * RW_WIDTH + RW_DECAY_LORA)
CA_HEADS = 4
CA_WIDTH = BRANCH_WIDTH
CA_HEAD_DIM = CA_WIDTH // CA_HEADS
ROPE_THETA = 500000.0
ROPE_FRAC = 4
Q_BLOCK = 128
NORM_EPS = 1e-6
GN_EPS = 64e-5

IN_SIZES = (
    DA_QK_WIDTH, DA_QK_WIDTH, DA_WIDTH, DA_WIDTH,
    RW_WIDTH, RW_WIDTH, RW_WIDTH, RW_DECAY_LORA, RW_AAA_LORA,
    RW_WIDTH,
    CA_WIDTH, CA_WIDTH,
    N_BRANCH * D_MODEL,
)
IN_WIDTH = sum(IN_SIZES)
IN_SPLITS = tuple(sum(IN_SIZES[:i + 1]) for i in range(len(IN_SIZES) - 1))

kernel_name = 'hybrid_diffattn_rwkv7_memory_block'


def _rms(x, g, eps=NORM_EPS):
    xf = x.astype(jnp.float32)
    y = xf * lax.rsqrt(jnp.mean(jnp.square(xf), axis=-1, keepdims=True) + eps)
    return (y * g.astype(jnp.float32)).astype(x.dtype)


def _rope_tables(seq, dim):
    rot = dim // ROPE_FRAC
    inv = 1.0 / (ROPE_THETA ** (jnp.arange(0, rot, 2, dtype=jnp.float32) / rot))
    ang = jnp.arange(seq, dtype=jnp.float32)[:, None] * inv[None, :]
    return jnp.cos(ang), jnp.sin(ang)


def _partial_rope(t, cos, sin):
    half = cos.shape[-1]
    tf = t.astype(jnp.float32)
    t1, t2, rest = tf[..., :half], tf[..., half:2 * half], tf[..., 2 * half:]
    c = cos[:, None, None, :]
    s = sin[:, None, None, :]
    out = jnp.concatenate([t1 * c - t2 * s, t2 * c + t1 * s, rest], axis=-1)
    return out.astype(t.dtype)


def _token_shift(p, mu):
    prev = jnp.pad(p, ((0, 0), (1, 0), (0, 0)))[:, :-1]
    return p + (prev - p) * mu


def _diff_attention_branch(q, k, v, z, q_g, k_g, lam_vecs, subln_g, lam_init, cos, sin):
    B, S, _ = q.shape
    q = _partial_rope(_rms(q.reshape(B, S, DA_HEADS, 2, DA_QK_DIM), q_g), cos, sin)
    k = _partial_rope(_rms(k.reshape(B, S, DA_HEADS, 2, DA_QK_DIM), k_g), cos, sin)
    v = v.reshape(B, S, DA_HEADS, DA_V_DIM)
    lv = lam_vecs.astype(jnp.float32)
    lam = jnp.exp(jnp.sum(lv[0] * lv[1])) - jnp.exp(jnp.sum(lv[2] * lv[3])) + lam_init
    scale = DA_QK_DIM ** -0.5
    outs = []
    for i in range(S // Q_BLOCK):
        s0, s1 = i * Q_BLOCK, (i + 1) * Q_BLOCK
        qb, kb, vb = q[:, s0:s1], k[:, :s1], v[:, :s1]
        sc = jnp.einsum('bqhcd,bkhcd->bhcqk', qb, kb).astype(jnp.float32) * scale
        mask = (s0 + jnp.arange(Q_BLOCK))[:, None] >= jnp.arange(s1)[None, :]
        sc = jnp.where(mask, sc, -jnp.inf)
        p = jax.nn.softmax(sc, axis=-1)
        wdiff = p[:, :, 0] - lam * p[:, :, 1]
        outs.append(jnp.einsum('bhqk,bkhd->bqhd', wdiff.astype(vb.dtype), vb))
    o = jnp.concatenate(outs, axis=1)
    o = _rms(o, subln_g) * (1.0 - lam_init)
    return o.reshape(B, S, DA_WIDTH) * jax.nn.silu(z)


def _rwkv7_scan(r, w, k, v, a_vec, b_vec):
    B, S, H, N = r.shape
    xs = tuple(jnp.moveaxis(t, 1, 0) for t in (r, w, k, v, a_vec, b_vec))

    def step(state, inp):
        r_t, w_t, k_t, v_t, a_t, b_t = inp
        sa = jnp.einsum('bhvk,bhk->bhv', state, a_t)
        state = (state * w_t[:, :, None, :] + sa[..., None] * b_t[:, :, None, :]
                 + v_t[..., None] * k_t[:, :, None, :])
        return state, jnp.einsum('bhvk,bhk->bhv', state, r_t)

    s0 = jnp.zeros((B, H, N, N), jnp.float32)
    _, ys = lax.scan(step, s0, xs)
    return jnp.moveaxis(ys, 0, 1)


def _rwkv7_branch(r, k, v, wl, al, z, mu, w0, w_up, a0, a_up, k_k, k_a, r_k, ln_g, ln_b):
    B, S, _ = r.shape
    f32 = jnp.float32
    mu_r, mu_k, mu_v, mu_w, mu_a = jnp.split(mu, MU_SPLITS)
    r = _token_shift(r, mu_r)
    k = _token_shift(k, mu_k)
    v = _token_shift(v, mu_v)
    wl = _token_shift(wl, mu_w)
    al = _token_shift(al, mu_a)
    heads = lambda t: t.reshape(B, S, RW_HEADS, RW_HEAD)
    w = -jax.nn.softplus(-(w0 + jnp.tanh(wl) @ w_up).astype(f32)) - 0.5
    decay = jnp.exp(-jnp.exp(w))
    a = jax.nn.sigmoid((a0 + al @ a_up).astype(f32))
    rf, kf, vf = r.astype(f32), k.astype(f32), v.astype(f32)
    kk = heads(kf * k_k.astype(f32))
    kk = kk / jnp.maximum(jnp.sqrt(jnp.sum(kk * kk, axis=-1, keepdims=True)), 1e-12)
    kf = kf * (1.0 + (a - 1.0) * k_a.astype(f32))
    y = _rwkv7_scan(heads(rf), heads(decay), heads(kf), heads(vf), -kk, kk * heads(a))
    mean = jnp.mean(y, axis=-1, keepdims=True)
    var = jnp.mean(jnp.square(y - mean), axis=-1, keepdims=True)
    y = ((y - mean) * lax.rsqrt(var + GN_EPS)).reshape(B, S, RW_WIDTH)
    y = y * ln_g.astype(f32) + ln_b.astype(f32)
    bonus = jnp.sum(heads(rf) * heads(kf) * r_k.astype(f32), axis=-1, keepdims=True) * heads(vf)
    y = y + bonus.reshape(B, S, RW_WIDTH)
    return y.astype(z.dtype) * jax.nn.silu(z)


def _memory_cross_attention_branch(q, z, mem_kv, q_g, k_g):
    B, S, _ = q.shape
    M = mem_kv.shape[1]
    q = _rms(q.reshape(B, S, CA_HEADS, CA_HEAD_DIM), q_g)
    kv = mem_kv.reshape(B, M, 2, CA_HEADS, CA_HEAD_DIM)
    km = _rms(kv[:, :, 0], k_g)
    vm = kv[:, :, 1]
    sc = jnp.einsum('bshd,bmhd->bhsm', q, km).astype(jnp.float32) * (CA_HEAD_DIM ** -0.5)
    p = jax.nn.softmax(sc, axis=-1).astype(vm.dtype)
    o = jnp.einsum('bhsm,bmhd->bshd', p, vm).reshape(B, S, CA_WIDTH)
    return o * jax.nn.silu(z)


def setup_inputs(seed: int = 0) -> dict:
    key = jax.random.key(seed)
    ks = jax.random.split(key, 24)
    L = DEPTH
    n = lambda k, shape, s: s * jax.random.normal(k, shape, jnp.float32)
    return {
        'x': n(ks[0], (BATCH, SEQ, D_MODEL), 1.0),
        'mem': n(ks[1], (BATCH, MEM_LEN, D_MODEL), 1.0),
        'norm_g': 1.0 + n(ks[2], (L, D_MODEL), 0.05),
        'mem_norm_g': 1.0 + n(ks[3], (L, D_MODEL), 0.05),
        'w_in': n(ks[4], (L, D_MODEL, IN_WIDTH), D_MODEL ** -0.5),
        'w_mem_kv': n(ks[5], (L, D_MODEL, 2 * CA_WIDTH), D_MODEL ** -0.5),
        'da_q_norm': 1.0 + n(ks[6], (L, DA_QK_DIM), 0.05),
        'da_k_norm': 1.0 + n(ks[7], (L, DA_QK_DIM), 0.05),
        'da_lambda': n(ks[8], (L, 4, DA_QK_DIM), 0.1),
        'da_subln': 1.0 + n(ks[9], (L, DA_V_DIM), 0.05),
        'rw_mu': jax.random.uniform(ks[10], (L, RW_SHIFT_WIDTH), jnp.float32),
        'rw_w0': -3.0 + n(ks[11], (L, RW_WIDTH), 1.0),
        'rw_w_up': n(ks[12], (L, RW_DECAY_LORA, RW_WIDTH), 0.1),
        'rw_a0': n(ks[13], (L, RW_WIDTH), 0.1),
        'rw_a_up': n(ks[14], (L, RW_AAA_LORA, RW_WIDTH), RW_AAA_LORA ** -0.5),
        'rw_k_k': 0.85 + n(ks[15], (L, RW_WIDTH), 0.05),
        'rw_k_a': 1.0 + n(ks[16], (L, RW_WIDTH), 0.05),
        'rw_r_k': n(ks[17], (L, RW_HEADS, RW_HEAD), 0.1),
        'rw_ln_g': 1.0 + n(ks[18], (L, RW_WIDTH), 0.05),
        'rw_ln_b': n(ks[19], (L, RW_WIDTH), 0.02),
        'ca_q_norm': 1.0 + n(ks[20], (L, CA_HEAD_DIM), 0.05),
        'ca_k_norm': 1.0 + n(ks[21], (L, CA_HEAD_DIM), 0.05),
        'w_branch': n(ks[22], (L, N_BRANCH, BRANCH_WIDTH, D_MODEL), BRANCH_WIDTH ** -0.5),
        'w_out': n(ks[23], (L, D_MODEL, D_MODEL), D_MODEL ** -0.5),
    }


def reference(x, mem, norm_g, mem_norm_g, w_in, w_mem_kv, da_q_norm, da_k_norm, da_lambda,
              da_subln, rw_mu, rw_w0, rw_w_up, rw_a0, rw_a_up, rw_k_k, rw_k_a, rw_r_k,
              rw_ln_g, rw_ln_b, ca_q_norm, ca_k_norm, w_branch, w_out):
    B, S, _ = x.shape
    cos, sin = _rope_tables(S, DA_QK_DIM)
    for l in range(DEPTH):
        h = _rms(x, norm_g[l])
        proj = h @ w_in[l]
        (da_q, da_k, da_v, da_z, rw_r, rw_k, rw_v, rw_wl, rw_al, rw_z,
         ca_q, ca_z, gate_logits) = jnp.split(proj, IN_SPLITS, axis=-1)
        lam_init = 0.8 - 0.6 * math.exp(-0.3 * l)
        y_a = _diff_attention_branch(da_q, da_k, da_v, da_z, da_q_norm[l], da_k_norm[l],
                                     da_lambda[l], da_subln[l], lam_init, cos, sin)
        y_b = _rwkv7_branch(rw_r, rw_k, rw_v, rw_wl, rw_al, rw_z, rw_mu[l], rw_w0[l],
                            rw_w_up[l], rw_a0[l], rw_a_up[l], rw_k_k[l], rw_k_a[l],
                            rw_r_k[l], rw_ln_g[l], rw_ln_b[l])
        mem_kv = _rms(mem, mem_norm_g[l]) @ w_mem_kv[l]
        y_c = _memory_cross_attention_branch(ca_q, ca_z, mem_kv, ca_q_norm[l], ca_k_norm[l])
        gates = jax.nn.sigmoid(gate_logits.reshape(B, S, N_BRANCH, D_MODEL))
        branches = jnp.stack([y_a, y_b, y_c], axis=2)
        branch_proj = jnp.einsum('bsnc,ncd->bsnd', branches, w_branch[l])
        merged = jnp.sum(gates * branch_proj, axis=2)
        x = x + merged @ w_out[l]
    return x
```

```python
import math
from contextlib import ExitStack
import numpy as np
import concourse.bass as bass
import concourse.mybir as mybir
from concourse.bass_utils import run_bass_kernel_spmd

F32 = mybir.dt.float32
BF16 = mybir.dt.bfloat16
AF = mybir.ActivationFunctionType
ALU = mybir.AluOpType

ENGS = ("pe", "act", "dve", "pool", "sp")
ATTACH_WAIT = True


class Buf:
    __slots__ = ("name", "last_w", "readers", "excl")

    def __init__(self, name="", last_w=None):
        self.name = name
        self.last_w = last_w
        self.readers = []
        self.excl = False


class Op:
    __slots__ = ("eng", "fn", "deps", "needed", "val", "is_dma", "sem", "idx", "dmak")

    def __init__(self, eng, fn, is_dma=False):
        self.eng = eng
        self.fn = fn
        self.deps = []
        self.needed = False
        self.val = None
        self.is_dma = is_dma
        self.sem = None
        self.idx = None
        self.dmak = None


class Prog:
    def __init__(self, nc, same_engine_sync=True, n_dma_sems=16):
        self.nc = nc
        self.ops = []
        self.same_engine_sync = same_engine_sync
        self.n_dma_sems = n_dma_sems
        self.final_waits = []
        self.fence = None
        self.phase_bufs = []

    def buf(self, name="", local=True):
        b = Buf(name, self.fence if local else None)
        if local:
            self.phase_bufs.append(b)
        return b

    def op(self, eng, fn, reads=(), writes=(), is_dma=False):
        o = Op(eng, fn, is_dma)
        o.idx = len(self.ops)
        deps = []
        for b in reads:
            if b.last_w is not None:
                deps.append(b.last_w)
            if b.excl:
                deps.extend(r for r in b.readers if r.eng != eng)
        for b in writes:
            if b.last_w is not None:
                deps.append(b.last_w)
            deps.extend(b.readers)
        seen = set()
        for d in deps:
            if d is o or id(d) in seen:
                continue
            seen.add(id(d))
            if (not is_dma) and (not d.is_dma) and d.eng == eng:
                if eng == "pe" or not self.same_engine_sync:
                    continue
            o.deps.append(d)
        for b in writes:
            b.last_w = o
            b.readers = []
        for b in reads:
            if b not in writes:
                b.readers.append(o)
        self.ops.append(o)
        return o

    def dma(self, q, out_ap, in_ap, reads=(), writes=(), final=False):
        def fn(e):
            return e.dma_start(out=out_ap, in_=in_ap)
        o = self.op(q, fn, reads, writes, is_dma=True)
        if final:
            self.final_waits.append(o)
        return o

    def end_phase(self, scratch_ap, scratch_buf):
        bufs = self.phase_bufs + [scratch_buf]
        self.fence = self.op("dve", lambda e: e.memset(scratch_ap, 0.0), reads=(), writes=bufs)
        self.phase_bufs = []

    def emit(self):
        nc = self.nc
        ops = self.ops
        for o in ops:
            for d in o.deps:
                d.needed = True
        cnt = {e: 0 for e in ENGS}
        dcount = {e: 0 for e in ENGS}
        for o in ops:
            if o.is_dma:
                o.dmak = dcount[o.eng]
                dcount[o.eng] += 1
            elif o.needed:
                cnt[o.eng] += 1
                o.val = cnt[o.eng]
        with ExitStack() as es:
            csem = {e: es.enter_context(nc.semaphore("cs_" + e)) for e in ENGS}
            dsem = {}
            for e in ENGS:
                if dcount[e] > 0:
                    dsem[e] = [es.enter_context(nc.semaphore("ds_%s_%d" % (e, i)))
                               for i in range(min(self.n_dma_sems, dcount[e]))]
            for o in ops:
                if o.is_dma:
                    pool = dsem[o.eng]
                    o.sem = pool[o.dmak % len(pool)]
                    o.val = 16 * (o.dmak // len(pool) + 1)
            per_eng = {e: [o for o in ops if o.eng == e] for e in ENGS}
            seen = {e: {} for e in ENGS}
            dma_lists = {e: [o for o in per_eng[e] if o.is_dma] for e in ENGS}
            waits_of = {}
            vc_of = {}

            def semkey(d):
                return ("d", d.sem.num) if d.is_dma else ("c", d.eng)

            def semobj(d):
                return d.sem if d.is_dma else csem[d.eng]

            for o in ops:
                sn = seen[o.eng]
                deps = list(o.deps)
                if o.is_dma:
                    pool_n = len(dsem[o.eng])
                    if o.dmak >= pool_n:
                        deps.append(dma_lists[o.eng][o.dmak - pool_n])
                deps.sort(key=lambda d: -d.idx)
                need = []
                for d in deps:
                    k = semkey(d)
                    if sn.get(k, 0) >= d.val:
                        continue
                    need.append((semobj(d), d.val))
                    sn[k] = d.val
                    for k2, v2 in vc_of.get(d.idx, {}).items():
                        if sn.get(k2, 0) < v2:
                            sn[k2] = v2
                best = {}
                for s_, v_ in need:
                    if best.get(s_.num, (None, 0))[1] < v_:
                        best[s_.num] = (s_, v_)
                waits_of[o.idx] = list(best.values())
                if o.needed or o.is_dma:
                    vc_of[o.idx] = dict(sn)
            block = es.enter_context(nc.Block())

            def run(ename, eng):
                seen_c = {e: 0 for e in ENGS}
                seen_d = {}
                my = per_eng[ename]
                mydmas = [o for o in my if o.is_dma]
                for o in my:
                    waits = list(waits_of[o.idx])
                    attach = None
                    if ATTACH_WAIT and waits and not o.is_dma:
                        attach = waits.pop()
                    for sem_, val_ in waits:
                        eng.wait_ge(sem_, val_)
                    if o.is_dma:
                        ins = o.fn(eng)
                        ins.then_inc(o.sem, 16)
                    else:
                        ins = o.fn(eng)
                        if attach is not None:
                            ins._wait_ge(attach[0], attach[1])
                        if o.needed:
                            ins.then_inc(csem[ename], 1)
                for o in self.final_waits:
                    if o.eng == ename:
                        eng.wait_ge(o.sem, o.val)

            if per_eng["sp"]:
                @block.sync
                def _(e):
                    run("sp", e)
            if per_eng["pool"]:
                @block.gpsimd
                def _(e):
                    run("pool", e)
            if per_eng["act"]:
                @block.scalar
                def _(e):
                    run("act", e)
            if per_eng["dve"]:
                @block.vector
                def _(e):
                    run("dve", e)
            if per_eng["pe"]:
                @block.tensor
                def _(e):
                    run("pe", e)
        return {e: len(per_eng[e]) for e in ENGS}


def run_pipe(items, depth):
    norm = []
    for it in items:
        if isinstance(it, tuple):
            norm.append(it)
        else:
            norm.append((it, []))
    done = [False] * len(norm)
    nxt_i = 0
    free = list(range(depth))
    active = []
    while nxt_i < len(norm) or active:
        while nxt_i < len(norm) and free and all(done[d] for d in norm[nxt_i][1]):
            slot = free.pop(0)
            active.append((norm[nxt_i][0](slot), slot, nxt_i))
            nxt_i += 1
        assert active, "pipeline deadlock"
        nxt = []
        for g, slot, idx in active:
            try:
                next(g)
                nxt.append((g, slot, idx))
            except StopIteration:
                free.append(slot)
                free.sort()
                done[idx] = True
        active = nxt


D = 1024
KC = 8
IN_W = 8320
C_DAQ, C_DAK, C_DAV, C_DAZ = 0, 512, 1024, 1536
C_RR, C_RK, C_RV, C_RWA, C_RZ = 2048, 2560, 3072, 3584, 3712
C_CQ, C_CZ = 4224, 4736
C_G = 5248
NORM_EPS = 1e-6
GN_EPS = 64e-5
DECAY_C = math.exp(-0.5)
NG, MG, DQ, DK, DS, LAMC, MU_R, MU_K, MU_V, MU_WA = 0, 8, 16, 17, 18, 19, 23, 27, 31, 35
W0, A0, KKc, KAc, RKc, LGc, LBc, CQc, CKc = 36, 40, 44, 48, 52, 56, 60, 64, 65
NCP = 66
OM_R, OM_K, OM_V, OM_WA, OM_KA, LAM, NLAM, SUBS = 0, 4, 8, 12, 13, 17, 18, 19
NDER = 20


def build(S, NSEQ, NL, dbg=False, with_rwkv=True):
    nc = bass.Bass("TRN2", target_bir_lowering=False)
    NB = S // 512
    NT = S // 128
    NCH = S // 128

    def din(name, shape):
        return nc.dram_tensor(name, list(shape), F32, kind="ExternalInput").ap()

    xT = din("xT", [NSEQ, 128, 8, S])
    memT = din("memT", [NSEQ, 128, 8, 256])
    w_in = din("w_in", [NL, 1024, IN_W])
    w_mkv = din("w_mkv", [NL, 1024, 1024])
    w_br = din("w_br", [NL, 3, 512, 1024])
    w_out = din("w_out", [NL, 1024, 1024])
    wa_up = din("wa_up", [NL, 128, 512])
    cpd = din("cp", [128, NL * NCP])
    csq = din("csq", [128, 7 * 128])
    id2d = din("id2", [128, 64])
    roped = din("rope", [128, 2 * S])
    cmaskd = din("cmask", [128, 4 * 512])
    rmaskd = din("rmask", [128, 512])
    yT = nc.dram_tensor("yT", [NSEQ, 128, 8, S], F32, kind="ExternalOutput").ap()
    dbgt = {}
    if dbg:
        for nm in ("dA", "dB", "dC"):
            dbgt[nm] = nc.dram_tensor(nm, [128, 4, S], F32, kind="ExternalOutput").ap()

    with ExitStack() as es:
        LIMIT = 53000
        arena = es.enter_context(nc.sbuf_tensor("arena", [128, LIMIT], F32))
        ps = es.enter_context(nc.psum_tensor("ps", [128, 4096], F32))
        P = Prog(nc)
        st = {"off": 0}

        def alloc(words):
            o = st["off"]
            st["off"] += int(words)
            assert st["off"] <= LIMIT, ("SBUF overflow", st["off"])
            return o

        def fv(off, n):
            return arena[:, off:off + n]

        def bv(off, n):
            return arena[:, off:off + n // 2].bitcast(BF16)

        def afv(n):
            return fv(alloc(n), n)

        def abv(n):
            return bv(alloc(n // 2), n)

        PSB = [Buf("psb%d" % i) for i in range(8)]
        for b_ in PSB:
            b_.excl = True

        def psb(i):
            return ps[:, i * 512:(i + 1) * 512]

        def psb_bf(i):
            return ps[:, i * 512:(i + 1) * 512].bitcast(BF16)

        def MM(out, lhsT, rhs, start, stop, r, w):
            P.op("pe", lambda e: e.matmul(out, lhsT=lhsT, rhs=rhs, start=start, stop=stop), reads=r, writes=w)

        def ACT(out, in_, func, r, w, bias=0.0, scale=1.0):
            P.op("act", lambda e: e.activation(out=out, in_=in_, func=func, bias=bias, scale=scale), reads=r, writes=w)

        def TT(out, in0, in1, op, r, w, eng="dve"):
            P.op(eng, lambda e: e.tensor_tensor(out=out, in0=in0, in1=in1, op=op), reads=r, writes=w)

        def TS(out, in0, s1, s2, op0, op1, r, w, eng="dve"):
            P.op(eng, lambda e: e.tensor_scalar(out=out, in0=in0, scalar1=s1, scalar2=s2, op0=op0, op1=op1), reads=r, writes=w)

        def STT(out, in0, scalar, in1, op0, op1, r, w):
            P.op("dve", lambda e: e.scalar_tensor_tensor(out=out, in0=in0, scalar=scalar, in1=in1, op0=op0, op1=op1), reads=r, writes=w)

        def CP(eng, out, in_, r, w):
            if eng == "act":
                P.op("act", lambda e: e.copy(out=out, in_=in_), reads=r, writes=w)
            else:
                P.op(eng, lambda e: e.tensor_copy(out=out, in_=in_), reads=r, writes=w)

        def RECIP(out, in_, r, w):
            P.op("dve", lambda e: e.reciprocal(out=out, in_=in_), reads=r, writes=w)

        def RSQ(out, in_, r, w):
            ACT(out, in_, AF.Exp, r, w, scale=-0.5)

        def RINV(out, in_, r, w):
            ACT(out, in_, AF.Ln, r, w)
            ACT(out, out, AF.Exp, w, w, scale=-1.0)

        cp_rr = {"i": 0}

        def CPRR(out, in_, r, w):
            cp_rr["i"] += 1
            CP("act" if cp_rr["i"] % 2 else "dve", out, in_, r, w)

        CB = P.buf("consts", local=False)
        csq_bf = abv(7 * 128)
        P.dma("pool", csq_bf, csq, writes=[CB])
        ident_bf = csq_bf[:, 0:128]
        ones_bf = csq_bf[:, 128:256]
        bd_bf = csq_bf[:, 256:384]
        perm_bf = csq_bf[:, 384:512]
        mstrT = csq_bf[:, 512:640]
        minclT = csq_bf[:, 640:768]
        mstr = csq_bf[:, 768:896]
        ones_f = afv(128)
        P.dma("sp", ones_f, csq[:, 128:256], writes=[CB])
        id2_bf = abv(64)
        P.dma("pool", id2_bf, id2d, writes=[CB])
        rmask_bf = abv(512)
        P.dma("pool", rmask_bf, rmaskd, writes=[CB])
        NCOL = NCP + NDER
        cpt = afv(NL * NCOL)
        for l in range(NL):
            P.dma("sp", cpt[:, l * NCOL:l * NCOL + NCP], cpd[:, l * NCP:(l + 1) * NCP], writes=[CB])
        waup_bf = abv(NL * 512)
        P.dma("pool", waup_bf, wa_up.rearrange("l p n -> p l n"), writes=[CB])
        scratch = afv(8)
        SCB = P.buf("scratch", local=False)

        def col(l, c, n=1):
            return cpt[:, l * NCOL + c:l * NCOL + c + n]

        def dcol(l, c, n=1):
            return cpt[:, l * NCOL + NCP + c:l * NCOL + NCP + c + n]

        for l in range(NL):
            lam_init = 0.8 - 0.6 * math.exp(-0.3 * l)
            for (src, dst, n) in ((MU_R, OM_R, 4), (MU_K, OM_K, 4), (MU_V, OM_V, 4), (MU_WA, OM_WA, 1), (KAc, OM_KA, 4)):
                TS(dcol(l, dst, n), col(l, src, n), -1.0, 1.0, ALU.mult, ALU.add, [CB], [CB])
            pr2 = scratch[:, 0:2]
            TT(pr2[:, 0:1], col(l, LAMC), col(l, LAMC + 1), ALU.mult, [CB], [SCB])
            TT(pr2[:, 1:2], col(l, LAMC + 2), col(l, LAMC + 3), ALU.mult, [CB, SCB], [SCB])
            MM(psb(0)[:, 0:2], ones_f, pr2, True, True, [CB, SCB], [PSB[0]])
            ex2 = scratch[:, 2:4]
            ACT(ex2, psb(0)[:, 0:2], AF.Exp, [PSB[0]], [SCB])
            TS(dcol(l, LAM), ex2[:, 0:1], ex2[:, 1:2], lam_init, ALU.subtract, ALU.add, [SCB], [CB])
            TS(dcol(l, NLAM), dcol(l, LAM), -1.0, None, ALU.mult, ALU.bypass, [CB], [CB])
            TS(dcol(l, SUBS), col(l, DS), 1.0 - lam_init, None, ALU.mult, ALU.bypass, [CB], [CB])

        hT = abv(8 * S).rearrange("p (k s) -> p k s", k=8)
        HB = [P.buf("hT%d" % n, local=False) for n in range(NB)]
        yBr = [None, None, None]
        YB = [None, None, None]
        NSLOT = 4
        wslot = [abv(8 * 512).rearrange("p (k n) -> p k n", k=8) for _ in range(NSLOT)]
        yac_start = None
        for br in (1, 0, 2):
            if br == 0:
                yac_start = st["off"]
            yBr[br] = abv(4 * S).rearrange("p (k s) -> p k s", k=4)
            YB[br] = [[P.buf("y%d_%d_%d" % (br, c, n), local=False) for n in range(NB)] for c in range(4)]
        WSB = [P.buf("wslot%d" % i, local=False) for i in range(NSLOT)]
        wst = {"i": 0}

        def load_w(src2d, ncols=512):
            i = wst["i"] % NSLOT
            wst["i"] += 1
            dst = wslot[i][:, :, 0:ncols]
            P.dma("pool", dst, src2d.rearrange("(k p) n -> p k n", p=128), writes=[WSB[i]])
            return wslot[i], WSB[i]

        persistent_end = st["off"]
        DR = {}

        def ybuf(b, dc, n):
            k = (b, dc, n)
            if k not in DR:
                DR[k] = P.buf("yd", local=False)
            return DR[k]

        def blk(n):
            return slice(n * 512, (n + 1) * 512)

        def proj_fm(wv, wb, c, n, bank):
            for kc in range(8):
                MM(psb(bank), wv[:, kc, c * 128:(c + 1) * 128], hT[:, kc, blk(n)], kc == 0, kc == 7,
                   [wb, HB[n]], [PSB[bank]])

        def mk_rms():
            d_ = dict(sq=abv(512), SQ=P.buf("sq"), sd=afv(512), SD=P.buf("sd"))
            d_["rs"] = d_["sd"]
            d_["RS"] = d_["SD"]
            return d_

        def rms_stat(RT, src_ap, src_bufs, ones_m, nelem, eps, bank, n512=512):
            sq, SQ = RT["sq"][:, 0:n512], RT["SQ"]
            ACT(sq, src_ap, AF.Square, src_bufs, [SQ])
            MM(psb(bank)[:, 0:n512], ones_m, sq, True, True, [CB, SQ], [PSB[bank]])
            sd, SD = RT["sd"][:, 0:n512], RT["SD"]
            ACT(sd, psb(bank)[:, 0:n512], AF.Ln, [PSB[bank]], [SD], bias=eps, scale=1.0 / nelem)
            rs, RS = RT["rs"][:, 0:n512], RT["RS"]
            RSQ(rs, sd, [SD], [RS])
            return rs, RS

        def v3(ap, a):
            return ap.rearrange("p (a b) -> p a b", a=a)

        def TRN(out, in_, r, w):
            P.op("pe", lambda e: e.transpose(out=out, in_=in_, identity=ident_bf), reads=r, writes=w)

        def rwkv(b, l):
            c_ = DECAY_C
            yB = yBr[1]
            names = ["rp", "kp", "vp", "e", "a", "kk", "m", "ka", "Lp", "Lx", "ex0", "ex1", "sd"]
            T = {nm: afv(512) for nm in names}
            TB = {nm: P.buf(nm) for nm in names}
            T["rs"] = T["sd"]
            TB["rs"] = TB["sd"]
            wla = T["ex0"].bitcast(BF16).rearrange("p (k n) -> p k n", k=8)
            WLA = TB["ex0"]
            P.dma("pool", wla, w_in[l][:, C_RWA:C_RWA + 128].rearrange("(k p) n -> p k n", p=128), writes=[WLA])
            wr, WR = load_w(w_in[l][:, C_RR:C_RR + 512])
            wk, WK = load_w(w_in[l][:, C_RK:C_RK + 512])
            wv, WV = load_w(w_in[l][:, C_RV:C_RV + 512])
            wz, WZ = load_w(w_in[l][:, C_RZ:C_RZ + 512])
            tw = abv(S)
            TWB = [P.buf("tw") for _ in range(NB)]
            carry = afv(4)
            CARB = [P.buf("car") for _ in range(4)]
            tmps = [afv(512), afv(512)]
            TMPS = [P.buf("tmp0"), P.buf("tmp1")]
            tcnt = {"i": 0}

            def shift(bk, ci, mu_ap, om_ap, out, OUT, n):
                i = tcnt["i"] % 2
                tcnt["i"] += 1
                tmp, TMP = tmps[i], TMPS[i]
                TS(tmp[:, 1:512], psb(bk)[:, 0:511], mu_ap, None, ALU.mult, ALU.bypass, [PSB[bk], CB], [TMP])
                if n == 0:
                    P.op("dve", lambda e: e.memset(tmp[:, 0:1], 0.0), writes=[TMP])
                else:
                    TS(tmp[:, 0:1], carry[:, ci:ci + 1], mu_ap, None, ALU.mult, ALU.bypass, [CARB[ci], CB], [TMP])
                STT(out, psb(bk), om_ap, tmp, ALU.mult, ALU.add, [PSB[bk], TMP, CB], [OUT])
                CP("dve", carry[:, ci:ci + 1], psb(bk)[:, 511:512], [PSB[bk]], [CARB[ci]])

            sh = T["Lx"]
            SH = TB["Lx"]
            for n in range(NB):
                for kc in range(8):
                    MM(psb(0), wla[:, kc, :], hT[:, kc, blk(n)], kc == 0, kc == 7, [WLA, HB[n]], [PSB[0]])
                shift(0, 0, col(l, MU_WA), dcol(l, OM_WA), sh, SH, n)
                ACT(tw[0:64, blk(n)], sh[0:64, :], AF.Tanh, [SH], [TWB[n]])
                CP("act", tw[64:128, blk(n)], sh[64:128, :], [SH], [TWB[n]])

            NGR = S // 256
            RpT = abv(S)
            RPB = [P.buf("rp") for _ in range(NGR)]
            Y0T = abv(S)
            Y0B = [P.buf("y0") for _ in range(NGR)]
            bon = abv(S)
            BONB = [P.buf("bon") for _ in range(NB)]
            szb = abv(S)
            SZB = [P.buf("szb") for _ in range(NB)]
            McT = abv(NCH * 64).rearrange("p (c k) -> p c k", c=NCH)
            MCB = [P.buf("mc") for _ in range(NGR)]
            Ncs = abv(NCH * 64).rearrange("p (c k) -> p c k", c=NCH)
            NCB = [P.buf("nc") for _ in range(NGR)]
            ST = abv((NCH + 1) * 64).rearrange("p (c k) -> p c k", c=NCH + 1)
            STB = [P.buf("st") for _ in range(NCH + 1)]
            PC = afv(NCH)
            PCB = [P.buf("pc") for _ in range(NB)]
            sqb = abv(512)
            SQB = P.buf("sqb")
            PO = []
            for par in range(2):
                d_ = dict(ARt=abv(4 * 256).rearrange("p (u x) -> p u x", u=4), ARB=P.buf("ARt"))
                for nm in ("BtT0", "BtT1", "KtT0", "KtT1", "BhT", "KhT", "vTb"):
                    d_[nm] = abv(512)
                    d_[nm + "_B"] = P.buf(nm)
                for nm in ("BtT0", "BtT1", "KtT0", "KtT1"):
                    P.op("pool", (lambda e, ap=d_[nm]: e.memset(ap, 0.0)), writes=[d_[nm + "_B"]])
                PO.append(d_)
            GS = []
            for sl in range(2):
                GS.append(dict(tokm=abv(1024).rearrange("p (j k x) -> p j k x", j=2, k=4), TK=P.buf("tok"),
                               Am=[abv(512), abv(512)], AMB=[P.buf("am0"), P.buf("am1")],
                               AmT=[abv(512), abv(512)], AMTB=[P.buf("amt0"), P.buf("amt1")],
                               ArbT=abv(512), ARBT=P.buf("arbt"), AakT=abv(512), AAKT=P.buf("aakt"),
                               ArkT=abv(512), ARKT=P.buf("arkt"),
                               X=[abv(512).rearrange("p (s x) -> p s x", s=4) for _ in range(2)],
                               XB=[P.buf("x0"), P.buf("x1")], tmpM=afv(256), TMPM=P.buf("tmpM"),
                               banks=(3 * sl, 3 * sl + 1, 3 * sl + 2)))
            ytile = T["e"]
            YT_ = TB["e"]
            ybf = sqb
            YBF = SQB

            def bmask(m):
                return m.unsqueeze(1).to_broadcast([128, 4, 128])

            def prep_item(f, n):
                O_ = PO[n % 2]

                def gen(slot):
                    ba, bb = 6, 7
                    proj_fm(wr, WR, f, n, ba)
                    yield
                    shift(ba, 1, col(l, MU_R + f), dcol(l, OM_R + f), T["rp"], TB["rp"], n)
                    proj_fm(wk, WK, f, n, bb)
                    yield
                    shift(bb, 2, col(l, MU_K + f), dcol(l, OM_K + f), T["kp"], TB["kp"], n)
                    proj_fm(wv, WV, f, n, ba)
                    yield
                    shift(ba, 3, col(l, MU_V + f), dcol(l, OM_V + f), T["vp"], TB["vp"], n)
                    MM(psb(bb), waup_bf[0:64, l * 512 + f * 128:l * 512 + (f + 1) * 128], tw[0:64, blk(n)], True, True,
                       [CB, TWB[n]], [PSB[bb]])
                    yield
                    ACT(T["e"], psb(bb), AF.Sigmoid, [PSB[bb], CB], [TB["e"]], bias=col(l, W0 + f))
                    MM(psb(ba), waup_bf[64:128, l * 512 + f * 128:l * 512 + (f + 1) * 128], tw[64:128, blk(n)], True, True,
                       [CB, TWB[n]], [PSB[ba]])
                    TS(T["kk"], T["kp"], col(l, KKc + f), None, ALU.mult, ALU.bypass, [TB["kp"], CB], [TB["kk"]])
                    yield
                    ACT(T["a"], psb(ba), AF.Sigmoid, [PSB[ba], CB], [TB["a"]], bias=col(l, A0 + f))
                    ACT(sqb, T["kk"], AF.Square, [TB["kk"]], [SQB])
                    P.op("dve", lambda e: e.tensor_tensor_scan(out=T["Lp"], data0=rmask_bf, data1=T["e"], initial=0.0,
                                                               op0=ALU.mult, op1=ALU.add),
                         reads=[CB, TB["e"]], writes=[TB["Lp"]])
                    yield
                    MM(psb(bb), bd_bf, sqb, True, True, [CB, SQB], [PSB[bb]])
                    TT(T["Lx"], T["Lp"], T["e"], ALU.subtract, [TB["Lp"], TB["e"]], [TB["Lx"]], eng="pool")
                    TS(T["m"], T["a"], col(l, KAc + f), dcol(l, OM_KA + f), ALU.mult, ALU.add, [TB["a"], CB], [TB["m"]])
                    yield
                    ACT(T["sd"], psb(bb), AF.Ln, [PSB[bb]], [TB["sd"]], bias=1e-18, scale=1.0)
                    RSQ(T["rs"], T["sd"], [TB["sd"]], [TB["rs"]])
                    ACT(T["ex0"], T["Lx"], AF.Exp, [TB["Lx"]], [TB["ex0"]], scale=-c_)
                    TT(T["m"], T["kp"], T["m"], ALU.mult, [TB["kp"], TB["m"]], [TB["m"]], eng="pool")
                    yield
                    TT(T["kk"], T["kk"], T["rs"], ALU.mult, [TB["kk"], TB["rs"]], [TB["kk"]])
                    ACT(T["ex1"], T["Lp"], AF.Exp, [TB["Lp"]], [TB["ex1"]], scale=-c_)
                    yield
                    STT(O_["ARt"][:, :, 0:128], v3(T["kk"], 4), -1.0, v3(T["ex0"], 4), ALU.mult, ALU.mult,
                        [TB["kk"], TB["ex0"]], [O_["ARB"]])
                    TT(T["ka"], T["kk"], T["a"], ALU.mult, [TB["kk"], TB["a"]], [TB["ka"]], eng="pool")
                    yield
                    TT(O_["ARt"][:, :, 128:256], v3(T["rp"], 4), v3(T["ex1"], 4), ALU.mult, [TB["rp"], TB["ex1"]], [O_["ARB"]])
                    ACT(T["ex0"], T["Lp"], AF.Exp, [TB["Lp"]], [TB["ex0"]], scale=c_)
                    Lp3 = v3(T["Lp"], 4)
                    ACT(PC[:, n * 4:(n + 1) * 4], Lp3[:, :, 127], AF.Exp, [TB["Lp"]], [PCB[n]], scale=-c_)
                    TT(v3(T["Lx"], 4), Lp3[:, :, 127:128].to_broadcast([128, 4, 128]), Lp3, ALU.subtract,
                       [TB["Lp"]], [TB["Lx"]])
                    yield
                    ACT(T["ex1"], T["Lx"], AF.Exp, [TB["Lx"]], [TB["ex1"]], scale=-c_)
                    for hp_ in range(2):
                        rw_ = slice(hp_ * 64, (hp_ + 1) * 64)
                        TT(O_["BtT%d" % hp_][rw_, :], T["ka"][rw_, :], T["ex0"][rw_, :], ALU.mult, [TB["ka"], TB["ex0"]],
                           [O_["BtT%d_B" % hp_]], eng="pool")
                    yield
                    for hp_ in range(2):
                        rw_ = slice(hp_ * 64, (hp_ + 1) * 64)
                        TT(O_["KtT%d" % hp_][rw_, :], T["m"][rw_, :], T["ex0"][rw_, :], ALU.mult, [TB["m"], TB["ex0"]],
                           [O_["KtT%d_B" % hp_]], eng="dve")
                    CP("act", O_["vTb"], T["vp"], [TB["vp"]], [O_["vTb_B"]])
                    yield
                    TT(O_["BhT"], T["ka"], T["ex1"], ALU.mult, [TB["ka"], TB["ex1"]], [O_["BhT_B"]], eng="pool")
                    TT(O_["KhT"], T["m"], T["ex1"], ALU.mult, [TB["m"], TB["ex1"]], [O_["KhT_B"]])
                    yield
                    STT(sqb, T["rp"], col(l, RKc + f), T["m"], ALU.mult, ALU.mult, [TB["rp"], TB["m"], CB], [SQB])
                    proj_fm(wz, WZ, f, n, bb)
                    yield
                    MM(psb(ba), bd_bf, sqb, True, True, [CB, SQB], [PSB[ba]])
                    ACT(szb[:, blk(n)], psb(bb), AF.Silu, [PSB[bb]], [SZB[n]])
                    yield
                    TT(bon[:, blk(n)], psb(ba), T["vp"], ALU.mult, [PSB[ba], TB["vp"]], [BONB[n]])
                return gen

            def grp_item(f, g):
                n = g // 2
                gi = g % 2
                O_ = PO[n % 2]
                G_ = GS[g % 2]
                ba, bb, bc = G_["banks"]
                tokm, TK = G_["tokm"], G_["TK"]
                Am, AMB, AmT, AMTB = G_["Am"], G_["AMB"], G_["AmT"], G_["AMTB"]
                X, XB_ = G_["X"], G_["XB"]

                def sets():
                    for j in range(2):
                        for hp in range(2):
                            yield j, hp, j * 2 + hp, 2 * gi + j

                def gen(slot):
                    pb = psb_bf(ba)
                    for j in range(2):
                        ub = 2 * gi + j
                        tk = slice(ub * 128, (ub + 1) * 128)
                        srcs = ((O_["ARt"][:, ub, 0:128], O_["ARB"]), (O_["BhT"][:, tk], O_["BhT_B"]),
                                (O_["KhT"][:, tk], O_["KhT_B"]), (O_["vTb"][:, tk], O_["vTb_B"]))
                        for kind, (sap, sbuf_) in enumerate(srcs):
                            TRN(pb[:, (j * 4 + kind) * 128:(j * 4 + kind + 1) * 128], sap, [sbuf_, CB], [PSB[ba]])
                    yield
                    CPRR(tokm.rearrange("p j k x -> p (j k x)"), pb, [PSB[ba]], [TK])
                    kinds = (
                        (bb, lambda j, hp, ub, tk: (O_["ARt"][:, ub, 0:128], O_["BtT%d" % hp][:, tk]), "B", Am[0], AMB[0], mstr),
                        (bc, lambda j, hp, ub, tk: (O_["BtT%d" % hp][:, tk], O_["ARt"][:, ub, 0:128]), "B", AmT[0], AMTB[0], mstrT),
                        (ba, lambda j, hp, ub, tk: (O_["BtT%d" % hp][:, tk], O_["ARt"][:, ub, 128:256]), "B", G_["ArbT"], G_["ARBT"], minclT),
                        (bb, lambda j, hp, ub, tk: (O_["KtT%d" % hp][:, tk], O_["ARt"][:, ub, 0:128]), "K", G_["AakT"], G_["AAKT"], mstrT),
                        (bc, lambda j, hp, ub, tk: (O_["KtT%d" % hp][:, tk], O_["ARt"][:, ub, 128:256]), "K", G_["ArkT"], G_["ARKT"], minclT),
                    )
                    pend = None
                    for (bk, opf, which, dst, DST, msk) in kinds:
                        for j, hp, s, ub in sets():
                            tk = slice(ub * 128, (ub + 1) * 128)
                            lhs, rhs = opf(j, hp, ub, tk)
                            wbuf = O_[("BtT%d_B" if which == "B" else "KtT%d_B") % hp]
                            MM(psb(bk)[:, s * 128:(s + 1) * 128], lhs, rhs, True, True, [O_["ARB"], wbuf], [PSB[bk]])
                        if pend is not None:
                            pend()
                        pend = (lambda bk=bk, dst=dst, DST=DST, msk=msk:
                                TT(v3(dst, 4), v3(psb(bk), 4), bmask(msk), ALU.mult, [PSB[bk], CB], [DST]))
                        yield
                    pend()
                    for j, hp, s, ub in sets():
                        MM(psb(ba)[:, s * 64:(s + 1) * 64], G_["AakT"][:, s * 128:(s + 1) * 128],
                           tokm[:, j, 3, hp * 64:(hp + 1) * 64], True, True, [G_["AAKT"], TK], [PSB[ba]])
                    for j in range(2):
                        CP("act", X[0][:, 2 * j:2 * j + 2, 0:64], tokm[:, j, 0, :].rearrange("p (h k) -> p h k", h=2),
                           [TK], [XB_[0]])
                    yield
                    CP("dve", X[0][:, :, 64:128], psb(ba)[:, 0:256].rearrange("p (s v) -> p s v", s=4), [PSB[ba]], [XB_[0]])
                    yield
                    cur = 0
                    for jj in range(7):
                        nxt = 1 - cur
                        xi, xo = jj % 2, (jj + 1) % 2
                        for s in range(4):
                            sc_ = slice(s * 128, (s + 1) * 128)
                            MM(psb(ba)[:, sc_], ident_bf, X[xi][:, s, :], True, False, [CB, XB_[xi]], [PSB[ba]])
                            MM(psb(ba)[:, sc_], AmT[cur][:, sc_], X[xi][:, s, :], False, True, [AMTB[cur], XB_[xi]], [PSB[ba]])
                        if jj < 6:
                            for s in range(4):
                                sc_ = slice(s * 128, (s + 1) * 128)
                                MM(psb(bb)[:, sc_], Am[cur][:, sc_], AmT[cur][:, sc_], True, True, [AMB[cur], AMTB[cur]], [PSB[bb]])
                            if jj < 5:
                                for s in range(4):
                                    sc_ = slice(s * 128, (s + 1) * 128)
                                    MM(psb(bc)[:, sc_], AmT[cur][:, sc_], Am[cur][:, sc_], True, True, [AMB[cur], AMTB[cur]], [PSB[bc]])
                        yield
                        CP("act", X[xo].rearrange("p s x -> p (s x)"), psb(ba), [PSB[ba]], [XB_[xo]])
                        if jj < 6:
                            CP("dve", AmT[nxt], psb(bb), [PSB[bb]], [AMTB[nxt]])
                            if jj < 5:
                                CPRR(Am[nxt], psb(bc), [PSB[bc]], [AMB[nxt]])
                            cur = nxt
                        yield
                    Xf, XFB = X[1], XB_[1]
                    ArbT, ARBT, ArkT, ARKT = G_["ArbT"], G_["ARBT"], G_["ArkT"], G_["ARKT"]
                    for j, hp, s, ub in sets():
                        rows = slice(hp * 64, (hp + 1) * 64)
                        sc_ = slice(s * 128, (s + 1) * 128)
                        jc = slice(j * 128, (j + 1) * 128)
                        MM(psb(ba)[rows, jc], Xf[:, s, 0:64], ArbT[:, sc_], True, True, [XFB, ARBT], [PSB[ba]])
                        MM(psb(bb)[rows, jc], Xf[:, s, 64:128], ArbT[:, sc_], True, False, [XFB, ARBT], [PSB[bb]])
                        MM(psb(bb)[rows, jc], tokm[:, j, 3, hp * 64:(hp + 1) * 64], ArkT[:, sc_], False, True,
                           [TK, ARKT], [PSB[bb]])
                    yield
                    gt = slice(g * 256, (g + 1) * 256)
                    TT(v3(RpT[:, gt], 2), v3(psb(ba)[:, 0:256], 2), O_["ARt"][:, 2 * gi:2 * gi + 2, 128:256], ALU.add,
                       [PSB[ba], O_["ARB"]], [RPB[g]])
                    CP("act", Y0T[:, gt], psb(bb)[:, 0:256], [PSB[bb]], [Y0B[g]])
                    yield
                    for j, hp, s, ub in sets():
                        rows = slice(hp * 64, (hp + 1) * 64)
                        hc = slice(hp * 64, (hp + 1) * 64)
                        oc = slice(j * 64, (j + 1) * 64)
                        MM(psb(ba)[rows, oc], Xf[:, s, 0:64], tokm[:, j, 1, hc], True, True, [XFB, TK], [PSB[ba]])
                        MM(psb(bc)[rows, oc], tokm[:, j, 1, hc], Xf[:, s, 64:128], True, False, [XFB, TK], [PSB[bc]])
                        MM(psb(bc)[rows, oc], tokm[:, j, 2, hc], tokm[:, j, 3, hc], False, True, [TK], [PSB[bc]])
                    gcs = slice(g * 2, (g + 1) * 2)
                    tmpM, TMPM_ = G_["tmpM"][:, 0:128], G_["TMPM"]
                    TT(v3(tmpM, 2), id2_bf.unsqueeze(1).to_broadcast([128, 2, 64]),
                       PC[:, gcs].unsqueeze(2).to_broadcast([128, 2, 64]), ALU.mult, [CB, PCB[n]], [TMPM_], eng="pool")
                    yield
                    TT(McT[:, gcs, :], v3(psb(ba)[:, 0:128], 2), v3(tmpM, 2), ALU.add, [PSB[ba], TMPM_], [MCB[g]])
                    CP("act", Ncs[:, gcs, :], v3(psb(bc)[:, 0:128], 2), [PSB[bc]], [NCB[g]])
                return gen

            for f in range(4):
                items = []
                idx = {}
                for n in range(NB):
                    deps = []
                    if n >= 1:
                        deps.append(idx[("p", n - 1)])
                    if n >= 2:
                        deps += [idx[("g", 2 * n - 4)], idx[("g", 2 * n - 3)]]
                    idx[("p", n)] = len(items)
                    items.append((prep_item(f, n), deps))
                    if n >= 1:
                        pass
                    for gi in range(2):
                        g = 2 * n + gi
                        deps = [idx[("p", n)]]
                        if g >= 2:
                            deps.append(idx[("g", g - 2)])
                        idx[("g", g)] = len(items)
                        items.append((grp_item(f, g), deps))
                order = []
                for n in range(NB):
                    if n == 0:
                        order.append(("p", 0))
                    if n + 1 < NB:
                        order.append(("p", n + 1))
                    order += [("g", 2 * n), ("g", 2 * n + 1)]
                remap = {}
                new_items = []
                for key in order:
                    remap[idx[key]] = len(new_items)
                    new_items.append(items[idx[key]])
                new_items = [(fn, [remap[d] for d in deps]) for fn, deps in new_items]
                run_pipe(new_items, 3)
                P.op("dve", lambda e: e.memset(ST[:, 0, :], 0.0), writes=[STB[0]])
                cprog = {"c": 0}

                def chain_item():
                    def gen(slot):
                        for c in range(NCH):
                            for hp in range(2):
                                rows = slice(hp * 64, (hp + 1) * 64)
                                bk = 1 + 2 * (c % 2) + hp
                                MM(psb(bk)[rows, 0:64], McT[rows, c, :], ST[rows, c, :], True, True, [MCB[c // 2], STB[c]], [PSB[bk]])
                            yield
                            for hp in range(2):
                                rows = slice(hp * 64, (hp + 1) * 64)
                                bk = 1 + 2 * (c % 2) + hp
                                TT(ST[rows, c + 1, :], psb(bk)[rows, 0:64], Ncs[rows, c, :], ALU.add, [PSB[bk], NCB[c // 2]], [STB[c + 1]])
                            cprog["c"] = c + 1
                            yield
                    return gen

                def out_item(n):
                    def gen(slot):
                        while cprog["c"] < 4 * n + 4:
                            yield
                        for ub in range(4):
                            for hp in range(2):
                                rows = slice(hp * 64, (hp + 1) * 64)
                                c = n * 4 + ub
                                oc = slice(ub * 128, (ub + 1) * 128)
                                tc_ = slice(n * 512 + ub * 128, n * 512 + (ub + 1) * 128)
                                MM(psb(7 - hp)[rows, oc], ST[rows, c, :], RpT[rows, tc_], True, True,
                                   [STB[c], RPB[(n * 4 + ub) // 2]], [PSB[7 - hp]])
                        yield
                        for hp in range(2):
                            rows = slice(hp * 64, (hp + 1) * 64)
                            TT(ytile[rows, :], psb(7 - hp)[rows, :], Y0T[rows, blk(n)], ALU.add,
                               [PSB[7 - hp], Y0B[2 * n], Y0B[2 * n + 1]], [YT_])
                        yield
                        CP("act", ybf, ytile, [YT_], [YBF])
                        yield
                        MM(psb(0), bd_bf, ybf, True, True, [CB, YBF], [PSB[0]])
                        yield
                        STT(ytile, psb(0), -1.0 / 64.0, ytile, ALU.mult, ALU.add, [PSB[0], YT_], [YT_])
                        yield
                        ACT(ybf, ytile, AF.Square, [YT_], [YBF])
                        yield
                        MM(psb(0), bd_bf, ybf, True, True, [CB, YBF], [PSB[0]])
                        yield
                        ACT(T["sd"], psb(0), AF.Ln, [PSB[0]], [TB["sd"]], bias=GN_EPS, scale=1.0 / 64.0)
                        RSQ(T["rs"], T["sd"], [TB["sd"]], [TB["rs"]])
                        yield
                        TT(ytile, ytile, T["rs"], ALU.mult, [YT_, TB["rs"]], [YT_])
                        yield
                        ACT(ytile, ytile, AF.Identity, [YT_, CB], [YT_], bias=col(l, LBc + f), scale=col(l, LGc + f))
                        yield
                        TT(ytile, ytile, bon[:, blk(n)], ALU.add, [YT_, BONB[n]], [YT_], eng="pool")
                        yield
                        TT(yB[:, f, blk(n)], ytile, szb[:, blk(n)], ALU.mult, [YT_, SZB[n]], [YB[1][f][n]])
                    return gen

                its = [(chain_item(), [])]
                for n in range(NB):
                    its.append((out_item(n), [len(its) - 1] if n > 0 else []))
                run_pipe(its, 2)

        for b in range(NSEQ):
            for l in range(NL):
                xsrc = xT if l == 0 else yT
                st["off"] = persistent_end
                xb = [afv(8 * 512).rearrange("p (k s) -> p k s", k=8) for _ in range(2)]
                XBB = [P.buf("xb0"), P.buf("xb1")]
                sqx = abv(8 * 512).rearrange("p (k s) -> p k s", k=8)
                SQX = P.buf("sqx")
                sdx = afv(512)
                SDX = P.buf("sdx")
                rsx = afv(512)
                RSX = P.buf("rsx")
                for n in range(NB):
                    xv, XB_ = xb[n % 2], XBB[n % 2]
                    rd = [ybuf(b, dc, n) for dc in range(8)] if l > 0 else []
                    P.dma("sp", xv, xsrc[b][:, :, blk(n)], reads=rd, writes=[XB_])
                    ACT(sqx, xv, AF.Square, [XB_], [SQX])
                    for kc in range(8):
                        MM(psb(0), ones_bf, sqx[:, kc, :], kc == 0, kc == 7, [CB, SQX], [PSB[0]])
                    ACT(sdx, psb(0), AF.Ln, [PSB[0]], [SDX], bias=NORM_EPS, scale=1.0 / D)
                    RSQ(rsx, sdx, [SDX], [RSX])
                    for kc in range(8):
                        STT(hT[:, kc, blk(n)], xv[:, kc, :], col(l, NG + kc), rsx, ALU.mult, ALU.mult,
                            [XB_, RSX, CB], [HB[n]])
                P.end_phase(scratch[:, 4:5], SCB)

                st["off"] = yac_start
                if with_rwkv:
                    rwkv(b, l)
                else:
                    for c_ in range(4):
                        for n_ in range(NB):
                            P.op('dve', (lambda e, c_=c_, n_=n_: e.memset(yBr[1][:, c_, blk(n_)], 0.0)), writes=[YB[1][c_][n_]])
                P.end_phase(scratch[:, 4:5], SCB)

                st["off"] = persistent_end
                RT = mk_rms()
                mx = afv(8 * 256).rearrange("p (k s) -> p k s", k=8)
                MX = P.buf("mx")
                P.dma("sp", mx, memT[b], writes=[MX])
                sqm = abv(8 * 256).rearrange("p (k s) -> p k s", k=8)
                SQM = P.buf("sqm")
                ACT(sqm, mx, AF.Square, [MX], [SQM])
                for kc in range(8):
                    MM(psb(0)[:, 0:256], ones_bf, sqm[:, kc, :], kc == 0, kc == 7, [CB, SQM], [PSB[0]])
                sdm = afv(256)
                SDM = P.buf("sdm")
                ACT(sdm, psb(0)[:, 0:256], AF.Ln, [PSB[0]], [SDM], bias=NORM_EPS, scale=1.0 / D)
                rsm = afv(256)
                RSM = P.buf("rsm")
                RSQ(rsm, sdm, [SDM], [RSM])
                hmT = abv(8 * 256).rearrange("p (k s) -> p k s", k=8)
                HM = P.buf("hm")
                for kc in range(8):
                    STT(hmT[:, kc, :], mx[:, kc, :], col(l, MG + kc), rsm, ALU.mult, ALU.mult, [MX, RSM, CB], [HM])
                wk, WK = load_w(w_mkv[l][:, 0:512])
                wv_, WV = load_w(w_mkv[l][:, 512:1024])
                KmT = abv(4 * 256).rearrange("p (h s) -> p h s", h=4)
                KM = P.buf("KmT")
                kraw = afv(256)
                KR = P.buf("kraw")
                for hd in range(4):
                    for kc in range(8):
                        MM(psb(1)[:, 0:256], wk[:, kc, hd * 128:(hd + 1) * 128], hmT[:, kc, :], kc == 0, kc == 7,
                           [WK, HM], [PSB[1]])
                    CP("act", kraw, psb(1)[:, 0:256], [PSB[1]], [KR])
                    rs, RS = rms_stat(RT, kraw, [KR], ones_bf, 128.0, NORM_EPS, 2, 256)
                    STT(KmT[:, hd, :], kraw, col(l, CKc), rs, ALU.mult, ALU.mult, [KR, RS, CB], [KM])
                Vm = abv(2 * 512).rearrange("p (t n) -> p t n", t=2)
                VM = P.buf("Vm")
                for mt in range(2):
                    for kc in range(8):
                        MM(psb(3), hmT[:, kc, mt * 128:(mt + 1) * 128], wv_[:, kc, :], kc == 0, kc == 7, [WV, HM], [PSB[3]])
                    CP("act", Vm[:, mt, :], psb(3), [PSB[3]], [VM])
                wq, WQ = load_w(w_in[l][:, C_CQ:C_CQ + 512])
                wz, WZ = load_w(w_in[l][:, C_CZ:C_CZ + 512])
                yC = yBr[2]
                cscale = 128.0 ** -0.5
                CS = []
                for sl in range(4):
                    CS.append(dict(qraw=afv(512), QR=P.buf("qraw"), qn=abv(512), QN=P.buf("qn"),
                                   pT=[abv(512), abv(512)], PT=[P.buf("pT0"), P.buf("pT1")],
                                   accs=afv(512), ACS=P.buf("accs"), rsum=afv(512), RSU=P.buf("rsum"),
                                   sz=afv(512), SZ=P.buf("sz"), RT=mk_rms()))

                def ca_item(hd, n):
                    def gen(slot):
                        Q = CS[slot]
                        ba, bd_ = 2 * slot, 2 * slot + 1
                        bb = ba
                        bc = ba
                        proj_fm(wq, WQ, hd, n, ba)
                        yield
                        CP("act", Q["qraw"], psb(ba), [PSB[ba]], [Q["QR"]])
                        R_ = Q["RT"]
                        ACT(R_["sq"], Q["qraw"], AF.Square, [Q["QR"]], [R_["SQ"]])
                        proj_fm(wz, WZ, hd, n, bd_)
                        yield
                        MM(psb(ba), ones_bf, R_["sq"], True, True, [CB, R_["SQ"]], [PSB[ba]])
                        ACT(Q["sz"], psb(bd_), AF.Exp, [PSB[bd_]], [Q["SZ"]], scale=-1.0)
                        yield
                        ACT(R_["sd"], psb(ba), AF.Ln, [PSB[ba]], [R_["SD"]], bias=NORM_EPS, scale=1.0 / 128.0)
                        ACT(Q["sz"], Q["sz"], AF.Ln, [Q["SZ"]], [Q["SZ"]], bias=1.0)
                        ACT(Q["sz"], Q["sz"], AF.Exp, [Q["SZ"]], [Q["SZ"]], scale=-1.0)
                        yield
                        RSQ(R_["rs"], R_["sd"], [R_["SD"]], [R_["RS"]])
                        TT(Q["sz"], psb(bd_), Q["sz"], ALU.mult, [PSB[bd_], Q["SZ"]], [Q["SZ"]])
                        STT(Q["qn"], Q["qraw"], col(l, CQc), R_["rs"], ALU.mult, ALU.mult, [Q["QR"], R_["RS"], CB], [Q["QN"]])
                        yield
                        MM(psb(ba), KmT[:, hd, 0:128], Q["qn"], True, True, [KM, Q["QN"]], [PSB[ba]])
                        MM(psb(bd_), KmT[:, hd, 128:256], Q["qn"], True, True, [KM, Q["QN"]], [PSB[bd_]])
                        yield
                        ACT(Q["pT"][0], psb(ba), AF.Exp, [PSB[ba]], [Q["PT"][0]], scale=cscale)
                        ACT(Q["pT"][1], psb(bd_), AF.Exp, [PSB[bd_]], [Q["PT"][1]], scale=cscale)
                        yield
                        for mt in range(2):
                            MM(psb(bd_), Vm[:, mt, hd * 128:(hd + 1) * 128], Q["pT"][mt], mt == 0, mt == 1,
                               [VM, Q["PT"][mt]], [PSB[bd_]])
                        for mt in range(2):
                            MM(psb(ba), ones_bf, Q["pT"][mt], mt == 0, mt == 1, [CB, Q["PT"][mt]], [PSB[ba]])
                        yield
                        RINV(Q["rsum"], psb(ba), [PSB[ba]], [Q["RSU"]])
                        yield
                        TT(Q["rsum"], Q["rsum"], Q["sz"], ALU.mult, [Q["RSU"], Q["SZ"]], [Q["RSU"]], eng="pool")
                        yield
                        TT(yC[:, hd, blk(n)], psb(bd_), Q["rsum"], ALU.mult, [PSB[bd_], Q["RSU"]], [YB[2][hd][n]])
                    return gen

                run_pipe([ca_item(hd, n) for hd in range(4) for n in range(NB)], 4)
                P.end_phase(scratch[:, 4:5], SCB)

                st["off"] = persistent_end
                RCB = P.buf("ropecm")
                rope_bf = abv(2 * S)
                P.dma("pool", rope_bf, roped, writes=[RCB])
                ropeC = rope_bf[:, 0:S]
                ropeS = rope_bf[:, S:2 * S]
                cmask_bf = abv(4 * 512)
                P.dma("pool", cmask_bf, cmaskd, writes=[RCB])
                qz = [abv(S), abv(S)]
                QZB = [P.buf("qz0"), P.buf("qz1")]
                for c2 in range(2):
                    P.op("pool", (lambda e, ap=qz[c2]: e.memset(ap, 0.0)), writes=[QZB[c2]])
                qT = None
                kT = abv(S)
                QKB = [[P.buf("qk") for n in range(NB)] for _ in range(2)]
                Vtok = abv(NT * 512).rearrange("p (t n) -> p t n", t=NT)
                VTB = [P.buf("vt") for _ in range(NT)]
                wq_, WQ_ = load_w(w_in[l][:, C_DAQ:C_DAQ + 512])
                wk_, WK_ = load_w(w_in[l][:, C_DAK:C_DAK + 512])
                wv, WB_ = load_w(w_in[l][:, C_DAV:C_DAV + 512])
                wz_, WZ_ = load_w(w_in[l][:, C_DAZ:C_DAZ + 512])
                for t in range(NT):
                    bkv = 6 + (t % 2)
                    for kc in range(8):
                        MM(psb(bkv), hT[:, kc, t * 128:(t + 1) * 128], wv[:, kc, :], kc == 0, kc == 7,
                           [WB_, HB[t // 4]], [PSB[bkv]])
                    CPRR(Vtok[:, t, :], psb(bkv), [PSB[bkv]], [VTB[t]])
                QS = []
                for sl in range(2):
                    QS.append(dict(raw=afv(512), RAW=P.buf("raw"), qn=abv(512), QN=P.buf("qn"), t1=abv(512), T1=P.buf("t1"),
                                   t2=afv(512), T2=P.buf("t2"), RT=mk_rms()))

                def qk_item(h, qi, n):
                    wv2, WB2, dst, gcol = ((wq_, WQ_, qT, DQ), (wk_, WK_, kT, DK))[qi]

                    def gen(slot):
                        Q = QS[slot]
                        R_ = Q["RT"]
                        b0, b1, b2 = 3 * slot, 3 * slot + 1, 3 * slot + 2
                        proj_fm(wv2, WB2, h, n, b0)
                        yield
                        CP("act", Q["raw"], psb(b0), [PSB[b0]], [Q["RAW"]])
                        ACT(R_["sq"], Q["raw"], AF.Square, [Q["RAW"]], [R_["SQ"]])
                        yield
                        MM(psb(b1), bd_bf, R_["sq"], True, True, [CB, R_["SQ"]], [PSB[b1]])
                        yield
                        ACT(R_["sd"], psb(b1), AF.Ln, [PSB[b1]], [R_["SD"]], bias=NORM_EPS, scale=1.0 / 64.0)
                        yield
                        RSQ(R_["rs"], R_["sd"], [R_["SD"]], [R_["RS"]])
                        STT(Q["qn"], Q["raw"], col(l, gcol), R_["rs"], ALU.mult, ALU.mult, [Q["RAW"], R_["RS"], CB], [Q["QN"]])
                        yield
                        MM(psb(b2), perm_bf, Q["qn"], True, True, [CB, Q["QN"]], [PSB[b2]])
                        TT(Q["t1"], Q["qn"], ropeC[:, blk(n)], ALU.mult, [Q["QN"], RCB], [Q["T1"]], eng="pool")
                        yield
                        TT(Q["t2"], psb(b2), ropeS[:, blk(n)], ALU.mult, [PSB[b2], RCB], [Q["T2"]])
                        yield
                        if qi == 1:
                            TT(dst[:, blk(n)], Q["t1"], Q["t2"], ALU.add, [Q["T1"], Q["T2"]], [QKB[qi][n]], eng="pool")
                        else:
                            for c2 in range(2):
                                r2 = slice(c2 * 64, (c2 + 1) * 64)
                                TT(qz[c2][r2, blk(n)], Q["t1"][r2, :], Q["t2"][r2, :], ALU.add, [Q["T1"], Q["T2"], QZB[c2]],
                                   [QKB[qi][n]], eng="pool")
                    return gen

                pTs = [abv(512) for _ in range(3)]
                PTS = [P.buf("pT") for _ in range(3)]
                SBK = (0, 1, 2)
                A_ = dict(rinv=afv(512), RINV=P.buf("rinv"),
                          o0=afv(512), O0=P.buf("o0"), o1=afv(512), O1=P.buf("o1"),
                          sz=afv(512), SZ=P.buf("sz"), RT=mk_rms())
                yA = yBr[0]

                def att_item(h, n, comp, kt, par):
                    nkt = 4 * (n + 1)
                    ob = 3 + comp
                    sbk = 5 + comp
                    rows = slice(comp * 64, (comp + 1) * 64)

                    def gen(slot):
                        sb_ = SBK[slot]
                        q0 = 128 * max(0, kt - 4 * n)
                        cs_ = slice(q0, 512)
                        qs_ = slice(n * 512 + q0, (n + 1) * 512)
                        MM(psb(sb_)[:, cs_], kT[:, kt * 128:(kt + 1) * 128], qz[comp][:, qs_], True, True,
                           [QKB[1][kt // 4], QKB[0][n], QZB[comp]], [PSB[sb_]])
                        yield
                        ACT(pTs[slot][:, cs_], psb(sb_)[:, cs_], AF.Exp, [PSB[sb_]], [PTS[slot]], scale=0.125)
                        yield
                        if kt >= 4 * n:
                            j = kt - 4 * n
                            TT(pTs[slot][:, q0:q0 + 128], pTs[slot][:, q0:q0 + 128], cmask_bf[:, j * 512 + q0:j * 512 + q0 + 128],
                               ALU.mult, [PTS[slot], RCB], [PTS[slot]])
                        MM(psb(ob)[:, cs_], Vtok[:, kt, h * 128:(h + 1) * 128], pTs[slot][:, cs_], kt == 0, kt == nkt - 1,
                           [VTB[kt], PTS[slot]], [PSB[ob]])
                        MM(psb(sbk)[:, cs_], ones_bf, pTs[slot][:, cs_], kt == 0, kt == nkt - 1, [CB, PTS[slot]], [PSB[sbk]])
                        if kt < nkt - 1:
                            return
                        yield
                        RINV(A_["rinv"], psb(sbk), [PSB[sbk]], [A_["RINV"]])
                        yield
                        if comp == 0:
                            TT(A_["o0"], psb(ob), A_["rinv"], ALU.mult, [PSB[ob], A_["RINV"]], [A_["O0"]])
                            return
                        TT(A_["o1"], psb(ob), A_["rinv"], ALU.mult, [PSB[ob], A_["RINV"]], [A_["O1"]])
                        for kc in range(8):
                            MM(psb(7), wz_[:, kc, h * 128:(h + 1) * 128], hT[:, kc, blk(n)], kc == 0, kc == 7,
                               [WZ_, HB[n]], [PSB[7]])
                        yield
                        STT(A_["o0"], A_["o1"], dcol(l, NLAM), A_["o0"], ALU.mult, ALU.add, [A_["O0"], A_["O1"], CB], [A_["O0"]])
                        R_ = A_["RT"]
                        ACT(R_["sq"], A_["o0"], AF.Square, [A_["O0"]], [R_["SQ"]])
                        ACT(A_["sz"], psb(7), AF.Exp, [PSB[7]], [A_["SZ"]], scale=-1.0)
                        yield
                        ACT(A_["sz"], A_["sz"], AF.Ln, [A_["SZ"]], [A_["SZ"]], bias=1.0)
                        ACT(A_["sz"], A_["sz"], AF.Exp, [A_["SZ"]], [A_["SZ"]], scale=-1.0)
                        yield
                        TT(A_["sz"], psb(7), A_["sz"], ALU.mult, [PSB[7], A_["SZ"]], [A_["SZ"]])
                        yield
                        MM(psb(7), ones_bf, R_["sq"], True, True, [CB, R_["SQ"]], [PSB[7]])
                        yield
                        ACT(R_["sd"], psb(7), AF.Ln, [PSB[7]], [R_["SD"]], bias=NORM_EPS, scale=1.0 / 128.0)
                        RSQ(R_["rs"], R_["sd"], [R_["SD"]], [R_["RS"]])
                        yield
                        TT(R_["rs"], R_["rs"], A_["sz"], ALU.mult, [R_["RS"], A_["SZ"]], [R_["RS"]], eng="pool")
                        yield
                        STT(yA[:, h, blk(n)], A_["o0"], dcol(l, SUBS), R_["rs"], ALU.mult, ALU.mult,
                            [A_["O0"], R_["RS"], CB], [YB[0][h][n]])
                    return gen

                hn = 0
                for h in range(4):
                    run_pipe([qk_item(h, qi, n) for qi in range(2) for n in range(NB)], 2)
                    items = []
                    for n in range(NB):
                        for comp in range(2):
                            for kt in range(4 * (n + 1)):
                                items.append(att_item(h, n, comp, kt, hn % 2))
                        hn += 1
                    run_pipe(items, 3)
                P.end_phase(scratch[:, 4:5], SCB)

                st["off"] = persistent_end
                merged = abv(8 * S).rearrange("p (k s) -> p k s", k=8)
                MGB = [[P.buf("mg") for n in range(NB)] for dc in range(8)]
                wg = [[abv(8 * 128).rearrange("p (k n) -> p k n", k=8) for nb_ in range(3)] for _ in range(2)]
                WG = [[P.buf("wg") for nb_ in range(3)] for _ in range(2)]
                wbr = [[abv(4 * 128).rearrange("p (k n) -> p k n", k=4) for nb_ in range(3)] for _ in range(2)]
                WBR = [[P.buf("wbr") for nb_ in range(3)] for _ in range(2)]
                sgs = [afv(512) for _ in range(3)]
                SGS = [P.buf("sg") for _ in range(3)]
                macc = afv(512)
                MACC = P.buf("macc")
                tmpms = [afv(512), afv(512)]
                TMPMS = [P.buf("tmpm0"), P.buf("tmpm1")]
                mcnt = 0
                def load_merge_w(dc2):
                    par2 = dc2 % 2
                    for nb2 in range(3):
                        c0 = C_G + nb2 * 1024 + dc2 * 128
                        P.dma("pool", wg[par2][nb2], w_in[l][:, c0:c0 + 128].rearrange("(k p) n -> p k n", p=128),
                              writes=[WG[par2][nb2]])
                        P.dma("pool", wbr[par2][nb2],
                              w_br[l][nb2][:, dc2 * 128:(dc2 + 1) * 128].rearrange("(k p) n -> p k n", p=128),
                              writes=[WBR[par2][nb2]])

                load_merge_w(0)
                for dc in range(8):
                    par = dc % 2
                    if dc + 1 < 8:
                        load_merge_w(dc + 1)
                    for n in range(NB):
                        for nb_ in range(3):
                            pr_ = mcnt % 3
                            mcnt += 1
                            bg, bb_ = 2 * pr_, 2 * pr_ + 1
                            sg, SG = sgs[pr_], SGS[pr_]
                            tmpm, TMPM = tmpms[mcnt % 2], TMPMS[mcnt % 2]
                            for kc in range(8):
                                MM(psb(bg), wg[par][nb_][:, kc, :], hT[:, kc, blk(n)], kc == 0, kc == 7,
                                   [WG[par][nb_], HB[n]], [PSB[bg]])
                            for kc in range(4):
                                MM(psb(bb_), wbr[par][nb_][:, kc, :], yBr[nb_][:, kc, blk(n)], kc == 0, kc == 3,
                                   [WBR[par][nb_], YB[nb_][kc][n]], [PSB[bb_]])
                            ACT(sg, psb(bg), AF.Sigmoid, [PSB[bg]], [SG])
                            if nb_ == 0:
                                TT(macc, sg, psb(bb_), ALU.mult, [SG, PSB[bb_]], [MACC])
                            else:
                                TT(tmpm, sg, psb(bb_), ALU.mult, [SG, PSB[bb_]], [TMPM])
                                if nb_ == 1:
                                    TT(macc, macc, tmpm, ALU.add, [MACC, TMPM], [MACC])
                                else:
                                    TT(merged[:, dc, blk(n)], macc, tmpm, ALU.add, [MACC, TMPM], [MGB[dc][n]])
                wo = [abv(8 * 128).rearrange("p (k n) -> p k n", k=8) for _ in range(2)]
                WO = [P.buf("wo0"), P.buf("wo1")]
                xr = [afv(512) for _ in range(2)]
                XR = [P.buf("xr0"), P.buf("xr1")]
                ot = [afv(512) for _ in range(2)]
                OT = [P.buf("ot0"), P.buf("ot1")]
                cnt = 0
                P.dma("pool", wo[0], w_out[l][:, 0:128].rearrange("(k p) n -> p k n", p=128), writes=[WO[0]])
                for dc in range(8):
                    par = dc % 2
                    if dc + 1 < 8:
                        P.dma("pool", wo[1 - par], w_out[l][:, (dc + 1) * 128:(dc + 2) * 128].rearrange("(k p) n -> p k n", p=128),
                              writes=[WO[1 - par]])
                    for n in range(NB):
                        i2 = cnt % 2
                        cnt += 1
                        rd = [ybuf(b, dc, n)] if l > 0 else []
                        P.dma("sp", xr[i2], xsrc[b][:, dc, blk(n)], reads=rd, writes=[XR[i2]])
                        bo = 6 + i2
                        for kc in range(8):
                            MM(psb(bo), wo[par][:, kc, :], merged[:, kc, blk(n)], kc == 0, kc == 7,
                               [WO[par], MGB[kc][n]], [PSB[bo]])
                        TT(ot[i2], psb(bo), xr[i2], ALU.add, [PSB[bo], XR[i2]], [OT[i2]])
                        P.dma("sp", yT[b][:, dc, blk(n)], ot[i2], reads=[OT[i2]], writes=[ybuf(b, dc, n)],
                              final=(l == NL - 1))
                if dbg and b == 0 and l == 0:
                    dtmp = afv(4 * S).rearrange("p (k s) -> p k s", k=4)
                    DT = P.buf("dtmp")
                    for br, nm in enumerate(("dA", "dB", "dC")):
                        allb = [YB[br][c][n] for c in range(4) for n in range(NB)]
                        CP("dve", dtmp, yBr[br], allb, [DT])
                        P.dma("sp", dbgt[nm], dtmp, reads=[DT], final=True)
                P.end_phase(scratch[:, 4:5], SCB)
        counts = P.emit()
    return nc, counts


def _fm(a):
    T = a.shape[0]
    return np.ascontiguousarray(a.T.reshape(8, 128, T).transpose(1, 0, 2))


def _cols(v):
    v = np.asarray(v, np.float32).reshape(-1)
    return v.reshape(-1, 128).T


def _consts(S):
    P_ = 128
    ident = np.eye(P_, dtype=np.float32)
    ones = np.ones((P_, P_), np.float32)
    bd = np.zeros((P_, P_), np.float32)
    bd[:64, :64] = 1
    bd[64:, 64:] = 1
    perm = np.zeros((P_, P_), np.float32)
    for p in range(P_):
        d = p % 64
        if d < 8:
            perm[p + 8, p] = 1
        elif d < 16:
            perm[p - 8, p] = 1
    tt = np.arange(P_)
    same = np.ones((P_, P_), bool)
    strictT = (same & (tt[None, :] > tt[:, None])).astype(np.float32)
    inclT = (same & (tt[None, :] >= tt[:, None])).astype(np.float32)
    strict = strictT.T.copy()
    csq = np.concatenate([ident, ones, bd, perm, strictT, inclT, strict], axis=1)
    id2 = np.concatenate([np.eye(64, dtype=np.float32)] * 2, axis=0)
    rot = 16
    inv = (1.0 / (500000.0 ** (np.arange(0, rot, 2, dtype=np.float32) / np.float32(rot)))).astype(np.float32)
    ang = np.arange(S, dtype=np.float32)[:, None] * inv[None, :]
    cos, sin = np.cos(ang).astype(np.float32), np.sin(ang).astype(np.float32)
    C = np.ones((P_, S), np.float32)
    Sg = np.zeros((P_, S), np.float32)
    for p in range(P_):
        d = p % 64
        if d < 8:
            C[p] = cos[:, d]
            Sg[p] = -sin[:, d]
        elif d < 16:
            C[p] = cos[:, d - 8]
            Sg[p] = sin[:, d - 8]
    rope = np.concatenate([C, Sg], axis=1)
    key = np.arange(P_)[:, None]
    q = np.arange(512)[None, :]
    cmask = np.concatenate([(q >= key + 128 * j).astype(np.float32) for j in range(4)], axis=1)
    rmask = np.ones((P_, 512), np.float32)
    rmask[:, ::128] = 0
    return dict(csq=csq, id2=id2, rope=rope, cmask=cmask, rmask=rmask)


def _cp_table(inp, NL):
    out = []
    for l in range(NL):
        t = np.zeros((128, NCP), np.float32)
        t[:, NG:NG + 8] = _cols(inp["norm_g"][l])
        t[:, MG:MG + 8] = _cols(inp["mem_norm_g"][l])
        t[:, DQ] = np.tile(inp["da_q_norm"][l], 2)
        t[:, DK] = np.tile(inp["da_k_norm"][l], 2)
        t[:, DS] = inp["da_subln"][l]
        t[:64, LAMC:LAMC + 4] = np.asarray(inp["da_lambda"][l]).T
        mu = np.asarray(inp["rw_mu"][l])
        t[:, MU_R:MU_R + 4] = _cols(mu[0:512])
        t[:, MU_K:MU_K + 4] = _cols(mu[512:1024])
        t[:, MU_V:MU_V + 4] = _cols(mu[1024:1536])
        t[:, MU_WA] = mu[1536:1664]
        t[:, W0:W0 + 4] = _cols(inp["rw_w0"][l])
        t[:, A0:A0 + 4] = _cols(inp["rw_a0"][l])
        t[:, KKc:KKc + 4] = _cols(inp["rw_k_k"][l])
        t[:, KAc:KAc + 4] = _cols(inp["rw_k_a"][l])
        t[:, RKc:RKc + 4] = _cols(np.asarray(inp["rw_r_k"][l]).reshape(-1))
        t[:, LGc:LGc + 4] = _cols(inp["rw_ln_g"][l])
        t[:, LBc:LBc + 4] = _cols(inp["rw_ln_b"][l])
        t[:, CQc] = inp["ca_q_norm"][l]
        t[:, CKc] = inp["ca_k_norm"][l]
        out.append(t)
    return np.ascontiguousarray(np.concatenate(out, axis=1))


def prep_inputs(inp, S, NSEQ, NL, ncores):
    inp = {k: np.asarray(v, dtype=np.float32) for k, v in inp.items()}
    cst = _consts(S)
    shared = dict(
        w_in=np.ascontiguousarray(inp["w_in"][:NL]),
        w_mkv=np.ascontiguousarray(inp["w_mem_kv"][:NL]),
        w_br=np.ascontiguousarray(inp["w_branch"][:NL]),
        w_out=np.ascontiguousarray(inp["w_out"][:NL]),
        wa_up=np.ascontiguousarray(np.concatenate([inp["rw_w_up"][:NL], inp["rw_a_up"][:NL]], axis=1)),
        cp=_cp_table(inp, NL),
        **cst,
    )
    maps = []
    for c in range(ncores):
        m = dict(shared)
        m["xT"] = np.stack([_fm(inp["x"][c * NSEQ + j][:S]) for j in range(NSEQ)])
        m["memT"] = np.stack([_fm(inp["mem"][c * NSEQ + j]) for j in range(NSEQ)])
        maps.append(m)
    return maps


def unprep_output(res, S, NSEQ, ncores):
    outs = []
    for c in range(ncores):
        y = res[c]["yT"]
        for j in range(NSEQ):
            outs.append(y[j].transpose(1, 0, 2).reshape(1024, S).T)
    return np.ascontiguousarray(np.stack(outs)).astype(np.float32)


_CACHE = {}


def kernel(**inputs):
    S, NSEQ, NL, ncores = 2048, 2, 2, 8
    if "nc" not in _CACHE:
        _CACHE["nc"] = build(S, NSEQ, NL)[0]
    nc = _CACHE["nc"]
    maps = prep_inputs(inputs, S, NSEQ, NL, ncores)
    res = run_bass_kernel_spmd(nc, maps, core_ids=list(range(ncores)))
    return unprep_output(res.results, S, NSEQ, ncores)
```

```python
import math
from contextlib import ExitStack
import numpy as np
import concourse.bass as bass
import concourse.mybir as mybir
from concourse.bass_utils import run_bass_kernel_spmd

F32 = mybir.dt.float32
BF16 = mybir.dt.bfloat16
AF = mybir.ActivationFunctionType
ALU = mybir.AluOpType

ENGS = ("pe", "act", "dve", "pool", "sp")
ATTACH_WAIT = True


class Buf:
    __slots__ = ("name", "last_w", "readers", "excl")

    def __init__(self, name="", last_w=None):
        self.name = name
        self.last_w = last_w
        self.readers = []
        self.excl = False


class Op:
    __slots__ = ("eng", "fn", "deps", "needed", "val", "is_dma", "sem", "idx", "dmak")

    def __init__(self, eng, fn, is_dma=False):
        self.eng = eng
        self.fn = fn
        self.deps = []
        self.needed = False
        self.val = None
        self.is_dma = is_dma
        self.sem = None
        self.idx = None
        self.dmak = None


class Prog:
    def __init__(self, nc, same_engine_sync=True, n_dma_sems=16):
        self.nc = nc
        self.ops = []
        self.same_engine_sync = same_engine_sync
        self.n_dma_sems = n_dma_sems
        self.final_waits = []
        self.fence = None
        self.phase_bufs = []

    def buf(self, name="", local=True):
        b = Buf(name, self.fence if local else None)
        if local:
            self.phase_bufs.append(b)
        return b

    def op(self, eng, fn, reads=(), writes=(), is_dma=False):
        o = Op(eng, fn, is_dma)
        o.idx = len(self.ops)
        deps = []
        for b in reads:
            if b.last_w is not None:
                deps.append(b.last_w)
            if b.excl:
                deps.extend(r for r in b.readers if r.eng != eng)
        for b in writes:
            if b.last_w is not None:
                deps.append(b.last_w)
            deps.extend(b.readers)
        seen = set()
        for d in deps:
            if d is o or id(d) in seen:
                continue
            seen.add(id(d))
            if (not is_dma) and (not d.is_dma) and d.eng == eng:
                if eng == "pe" or not self.same_engine_sync:
                    continue
            o.deps.append(d)
        for b in writes:
            b.last_w = o
            b.readers = []
        for b in reads:
            if b not in writes:
                b.readers.append(o)
        self.ops.append(o)
        return o

    def dma(self, q, out_ap, in_ap, reads=(), writes=(), final=False):
        def fn(e):
            return e.dma_start(out=out_ap, in_=in_ap)
        o = self.op(q, fn, reads, writes, is_dma=True)
        if final:
            self.final_waits.append(o)
        return o

    def end_phase(self, scratch_ap, scratch_buf):
        bufs = self.phase_bufs + [scratch_buf]
        self.fence = self.op("dve", lambda e: e.memset(scratch_ap, 0.0), reads=(), writes=bufs)
        self.phase_bufs = []

    def emit(self):
        nc = self.nc
        ops = self.ops
        for o in ops:
            for d in o.deps:
                d.needed = True
        cnt = {e: 0 for e in ENGS}
        dcount = {e: 0 for e in ENGS}
        for o in ops:
            if o.is_dma:
                o.dmak = dcount[o.eng]
                dcount[o.eng] += 1
            elif o.needed:
                cnt[o.eng] += 1
                o.val = cnt[o.eng]
        with ExitStack() as es:
            csem = {e: es.enter_context(nc.semaphore("cs_" + e)) for e in ENGS}
            dsem = {}
            for e in ENGS:
                if dcount[e] > 0:
                    dsem[e] = [es.enter_context(nc.semaphore("ds_%s_%d" % (e, i)))
                               for i in range(min(self.n_dma_sems, dcount[e]))]
            for o in ops:
                if o.is_dma:
                    pool = dsem[o.eng]
                    o.sem = pool[o.dmak % len(pool)]
                    o.val = 16 * (o.dmak // len(pool) + 1)
            per_eng = {e: [o for o in ops if o.eng == e] for e in ENGS}
            seen = {e: {} for e in ENGS}
            dma_lists = {e: [o for o in per_eng[e] if o.is_dma] for e in ENGS}
            waits_of = {}
            vc_of = {}

            def semkey(d):
                return ("d", d.sem.num) if d.is_dma else ("c", d.eng)

            def semobj(d):
                return d.sem if d.is_dma else csem[d.eng]

            for o in ops:
                sn = seen[o.eng]
                deps = list(o.deps)
                if o.is_dma:
                    pool_n = len(dsem[o.eng])
                    if o.dmak >= pool_n:
                        deps.append(dma_lists[o.eng][o.dmak - pool_n])
                deps.sort(key=lambda d: -d.idx)
                need = []
                for d in deps:
                    k = semkey(d)
                    if sn.get(k, 0) >= d.val:
                        continue
                    need.append((semobj(d), d.val))
                    sn[k] = d.val
                    for k2, v2 in vc_of.get(d.idx, {}).items():
                        if sn.get(k2, 0) < v2:
                            sn[k2] = v2
                best = {}
                for s_, v_ in need:
                    if best.get(s_.num, (None, 0))[1] < v_:
                        best[s_.num] = (s_, v_)
                waits_of[o.idx] = list(best.values())
                if o.needed or o.is_dma:
                    vc_of[o.idx] = dict(sn)
            block = es.enter_context(nc.Block())

            def run(ename, eng):
                seen_c = {e: 0 for e in ENGS}
                seen_d = {}
                my = per_eng[ename]
                mydmas = [o for o in my if o.is_dma]
                for o in my:
                    waits = list(waits_of[o.idx])
                    attach = None
                    if ATTACH_WAIT and waits and not o.is_dma:
                        attach = waits.pop()
                    for sem_, val_ in waits:
                        eng.wait_ge(sem_, val_)
                    if o.is_dma:
                        ins = o.fn(eng)
                        ins.then_inc(o.sem, 16)
                    else:
                        ins = o.fn(eng)
                        if attach is not None:
                            ins._wait_ge(attach[0], attach[1])
                        if o.needed:
                            ins.then_inc(csem[ename], 1)
                for o in self.final_waits:
                    if o.eng == ename:
                        eng.wait_ge(o.sem, o.val)

            if per_eng["sp"]:
                @block.sync
                def _(e):
                    run("sp", e)
            if per_eng["pool"]:
                @block.gpsimd
                def _(e):
                    run("pool", e)
            if per_eng["act"]:
                @block.scalar
                def _(e):
                    run("act", e)
            if per_eng["dve"]:
                @block.vector
                def _(e):
                    run("dve", e)
            if per_eng["pe"]:
                @block.tensor
                def _(e):
                    run("pe", e)
        return {e: len(per_eng[e]) for e in ENGS}


def run_pipe(items, depth):
    norm = []
    for it in items:
        if isinstance(it, tuple):
            norm.append(it)
        else:
            norm.append((it, []))
    done = [False] * len(norm)
    nxt_i = 0
    free = list(range(depth))
    active = []
    while nxt_i < len(norm) or active:
        while nxt_i < len(norm) and free and all(done[d] for d in norm[nxt_i][1]):
            slot = free.pop(0)
            active.append((norm[nxt_i][0](slot), slot, nxt_i))
            nxt_i += 1
        assert active, "pipeline deadlock"
        nxt = []
        for g, slot, idx in active:
            try:
                next(g)
                nxt.append((g, slot, idx))
            except StopIteration:
                free.append(slot)
                free.sort()
                done[idx] = True
        active = nxt


D = 1024
KC = 8
IN_W = 8320
C_DAQ, C_DAK, C_DAV, C_DAZ = 0, 512, 1024, 1536
C_RR, C_RK, C_RV, C_RWA, C_RZ = 2048, 2560, 3072, 3584, 3712
C_CQ, C_CZ = 4224, 4736
C_G = 5248
NORM_EPS = 1e-6
GN_EPS = 64e-5
DECAY_C = math.exp(-0.5)
NG, MG, DQ, DK, DS, LAMC, MU_R, MU_K, MU_V, MU_WA = 0, 8, 16, 17, 18, 19, 23, 27, 31, 35
W0, A0, KKc, KAc, RKc, LGc, LBc, CQc, CKc = 36, 40, 44, 48, 52, 56, 60, 64, 65
NCP = 66
OM_R, OM_K, OM_V, OM_WA, OM_KA, LAM, NLAM, SUBS = 0, 4, 8, 12, 13, 17, 18, 19
NDER = 20


def build(S, NSEQ, NL, dbg=False, with_rwkv=True):
    nc = bass.Bass("TRN2", target_bir_lowering=False)
    NB = S // 512
    NT = S // 128
    NCH = S // 128

    def din(name, shape):
        return nc.dram_tensor(name, list(shape), F32, kind="ExternalInput").ap()

    xT = din("xT", [NSEQ, 128, 8, S])
    memT = din("memT", [NSEQ, 128, 8, 256])
    w_in = din("w_in", [NL, 1024, IN_W])
    w_mkv = din("w_mkv", [NL, 1024, 1024])
    w_br = din("w_br", [NL, 3, 512, 1024])
    w_out = din("w_out", [NL, 1024, 1024])
    wa_up = din("wa_up", [NL, 128, 512])
    cpd = din("cp", [128, NL * NCP])
    csq = din("csq", [128, 7 * 128])
    id2d = din("id2", [128, 64])
    roped = din("rope", [128, 2 * S])
    cmaskd = din("cmask", [128, 4 * 512])
    rmaskd = din("rmask", [128, 512])
    yT = nc.dram_tensor("yT", [NSEQ, 128, 8, S], F32, kind="ExternalOutput").ap()
    dbgt = {}
    if dbg:
        for nm in ("dA", "dB", "dC"):
            dbgt[nm] = nc.dram_tensor(nm, [128, 4, S], F32, kind="ExternalOutput").ap()

    with ExitStack() as es:
        LIMIT = 53000
        arena = es.enter_context(nc.sbuf_tensor("arena", [128, LIMIT], F32))
        ps = es.enter_context(nc.psum_tensor("ps", [128, 4096], F32))
        P = Prog(nc)
        st = {"off": 0}

        def alloc(words):
            o = st["off"]
            st["off"] += int(words)
            assert st["off"] <= LIMIT, ("SBUF overflow", st["off"])
            return o

        def fv(off, n):
            return arena[:, off:off + n]

        def bv(off, n):
            return arena[:, off:off + n // 2].bitcast(BF16)

        def afv(n):
            return fv(alloc(n), n)

        def abv(n):
            return bv(alloc(n // 2), n)

        PSB = [Buf("psb%d" % i) for i in range(8)]
        for b_ in PSB:
            b_.excl = True

        def psb(i):
            return ps[:, i * 512:(i + 1) * 512]

        def psb_bf(i):
            return ps[:, i * 512:(i + 1) * 512].bitcast(BF16)

        def MM(out, lhsT, rhs, start, stop, r, w):
            P.op("pe", lambda e: e.matmul(out, lhsT=lhsT, rhs=rhs, start=start, stop=stop), reads=r, writes=w)

        def ACT(out, in_, func, r, w, bias=0.0, scale=1.0):
            P.op("act", lambda e: e.activation(out=out, in_=in_, func=func, bias=bias, scale=scale), reads=r, writes=w)

        def TT(out, in0, in1, op, r, w, eng="dve"):
            P.op(eng, lambda e: e.tensor_tensor(out=out, in0=in0, in1=in1, op=op), reads=r, writes=w)

        def TS(out, in0, s1, s2, op0, op1, r, w, eng="dve"):
            P.op(eng, lambda e: e.tensor_scalar(out=out, in0=in0, scalar1=s1, scalar2=s2, op0=op0, op1=op1), reads=r, writes=w)

        def STT(out, in0, scalar, in1, op0, op1, r, w):
            P.op("dve", lambda e: e.scalar_tensor_tensor(out=out, in0=in0, scalar=scalar, in1=in1, op0=op0, op1=op1), reads=r, writes=w)

        def CP(eng, out, in_, r, w):
            if eng == "act":
                P.op("act", lambda e: e.copy(out=out, in_=in_), reads=r, writes=w)
            else:
                P.op(eng, lambda e: e.tensor_copy(out=out, in_=in_), reads=r, writes=w)

        def RECIP(out, in_, r, w):
            P.op("dve", lambda e: e.reciprocal(out=out, in_=in_), reads=r, writes=w)

        def RSQ(out, in_, r, w):
            ACT(out, in_, AF.Exp, r, w, scale=-0.5)

        def RINV(out, in_, r, w):
            ACT(out, in_, AF.Ln, r, w)
            ACT(out, out, AF.Exp, w, w, scale=-1.0)

        cp_rr = {"i": 0}

        def CPRR(out, in_, r, w):
            cp_rr["i"] += 1
            CP("act" if cp_rr["i"] % 2 else "dve", out, in_, r, w)

        CB = P.buf("consts", local=False)
        csq_bf = abv(7 * 128)
        P.dma("pool", csq_bf, csq, writes=[CB])
        ident_bf = csq_bf[:, 0:128]
        ones_bf = csq_bf[:, 128:256]
        bd_bf = csq_bf[:, 256:384]
        perm_bf = csq_bf[:, 384:512]
        mstrT = csq_bf[:, 512:640]
        minclT = csq_bf[:, 640:768]
        mstr = csq_bf[:, 768:896]
        ones_f = afv(128)
        P.dma("sp", ones_f, csq[:, 128:256], writes=[CB])
        id2_bf = abv(64)
        P.dma("pool", id2_bf, id2d, writes=[CB])
        rmask_bf = abv(512)
        P.dma("pool", rmask_bf, rmaskd, writes=[CB])
        NCOL = NCP + NDER
        cpt = afv(NL * NCOL)
        for l in range(NL):
            P.dma("sp", cpt[:, l * NCOL:l * NCOL + NCP], cpd[:, l * NCP:(l + 1) * NCP], writes=[CB])
        waup_bf = abv(NL * 512)
        P.dma("pool", waup_bf, wa_up.rearrange("l p n -> p l n"), writes=[CB])
        scratch = afv(8)
        SCB = P.buf("scratch", local=False)

        def col(l, c, n=1):
            return cpt[:, l * NCOL + c:l * NCOL + c + n]

        def dcol(l, c, n=1):
            return cpt[:, l * NCOL + NCP + c:l * NCOL + NCP + c + n]

        for l in range(NL):
            lam_init = 0.8 - 0.6 * math.exp(-0.3 * l)
            for (src, dst, n) in ((MU_R, OM_R, 4), (MU_K, OM_K, 4), (MU_V, OM_V, 4), (MU_WA, OM_WA, 1), (KAc, OM_KA, 4)):
                TS(dcol(l, dst, n), col(l, src, n), -1.0, 1.0, ALU.mult, ALU.add, [CB], [CB])
            pr2 = scratch[:, 0:2]
            TT(pr2[:, 0:1], col(l, LAMC), col(l, LAMC + 1), ALU.mult, [CB], [SCB])
            TT(pr2[:, 1:2], col(l, LAMC + 2), col(l, LAMC + 3), ALU.mult, [CB, SCB], [SCB])
            MM(psb(0)[:, 0:2], ones_f, pr2, True, True, [CB, SCB], [PSB[0]])
            ex2 = scratch[:, 2:4]
            ACT(ex2, psb(0)[:, 0:2], AF.Exp, [PSB[0]], [SCB])
            TS(dcol(l, LAM), ex2[:, 0:1], ex2[:, 1:2], lam_init, ALU.subtract, ALU.add, [SCB], [CB])
            TS(dcol(l, NLAM), dcol(l, LAM), -1.0, None, ALU.mult, ALU.bypass, [CB], [CB])
            TS(dcol(l, SUBS), col(l, DS), 1.0 - lam_init, None, ALU.mult, ALU.bypass, [CB], [CB])

        hT = abv(8 * S).rearrange("p (k s) -> p k s", k=8)
        HB = [P.buf("hT%d" % n, local=False) for n in range(NB)]
        yBr = [None, None, None]
        YB = [None, None, None]
        NSLOT = 4
        wslot = [abv(8 * 512).rearrange("p (k n) -> p k n", k=8) for _ in range(NSLOT)]
        yac_start = None
        for br in (1, 0, 2):
            if br == 0:
                yac_start = st["off"]
            yBr[br] = abv(4 * S).rearrange("p (k s) -> p k s", k=4)
            YB[br] = [[P.buf("y%d_%d_%d" % (br, c, n), local=False) for n in range(NB)] for c in range(4)]
        WSB = [P.buf("wslot%d" % i, local=False) for i in range(NSLOT)]
        wst = {"i": 0}

        def load_w(src2d, ncols=512):
            i = wst["i"] % NSLOT
            wst["i"] += 1
            dst = wslot[i][:, :, 0:ncols]
            P.dma("pool", dst, src2d.rearrange("(k p) n -> p k n", p=128), writes=[WSB[i]])
            return wslot[i], WSB[i]

        persistent_end = st["off"]
        DR = {}

        def ybuf(b, dc, n):
            k = (b, dc, n)
            if k not in DR:
                DR[k] = P.buf("yd", local=False)
            return DR[k]

        def blk(n):
            return slice(n * 512, (n + 1) * 512)

        def proj_fm(wv, wb, c, n, bank):
            for kc in range(8):
                MM(psb(bank), wv[:, kc, c * 128:(c + 1) * 128], hT[:, kc, blk(n)], kc == 0, kc == 7,
                   [wb, HB[n]], [PSB[bank]])

        def mk_rms():
            d_ = dict(sq=abv(512), SQ=P.buf("sq"), sd=afv(512), SD=P.buf("sd"))
            d_["rs"] = d_["sd"]
            d_["RS"] = d_["SD"]
            return d_

        def rms_stat(RT, src_ap, src_bufs, ones_m, nelem, eps, bank, n512=512):
            sq, SQ = RT["sq"][:, 0:n512], RT["SQ"]
            ACT(sq, src_ap, AF.Square, src_bufs, [SQ])
            MM(psb(bank)[:, 0:n512], ones_m, sq, True, True, [CB, SQ], [PSB[bank]])
            sd, SD = RT["sd"][:, 0:n512], RT["SD"]
            ACT(sd, psb(bank)[:, 0:n512], AF.Ln, [PSB[bank]], [SD], bias=eps, scale=1.0 / nelem)
            rs, RS = RT["rs"][:, 0:n512], RT["RS"]
            RSQ(rs, sd, [SD], [RS])
            return rs, RS

        def v3(ap, a):
            return ap.rearrange("p (a b) -> p a b", a=a)

        def TRN(out, in_, r, w):
            P.op("pe", lambda e: e.transpose(out=out, in_=in_, identity=ident_bf), reads=r, writes=w)

        def rwkv(b, l):
            c_ = DECAY_C
            yB = yBr[1]
            names = ["rp", "kp", "vp", "e", "a", "kk", "m", "ka", "Lp", "Lx", "ex0", "ex1", "sd"]
            T = {nm: afv(512) for nm in names}
            TB = {nm: P.buf(nm) for nm in names}
            T["rs"] = T["sd"]
            TB["rs"] = TB["sd"]
            wla = T["ex0"].bitcast(BF16).rearrange("p (k n) -> p k n", k=8)
            WLA = TB["ex0"]
            P.dma("pool", wla, w_in[l][:, C_RWA:C_RWA + 128].rearrange("(k p) n -> p k n", p=128), writes=[WLA])
            wr, WR = load_w(w_in[l][:, C_RR:C_RR + 512])
            wk, WK = load_w(w_in[l][:, C_RK:C_RK + 512])
            wv, WV = load_w(w_in[l][:, C_RV:C_RV + 512])
            wz, WZ = load_w(w_in[l][:, C_RZ:C_RZ + 512])
            tw = abv(S)
            TWB = [P.buf("tw") for _ in range(NB)]
            carry = afv(4)
            CARB = [P.buf("car") for _ in range(4)]
            tmps = [afv(512), afv(512)]
            TMPS = [P.buf("tmp0"), P.buf("tmp1")]
            tcnt = {"i": 0}

            def shift(bk, ci, mu_ap, om_ap, out, OUT, n):
                i = tcnt["i"] % 2
                tcnt["i"] += 1
                tmp, TMP = tmps[i], TMPS[i]
                TS(tmp[:, 1:512], psb(bk)[:, 0:511], mu_ap, None, ALU.mult, ALU.bypass, [PSB[bk], CB], [TMP])
                if n == 0:
                    P.op("dve", lambda e: e.memset(tmp[:, 0:1], 0.0), writes=[TMP])
                else:
                    TS(tmp[:, 0:1], carry[:, ci:ci + 1], mu_ap, None, ALU.mult, ALU.bypass, [CARB[ci], CB], [TMP])
                STT(out, psb(bk), om_ap, tmp, ALU.mult, ALU.add, [PSB[bk], TMP, CB], [OUT])
                CP("dve", carry[:, ci:ci + 1], psb(bk)[:, 511:512], [PSB[bk]], [CARB[ci]])

            sh = T["Lx"]
            SH = TB["Lx"]
            for n in range(NB):
                for kc in range(8):
                    MM(psb(0), wla[:, kc, :], hT[:, kc, blk(n)], kc == 0, kc == 7, [WLA, HB[n]], [PSB[0]])
                shift(0, 0, col(l, MU_WA), dcol(l, OM_WA), sh, SH, n)
                ACT(tw[0:64, blk(n)], sh[0:64, :], AF.Tanh, [SH], [TWB[n]])
                CP("act", tw[64:128, blk(n)], sh[64:128, :], [SH], [TWB[n]])

            NGR = S // 256
            RpT = abv(S)
            RPB = [P.buf("rp") for _ in range(NGR)]
            Y0T = abv(S)
            Y0B = [P.buf("y0") for _ in range(NGR)]
            bon = abv(S)
            BONB = [P.buf("bon") for _ in range(NB)]
            szb = abv(S)
            SZB = [P.buf("szb") for _ in range(NB)]
            McT = abv(NCH * 64).rearrange("p (c k) -> p c k", c=NCH)
            MCB = [P.buf("mc") for _ in range(NGR)]
            Ncs = abv(NCH * 64).rearrange("p (c k) -> p c k", c=NCH)
            NCB = [P.buf("nc") for _ in range(NGR)]
            ST = abv((NCH + 1) * 64).rearrange("p (c k) -> p c k", c=NCH + 1)
            STB = [P.buf("st") for _ in range(NCH + 1)]
            PC = afv(NCH)
            PCB = [P.buf("pc") for _ in range(NB)]
            sqb = abv(512)
            SQB = P.buf("sqb")
            PO = []
            for par in range(2):
                d_ = dict(ARt=abv(4 * 256).rearrange("p (u x) -> p u x", u=4), ARB=P.buf("ARt"))
                for nm in ("BtT0", "BtT1", "KtT0", "KtT1", "BhT", "KhT", "vTb"):
                    d_[nm] = abv(512)
                    d_[nm + "_B"] = P.buf(nm)
                for nm in ("BtT0", "BtT1", "KtT0", "KtT1"):
                    P.op("pool", (lambda e, ap=d_[nm]: e.memset(ap, 0.0)), writes=[d_[nm + "_B"]])
                PO.append(d_)
            GS = []
            for sl in range(2):
                GS.append(dict(tokm=abv(1024).rearrange("p (j k x) -> p j k x", j=2, k=4), TK=P.buf("tok"),
                               Am=[abv(512), abv(512)], AMB=[P.buf("am0"), P.buf("am1")],
                               AmT=[abv(512), abv(512)], AMTB=[P.buf("amt0"), P.buf("amt1")],
                               ArbT=abv(512), ARBT=P.buf("arbt"), AakT=abv(512), AAKT=P.buf("aakt"),
                               ArkT=abv(512), ARKT=P.buf("arkt"),
                               X=[abv(512).rearrange("p (s x) -> p s x", s=4) for _ in range(2)],
                               XB=[P.buf("x0"), P.buf("x1")], tmpM=afv(256), TMPM=P.buf("tmpM"),
                               banks=(3 * sl, 3 * sl + 1, 3 * sl + 2)))
            ytile = T["e"]
            YT_ = TB["e"]
            ybf = sqb
            YBF = SQB

            def bmask(m):
                return m.unsqueeze(1).to_broadcast([128, 4, 128])

            def prep_item(f, n):
                O_ = PO[n % 2]

                def gen(slot):
                    ba, bb = 6, 7
                    proj_fm(wr, WR, f, n, ba)
                    yield
                    shift(ba, 1, col(l, MU_R + f), dcol(l, OM_R + f), T["rp"], TB["rp"], n)
                    proj_fm(wk, WK, f, n, bb)
                    yield
                    shift(bb, 2, col(l, MU_K + f), dcol(l, OM_K + f), T["kp"], TB["kp"], n)
                    proj_fm(wv, WV, f, n, ba)
                    yield
                    shift(ba, 3, col(l, MU_V + f), dcol(l, OM_V + f), T["vp"], TB["vp"], n)
                    MM(psb(bb), waup_bf[0:64, l * 512 + f * 128:l * 512 + (f + 1) * 128], tw[0:64, blk(n)], True, True,
                       [CB, TWB[n]], [PSB[bb]])
                    yield
                    ACT(T["e"], psb(bb), AF.Sigmoid, [PSB[bb], CB], [TB["e"]], bias=col(l, W0 + f))
                    MM(psb(ba), waup_bf[64:128, l * 512 + f * 128:l * 512 + (f + 1) * 128], tw[64:128, blk(n)], True, True,
                       [CB, TWB[n]], [PSB[ba]])
                    TS(T["kk"], T["kp"], col(l, KKc + f), None, ALU.mult, ALU.bypass, [TB["kp"], CB], [TB["kk"]])
                    yield
                    ACT(T["a"], psb(ba), AF.Sigmoid, [PSB[ba], CB], [TB["a"]], bias=col(l, A0 + f))
                    ACT(sqb, T["kk"], AF.Square, [TB["kk"]], [SQB])
                    P.op("dve", lambda e: e.tensor_tensor_scan(out=T["Lp"], data0=rmask_bf, data1=T["e"], initial=0.0,
                                                               op0=ALU.mult, op1=ALU.add),
                         reads=[CB, TB["e"]], writes=[TB["Lp"]])
                    yield
                    MM(psb(bb), bd_bf, sqb, True, True, [CB, SQB], [PSB[bb]])
                    TT(T["Lx"], T["Lp"], T["e"], ALU.subtract, [TB["Lp"], TB["e"]], [TB["Lx"]], eng="pool")
                    TS(T["m"], T["a"], col(l, KAc + f), dcol(l, OM_KA + f), ALU.mult, ALU.add, [TB["a"], CB], [TB["m"]])
                    yield
                    ACT(T["sd"], psb(bb), AF.Ln, [PSB[bb]], [TB["sd"]], bias=1e-18, scale=1.0)
                    RSQ(T["rs"], T["sd"], [TB["sd"]], [TB["rs"]])
                    ACT(T["ex0"], T["Lx"], AF.Exp, [TB["Lx"]], [TB["ex0"]], scale=-c_)
                    TT(T["m"], T["kp"], T["m"], ALU.mult, [TB["kp"], TB["m"]], [TB["m"]], eng="pool")
                    yield
                    TT(T["kk"], T["kk"], T["rs"], ALU.mult, [TB["kk"], TB["rs"]], [TB["kk"]])
                    ACT(T["ex1"], T["Lp"], AF.Exp, [TB["Lp"]], [TB["ex1"]], scale=-c_)
                    yield
                    STT(O_["ARt"][:, :, 0:128], v3(T["kk"], 4), -1.0, v3(T["ex0"], 4), ALU.mult, ALU.mult,
                        [TB["kk"], TB["ex0"]], [O_["ARB"]])
                    TT(T["ka"], T["kk"], T["a"], ALU.mult, [TB["kk"], TB["a"]], [TB["ka"]], eng="pool")
                    yield
                    TT(O_["ARt"][:, :, 128:256], v3(T["rp"], 4), v3(T["ex1"], 4), ALU.mult, [TB["rp"], TB["ex1"]], [O_["ARB"]])
                    ACT(T["ex0"], T["Lp"], AF.Exp, [TB["Lp"]], [TB["ex0"]], scale=c_)
                    Lp3 = v3(T["Lp"], 4)
                    ACT(PC[:, n * 4:(n + 1) * 4], Lp3[:, :, 127], AF.Exp, [TB["Lp"]], [PCB[n]], scale=-c_)
                    TT(v3(T["Lx"], 4), Lp3[:, :, 127:128].to_broadcast([128, 4, 128]), Lp3, ALU.subtract,
                       [TB["Lp"]], [TB["Lx"]])
                    yield
                    ACT(T["ex1"], T["Lx"], AF.Exp, [TB["Lx"]], [TB["ex1"]], scale=-c_)
                    for hp_ in range(2):
                        rw_ = slice(hp_ * 64, (hp_ + 1) * 64)
                        TT(O_["BtT%d" % hp_][rw_, :], T["ka"][rw_, :], T["ex0"][rw_, :], ALU.mult, [TB["ka"], TB["ex0"]],
                           [O_["BtT%d_B" % hp_]], eng="pool")
                    yield
                    for hp_ in range(2):
                        rw_ = slice(hp_ * 64, (hp_ + 1) * 64)
                        TT(O_["KtT%d" % hp_][rw_, :], T["m"][rw_, :], T["ex0"][rw_, :], ALU.mult, [TB["m"], TB["ex0"]],
                           [O_["KtT%d_B" % hp_]], eng="dve")
                    CP("act", O_["vTb"], T["vp"], [TB["vp"]], [O_["vTb_B"]])
                    yield
                    TT(O_["BhT"], T["ka"], T["ex1"], ALU.mult, [TB["ka"], TB["ex1"]], [O_["BhT_B"]], eng="pool")
                    TT(O_["KhT"], T["m"], T["ex1"], ALU.mult, [TB["m"], TB["ex1"]], [O_["KhT_B"]])
                    yield
                    STT(sqb, T["rp"], col(l, RKc + f), T["m"], ALU.mult, ALU.mult, [TB["rp"], TB["m"], CB], [SQB])
                    proj_fm(wz, WZ, f, n, bb)
                    yield
                    MM(psb(ba), bd_bf, sqb, True, True, [CB, SQB], [PSB[ba]])
                    ACT(szb[:, blk(n)], psb(bb), AF.Silu, [PSB[bb]], [SZB[n]])
                    yield
                    TT(bon[:, blk(n)], psb(ba), T["vp"], ALU.mult, [PSB[ba], TB["vp"]], [BONB[n]])
                return gen

            def grp_item(f, g):
                n = g // 2
                gi = g % 2
                O_ = PO[n % 2]
                G_ = GS[g % 2]
                ba, bb, bc = G_["banks"]
                tokm, TK = G_["tokm"], G_["TK"]
                Am, AMB, AmT, AMTB = G_["Am"], G_["AMB"], G_["AmT"], G_["AMTB"]
                X, XB_ = G_["X"], G_["XB"]

                def sets():
                    for j in range(2):
                        for hp in range(2):
                            yield j, hp, j * 2 + hp, 2 * gi + j

                def gen(slot):
                    pb = psb_bf(ba)
                    for j in range(2):
                        ub = 2 * gi + j
                        tk = slice(ub * 128, (ub + 1) * 128)
                        srcs = ((O_["ARt"][:, ub, 0:128], O_["ARB"]), (O_["BhT"][:, tk], O_["BhT_B"]),
                                (O_["KhT"][:, tk], O_["KhT_B"]), (O_["vTb"][:, tk], O_["vTb_B"]))
                        for kind, (sap, sbuf_) in enumerate(srcs):
                            TRN(pb[:, (j * 4 + kind) * 128:(j * 4 + kind + 1) * 128], sap, [sbuf_, CB], [PSB[ba]])
                    yield
                    CPRR(tokm.rearrange("p j k x -> p (j k x)"), pb, [PSB[ba]], [TK])
                    kinds = (
                        (bb, lambda j, hp, ub, tk: (O_["ARt"][:, ub, 0:128], O_["BtT%d" % hp][:, tk]), "B", Am[0], AMB[0], mstr),
                        (bc, lambda j, hp, ub, tk: (O_["BtT%d" % hp][:, tk], O_["ARt"][:, ub, 0:128]), "B", AmT[0], AMTB[0], mstrT),
                        (ba, lambda j, hp, ub, tk: (O_["BtT%d" % hp][:, tk], O_["ARt"][:, ub, 128:256]), "B", G_["ArbT"], G_["ARBT"], minclT),
                        (bb, lambda j, hp, ub, tk: (O_["KtT%d" % hp][:, tk], O_["ARt"][:, ub, 0:128]), "K", G_["AakT"], G_["AAKT"], mstrT),
                        (bc, lambda j, hp, ub, tk: (O_["KtT%d" % hp][:, tk], O_["ARt"][:, ub, 128:256]), "K", G_["ArkT"], G_["ARKT"], minclT),
                    )
                    pend = None
                    for (bk, opf, which, dst, DST, msk) in kinds:
                        for j, hp, s, ub in sets():
                            tk = slice(ub * 128, (ub + 1) * 128)
                            lhs, rhs = opf(j, hp, ub, tk)
                            wbuf = O_[("BtT%d_B" if which == "B" else "KtT%d_B") % hp]
                            MM(psb(bk)[:, s * 128:(s + 1) * 128], lhs, rhs, True, True, [O_["ARB"], wbuf], [PSB[bk]])
                        if pend is not None:
                            pend()
                        pend = (lambda bk=bk, dst=dst, DST=DST, msk=msk:
                                TT(v3(dst, 4), v3(psb(bk), 4), bmask(msk), ALU.mult, [PSB[bk], CB], [DST]))
                        yield
                    pend()
                    for j, hp, s, ub in sets():
                        MM(psb(ba)[:, s * 64:(s + 1) * 64], G_["AakT"][:, s * 128:(s + 1) * 128],
                           tokm[:, j, 3, hp * 64:(hp + 1) * 64], True, True, [G_["AAKT"], TK], [PSB[ba]])
                    for j in range(2):
                        CP("act", X[0][:, 2 * j:2 * j + 2, 0:64], tokm[:, j, 0, :].rearrange("p (h k) -> p h k", h=2),
                           [TK], [XB_[0]])
                    yield
                    CP("dve", X[0][:, :, 64:128], psb(ba)[:, 0:256].rearrange("p (s v) -> p s v", s=4), [PSB[ba]], [XB_[0]])
                    yield
                    cur = 0
                    for jj in range(7):
                        nxt = 1 - cur
                        xi, xo = jj % 2, (jj + 1) % 2
                        for s in range(4):
                            sc_ = slice(s * 128, (s + 1) * 128)
                            MM(psb(ba)[:, sc_], ident_bf, X[xi][:, s, :], True, False, [CB, XB_[xi]], [PSB[ba]])
                            MM(psb(ba)[:, sc_], AmT[cur][:, sc_], X[xi][:, s, :], False, True, [AMTB[cur], XB_[xi]], [PSB[ba]])
                        if jj < 6:
                            for s in range(4):
                                sc_ = slice(s * 128, (s + 1) * 128)
                                MM(psb(bb)[:, sc_], Am[cur][:, sc_], AmT[cur][:, sc_], True, True, [AMB[cur], AMTB[cur]], [PSB[bb]])
                            if jj < 5:
                                for s in range(4):
                                    sc_ = slice(s * 128, (s + 1) * 128)
                                    MM(psb(bc)[:, sc_], AmT[cur][:, sc_], Am[cur][:, sc_], True, True, [AMB[cur], AMTB[cur]], [PSB[bc]])
                        yield
                        CP("act", X[xo].rearrange("p s x -> p (s x)"), psb(ba), [PSB[ba]], [XB_[xo]])
                        if jj < 6:
                            CP("dve", AmT[nxt], psb(bb), [PSB[bb]], [AMTB[nxt]])
                            if jj < 5:
                                CPRR(Am[nxt], psb(bc), [PSB[bc]], [AMB[nxt]])
                            cur = nxt
                        yield
                    Xf, XFB = X[1], XB_[1]
                    ArbT, ARBT, ArkT, ARKT = G_["ArbT"], G_["ARBT"], G_["ArkT"], G_["ARKT"]
                    for j, hp, s, ub in sets():
                        rows = slice(hp * 64, (hp + 1) * 64)
                        sc_ = slice(s * 128, (s + 1) * 128)
                        jc = slice(j * 128, (j + 1) * 128)
                        MM(psb(ba)[rows, jc], Xf[:, s, 0:64], ArbT[:, sc_], True, True, [XFB, ARBT], [PSB[ba]])
                        MM(psb(bb)[rows, jc], Xf[:, s, 64:128], ArbT[:, sc_], True, False, [XFB, ARBT], [PSB[bb]])
                        MM(psb(bb)[rows, jc], tokm[:, j, 3, hp * 64:(hp + 1) * 64], ArkT[:, sc_], False, True,
                           [TK, ARKT], [PSB[bb]])
                    yield
                    gt = slice(g * 256, (g + 1) * 256)
                    TT(v3(RpT[:, gt], 2), v3(psb(ba)[:, 0:256], 2), O_["ARt"][:, 2 * gi:2 * gi + 2, 128:256], ALU.add,
                       [PSB[ba], O_["ARB"]], [RPB[g]])
                    CP("act", Y0T[:, gt], psb(bb)[:, 0:256], [PSB[bb]], [Y0B[g]])
                    yield
                    for j, hp, s, ub in sets():
                        rows = slice(hp * 64, (hp + 1) * 64)
                        hc = slice(hp * 64, (hp + 1) * 64)
                        oc = slice(j * 64, (j + 1) * 64)
                        MM(psb(ba)[rows, oc], Xf[:, s, 0:64], tokm[:, j, 1, hc], True, True, [XFB, TK], [PSB[ba]])
                        MM(psb(bc)[rows, oc], tokm[:, j, 1, hc], Xf[:, s, 64:128], True, False, [XFB, TK], [PSB[bc]])
                        MM(psb(bc)[rows, oc], tokm[:, j, 2, hc], tokm[:, j, 3, hc], False, True, [TK], [PSB[bc]])
                    gcs = slice(g * 2, (g + 1) * 2)
                    tmpM, TMPM_ = G_["tmpM"][:, 0:128], G_["TMPM"]
                    TT(v3(tmpM, 2), id2_bf.unsqueeze(1).to_broadcast([128, 2, 64]),
                       PC[:, gcs].unsqueeze(2).to_broadcast([128, 2, 64]), ALU.mult, [CB, PCB[n]], [TMPM_], eng="pool")
                    yield
                    TT(McT[:, gcs, :], v3(psb(ba)[:, 0:128], 2), v3(tmpM, 2), ALU.add, [PSB[ba], TMPM_], [MCB[g]])
                    CP("act", Ncs[:, gcs, :], v3(psb(bc)[:, 0:128], 2), [PSB[bc]], [NCB[g]])
                return gen

            for f in range(4):
                items = []
                idx = {}
                for n in range(NB):
                    deps = []
                    if n >= 1:
                        deps.append(idx[("p", n - 1)])
                    if n >= 2:
                        deps += [idx[("g", 2 * n - 4)], idx[("g", 2 * n - 3)]]
                    idx[("p", n)] = len(items)
                    items.append((prep_item(f, n), deps))
                    if n >= 1:
                        pass
                    for gi in range(2):
                        g = 2 * n + gi
                        deps = [idx[("p", n)]]
                        if g >= 2:
                            deps.append(idx[("g", g - 2)])
                        idx[("g", g)] = len(items)
                        items.append((grp_item(f, g), deps))
                order = []
                for n in range(NB):
                    if n == 0:
                        order.append(("p", 0))
                    if n + 1 < NB:
                        order.append(("p", n + 1))
                    order += [("g", 2 * n), ("g", 2 * n + 1)]
                remap = {}
                new_items = []
                for key in order:
                    remap[idx[key]] = len(new_items)
                    new_items.append(items[idx[key]])
                new_items = [(fn, [remap[d] for d in deps]) for fn, deps in new_items]
                run_pipe(new_items, 3)
                P.op("dve", lambda e: e.memset(ST[:, 0, :], 0.0), writes=[STB[0]])
                cprog = {"c": 0}

                def chain_item():
                    def gen(slot):
                        for c in range(NCH):
                            for hp in range(2):
                                rows = slice(hp * 64, (hp + 1) * 64)
                                bk = 1 + 2 * (c % 2) + hp
                                MM(psb(bk)[rows, 0:64], McT[rows, c, :], ST[rows, c, :], True, True, [MCB[c // 2], STB[c]], [PSB[bk]])
                            yield
                            for hp in range(2):
                                rows = slice(hp * 64, (hp + 1) * 64)
                                bk = 1 + 2 * (c % 2) + hp
                                TT(ST[rows, c + 1, :], psb(bk)[rows, 0:64], Ncs[rows, c, :], ALU.add, [PSB[bk], NCB[c // 2]], [STB[c + 1]])
                            cprog["c"] = c + 1
                            yield
                    return gen

                def out_item(n):
                    def gen(slot):
                        while cprog["c"] < 4 * n + 4:
                            yield
                        for ub in range(4):
                            for hp in range(2):
                                rows = slice(hp * 64, (hp + 1) * 64)
                                c = n * 4 + ub
                                oc = slice(ub * 128, (ub + 1) * 128)
                                tc_ = slice(n * 512 + ub * 128, n * 512 + (ub + 1) * 128)
                                MM(psb(7 - hp)[rows, oc], ST[rows, c, :], RpT[rows, tc_], True, True,
                                   [STB[c], RPB[(n * 4 + ub) // 2]], [PSB[7 - hp]])
                        yield
                        for hp in range(2):
                            rows = slice(hp * 64, (hp + 1) * 64)
                            TT(ytile[rows, :], psb(7 - hp)[rows, :], Y0T[rows, blk(n)], ALU.add,
                               [PSB[7 - hp], Y0B[2 * n], Y0B[2 * n + 1]], [YT_])
                        yield
                        CP("act", ybf, ytile, [YT_], [YBF])
                        yield
                        MM(psb(0), bd_bf, ybf, True, True, [CB, YBF], [PSB[0]])
                        yield
                        STT(ytile, psb(0), -1.0 / 64.0, ytile, ALU.mult, ALU.add, [PSB[0], YT_], [YT_])
                        yield
                        ACT(ybf, ytile, AF.Square, [YT_], [YBF])
                        yield
                        MM(psb(0), bd_bf, ybf, True, True, [CB, YBF], [PSB[0]])
                        yield
                        ACT(T["sd"], psb(0), AF.Ln, [PSB[0]], [TB["sd"]], bias=GN_EPS, scale=1.0 / 64.0)
                        RSQ(T["rs"], T["sd"], [TB["sd"]], [TB["rs"]])
                        yield
                        TT(ytile, ytile, T["rs"], ALU.mult, [YT_, TB["rs"]], [YT_])
                        yield
                        ACT(ytile, ytile, AF.Identity, [YT_, CB], [YT_], bias=col(l, LBc + f), scale=col(l, LGc + f))
                        yield
                        TT(ytile, ytile, bon[:, blk(n)], ALU.add, [YT_, BONB[n]], [YT_], eng="pool")
                        yield
                        TT(yB[:, f, blk(n)], ytile, szb[:, blk(n)], ALU.mult, [YT_, SZB[n]], [YB[1][f][n]])
                    return gen

                its = [(chain_item(), [])]
                for n in range(NB):
                    its.append((out_item(n), [len(its) - 1] if n > 0 else []))
                run_pipe(its, 2)

        for b in range(NSEQ):
            for l in range(NL):
                xsrc = xT if l == 0 else yT
                st["off"] = persistent_end
                xb = [afv(8 * 512).rearrange("p (k s) -> p k s", k=8) for _ in range(2)]
                XBB = [P.buf("xb0"), P.buf("xb1")]
                sqx = abv(8 * 512).rearrange("p (k s) -> p k s", k=8)
                SQX = P.buf("sqx")
                sdx = afv(512)
                SDX = P.buf("sdx")
                rsx = afv(512)
                RSX = P.buf("rsx")
                for n in range(NB):
                    xv, XB_ = xb[n % 2], XBB[n % 2]
                    rd = [ybuf(b, dc, n) for dc in range(8)] if l > 0 else []
                    P.dma("sp", xv, xsrc[b][:, :, blk(n)], reads=rd, writes=[XB_])
                    ACT(sqx, xv, AF.Square, [XB_], [SQX])
                    for kc in range(8):
                        MM(psb(0), ones_bf, sqx[:, kc, :], kc == 0, kc == 7, [CB, SQX], [PSB[0]])
                    ACT(sdx, psb(0), AF.Ln, [PSB[0]], [SDX], bias=NORM_EPS, scale=1.0 / D)
                    RSQ(rsx, sdx, [SDX], [RSX])
                    for kc in range(8):
                        STT(hT[:, kc, blk(n)], xv[:, kc, :], col(l, NG + kc), rsx, ALU.mult, ALU.mult,
                            [XB_, RSX, CB], [HB[n]])
                P.end_phase(scratch[:, 4:5], SCB)

                st["off"] = yac_start
                if with_rwkv:
                    rwkv(b, l)
                else:
                    for c_ in range(4):
                        for n_ in range(NB):
                            P.op('dve', (lambda e, c_=c_, n_=n_: e.memset(yBr[1][:, c_, blk(n_)], 0.0)), writes=[YB[1][c_][n_]])
                P.end_phase(scratch[:, 4:5], SCB)

                st["off"] = persistent_end
                RT = mk_rms()
                mx = afv(8 * 256).rearrange("p (k s) -> p k s", k=8)
                MX = P.buf("mx")
                P.dma("sp", mx, memT[b], writes=[MX])
                sqm = abv(8 * 256).rearrange("p (k s) -> p k s", k=8)
                SQM = P.buf("sqm")
                ACT(sqm, mx, AF.Square, [MX], [SQM])
                for kc in range(8):
                    MM(psb(0)[:, 0:256], ones_bf, sqm[:, kc, :], kc == 0, kc == 7, [CB, SQM], [PSB[0]])
                sdm = afv(256)
                SDM = P.buf("sdm")
                ACT(sdm, psb(0)[:, 0:256], AF.Ln, [PSB[0]], [SDM], bias=NORM_EPS, scale=1.0 / D)
                rsm = afv(256)
                RSM = P.buf("rsm")
                RSQ(rsm, sdm, [SDM], [RSM])
                hmT = abv(8 * 256).rearrange("p (k s) -> p k s", k=8)
                HM = P.buf("hm")
                for kc in range(8):
                    STT(hmT[:, kc, :], mx[:, kc, :], col(l, MG + kc), rsm, ALU.mult, ALU.mult, [MX, RSM, CB], [HM])
                wk, WK = load_w(w_mkv[l][:, 0:512])
                wv_, WV = load_w(w_mkv[l][:, 512:1024])
                KmT = abv(4 * 256).rearrange("p (h s) -> p h s", h=4)
                KM = P.buf("KmT")
                kraw = afv(256)
                KR = P.buf("kraw")
                for hd in range(4):
                    for kc in range(8):
                        MM(psb(1)[:, 0:256], wk[:, kc, hd * 128:(hd + 1) * 128], hmT[:, kc, :], kc == 0, kc == 7,
                           [WK, HM], [PSB[1]])
                    CP("act", kraw, psb(1)[:, 0:256], [PSB[1]], [KR])
                    rs, RS = rms_stat(RT, kraw, [KR], ones_bf, 128.0, NORM_EPS, 2, 256)
                    STT(KmT[:, hd, :], kraw, col(l, CKc), rs, ALU.mult, ALU.mult, [KR, RS, CB], [KM])
                Vm = abv(2 * 512).rearrange("p (t n) -> p t n", t=2)
                VM = P.buf("Vm")
                for mt in range(2):
                    for kc in range(8):
                        MM(psb(3), hmT[:, kc, mt * 128:(mt + 1) * 128], wv_[:, kc, :], kc == 0, kc == 7, [WV, HM], [PSB[3]])
                    CP("act", Vm[:, mt, :], psb(3), [PSB[3]], [VM])
                wq, WQ = load_w(w_in[l][:, C_CQ:C_CQ + 512])
                wz, WZ = load_w(w_in[l][:, C_CZ:C_CZ + 512])
                yC = yBr[2]
                cscale = 128.0 ** -0.5
                CS = []
                for sl in range(4):
                    CS.append(dict(qraw=afv(512), QR=P.buf("qraw"), qn=abv(512), QN=P.buf("qn"),
                                   pT=[abv(512), abv(512)], PT=[P.buf("pT0"), P.buf("pT1")],
                                   accs=afv(512), ACS=P.buf("accs"), rsum=afv(512), RSU=P.buf("rsum"),
                                   sz=afv(512), SZ=P.buf("sz"), RT=mk_rms()))

                def ca_item(hd, n):
                    def gen(slot):
                        Q = CS[slot]
                        ba, bd_ = 2 * slot, 2 * slot + 1
                        bb = ba
                        bc = ba
                        proj_fm(wq, WQ, hd, n, ba)
                        yield
                        CP("act", Q["qraw"], psb(ba), [PSB[ba]], [Q["QR"]])
                        R_ = Q["RT"]
                        ACT(R_["sq"], Q["qraw"], AF.Square, [Q["QR"]], [R_["SQ"]])
                        proj_fm(wz, WZ, hd, n, bd_)
                        yield
                        MM(psb(ba), ones_bf, R_["sq"], True, True, [CB, R_["SQ"]], [PSB[ba]])
                        ACT(Q["sz"], psb(bd_), AF.Exp, [PSB[bd_]], [Q["SZ"]], scale=-1.0)
                        yield
                        ACT(R_["sd"], psb(ba), AF.Ln, [PSB[ba]], [R_["SD"]], bias=NORM_EPS, scale=1.0 / 128.0)
                        ACT(Q["sz"], Q["sz"], AF.Ln, [Q["SZ"]], [Q["SZ"]], bias=1.0)
                        ACT(Q["sz"], Q["sz"], AF.Exp, [Q["SZ"]], [Q["SZ"]], scale=-1.0)
                        yield
                        RSQ(R_["rs"], R_["sd"], [R_["SD"]], [R_["RS"]])
                        TT(Q["sz"], psb(bd_), Q["sz"], ALU.mult, [PSB[bd_], Q["SZ"]], [Q["SZ"]])
                        STT(Q["qn"], Q["qraw"], col(l, CQc), R_["rs"], ALU.mult, ALU.mult, [Q["QR"], R_["RS"], CB], [Q["QN"]])
                        yield
                        MM(psb(ba), KmT[:, hd, 0:128], Q["qn"], True, True, [KM, Q["QN"]], [PSB[ba]])
                        MM(psb(bd_), KmT[:, hd, 128:256], Q["qn"], True, True, [KM, Q["QN"]], [PSB[bd_]])
                        yield
                        ACT(Q["pT"][0], psb(ba), AF.Exp, [PSB[ba]], [Q["PT"][0]], scale=cscale)
                        ACT(Q["pT"][1], psb(bd_), AF.Exp, [PSB[bd_]], [Q["PT"][1]], scale=cscale)
                        yield
                        for mt in range(2):
                            MM(psb(bd_), Vm[:, mt, hd * 128:(hd + 1) * 128], Q["pT"][mt], mt == 0, mt == 1,
                               [VM, Q["PT"][mt]], [PSB[bd_]])
                        for mt in range(2):
                            MM(psb(ba), ones_bf, Q["pT"][mt], mt == 0, mt == 1, [CB, Q["PT"][mt]], [PSB[ba]])
                        yield
                        RINV(Q["rsum"], psb(ba), [PSB[ba]], [Q["RSU"]])
                        yield
                        TT(Q["rsum"], Q["rsum"], Q["sz"], ALU.mult, [Q["RSU"], Q["SZ"]], [Q["RSU"]], eng="pool")
                        yield
                        TT(yC[:, hd, blk(n)], psb(bd_), Q["rsum"], ALU.mult, [PSB[bd_], Q["RSU"]], [YB[2][hd][n]])
                    return gen

                run_pipe([ca_item(hd, n) for hd in range(4) for n in range(NB)], 4)
                P.end_phase(scratch[:, 4:5], SCB)

                st["off"] = persistent_end
                RCB = P.buf("ropecm")
                rope_bf = abv(2 * S)
                P.dma("pool", rope_bf, roped, writes=[RCB])
                ropeC = rope_bf[:, 0:S]
                ropeS = rope_bf[:, S:2 * S]
                cmask_bf = abv(4 * 512)
                P.dma("pool", cmask_bf, cmaskd, writes=[RCB])
                qz = [abv(S), abv(S)]
                QZB = [P.buf("qz0"), P.buf("qz1")]
                for c2 in range(2):
                    P.op("pool", (lambda e, ap=qz[c2]: e.memset(ap, 0.0)), writes=[QZB[c2]])
                qT = None
                kT = abv(S)
                QKB = [[P.buf("qk") for n in range(NB)] for _ in range(2)]
                Vtok = abv(NT * 512).rearrange("p (t n) -> p t n", t=NT)
                VTB = [P.buf("vt") for _ in range(NT)]
                wq_, WQ_ = load_w(w_in[l][:, C_DAQ:C_DAQ + 512])
                wk_, WK_ = load_w(w_in[l][:, C_DAK:C_DAK + 512])
                wv, WB_ = load_w(w_in[l][:, C_DAV:C_DAV + 512])
                wz_, WZ_ = load_w(w_in[l][:, C_DAZ:C_DAZ + 512])
                for t in range(NT):
                    bkv = 6 + (t % 2)
                    for kc in range(8):
                        MM(psb(bkv), hT[:, kc, t * 128:(t + 1) * 128], wv[:, kc, :], kc == 0, kc == 7,
                           [WB_, HB[t // 4]], [PSB[bkv]])
                    CPRR(Vtok[:, t, :], psb(bkv), [PSB[bkv]], [VTB[t]])
                QS = []
                for sl in range(2):
                    QS.append(dict(raw=afv(512), RAW=P.buf("raw"), qn=abv(512), QN=P.buf("qn"), t1=abv(512), T1=P.buf("t1"),
                                   t2=afv(512), T2=P.buf("t2"), RT=mk_rms()))

                def qk_item(h, qi, n):
                    wv2, WB2, dst, gcol = ((wq_, WQ_, qT, DQ), (wk_, WK_, kT, DK))[qi]

                    def gen(slot):
                        Q = QS[slot]
                        R_ = Q["RT"]
                        b0, b1, b2 = 3 * slot, 3 * slot + 1, 3 * slot + 2
                        proj_fm(wv2, WB2, h, n, b0)
                        yield
                        CP("act", Q["raw"], psb(b0), [PSB[b0]], [Q["RAW"]])
                        ACT(R_["sq"], Q["raw"], AF.Square, [Q["RAW"]], [R_["SQ"]])
                        yield
                        MM(psb(b1), bd_bf, R_["sq"], True, True, [CB, R_["SQ"]], [PSB[b1]])
                        yield
                        ACT(R_["sd"], psb(b1), AF.Ln, [PSB[b1]], [R_["SD"]], bias=NORM_EPS, scale=1.0 / 64.0)
                        yield
                        RSQ(R_["rs"], R_["sd"], [R_["SD"]], [R_["RS"]])
                        STT(Q["qn"], Q["raw"], col(l, gcol), R_["rs"], ALU.mult, ALU.mult, [Q["RAW"], R_["RS"], CB], [Q["QN"]])
                        yield
                        MM(psb(b2), perm_bf, Q["qn"], True, True, [CB, Q["QN"]], [PSB[b2]])
                        TT(Q["t1"], Q["qn"], ropeC[:, blk(n)], ALU.mult, [Q["QN"], RCB], [Q["T1"]], eng="pool")
                        yield
                        TT(Q["t2"], psb(b2), ropeS[:, blk(n)], ALU.mult, [PSB[b2], RCB], [Q["T2"]])
                        yield
                        if qi == 1:
                            TT(dst[:, blk(n)], Q["t1"], Q["t2"], ALU.add, [Q["T1"], Q["T2"]], [QKB[qi][n]], eng="pool")
                        else:
                            for c2 in range(2):
                                r2 = slice(c2 * 64, (c2 + 1) * 64)
                                TT(qz[c2][r2, blk(n)], Q["t1"][r2, :], Q["t2"][r2, :], ALU.add, [Q["T1"], Q["T2"], QZB[c2]],
                                   [QKB[qi][n]], eng="pool")
                    return gen

                pTs = [abv(512) for _ in range(3)]
                PTS = [P.buf("pT") for _ in range(3)]
                SBK = (0, 1, 2)
                A_ = dict(rinv=afv(512), RINV=P.buf("rinv"),
                          o0=afv(512), O0=P.buf("o0"), o1=afv(512), O1=P.buf("o1"),
                          sz=afv(512), SZ=P.buf("sz"), RT=mk_rms())
                yA = yBr[0]

                def att_item(h, n, comp, kt, par):
                    nkt = 4 * (n + 1)
                    ob = 3 + comp
                    sbk = 5 + comp
                    rows = slice(comp * 64, (comp + 1) * 64)

                    def gen(slot):
                        sb_ = SBK[slot]
                        q0 = 128 * max(0, kt - 4 * n)
                        cs_ = slice(q0, 512)
                        qs_ = slice(n * 512 + q0, (n + 1) * 512)
                        MM(psb(sb_)[:, cs_], kT[:, kt * 128:(kt + 1) * 128], qz[comp][:, qs_], True, True,
                           [QKB[1][kt // 4], QKB[0][n], QZB[comp]], [PSB[sb_]])
                        yield
                        ACT(pTs[slot][:, cs_], psb(sb_)[:, cs_], AF.Exp, [PSB[sb_]], [PTS[slot]], scale=0.125)
                        yield
                        if kt >= 4 * n:
                            j = kt - 4 * n
                            TT(pTs[slot][:, q0:q0 + 128], pTs[slot][:, q0:q0 + 128], cmask_bf[:, j * 512 + q0:j * 512 + q0 + 128],
                               ALU.mult, [PTS[slot], RCB], [PTS[slot]])
                        MM(psb(ob)[:, cs_], Vtok[:, kt, h * 128:(h + 1) * 128], pTs[slot][:, cs_], kt == 0, kt == nkt - 1,
                           [VTB[kt], PTS[slot]], [PSB[ob]])
                        MM(psb(sbk)[:, cs_], ones_bf, pTs[slot][:, cs_], kt == 0, kt == nkt - 1, [CB, PTS[slot]], [PSB[sbk]])
                        if kt < nkt - 1:
                            return
                        yield
                        RINV(A_["rinv"], psb(sbk), [PSB[sbk]], [A_["RINV"]])
                        yield
                        if comp == 0:
                            TT(A_["o0"], psb(ob), A_["rinv"], ALU.mult, [PSB[ob], A_["RINV"]], [A_["O0"]])
                            return
                        TT(A_["o1"], psb(ob), A_["rinv"], ALU.mult, [PSB[ob], A_["RINV"]], [A_["O1"]])
                        for kc in range(8):
                            MM(psb(7), wz_[:, kc, h * 128:(h + 1) * 128], hT[:, kc, blk(n)], kc == 0, kc == 7,
                               [WZ_, HB[n]], [PSB[7]])
                        yield
                        STT(A_["o0"], A_["o1"], dcol(l, NLAM), A_["o0"], ALU.mult, ALU.add, [A_["O0"], A_["O1"], CB], [A_["O0"]])
                        R_ = A_["RT"]
                        ACT(R_["sq"], A_["o0"], AF.Square, [A_["O0"]], [R_["SQ"]])
                        ACT(A_["sz"], psb(7), AF.Exp, [PSB[7]], [A_["SZ"]], scale=-1.0)
                        yield
                        ACT(A_["sz"], A_["sz"], AF.Ln, [A_["SZ"]], [A_["SZ"]], bias=1.0)
                        ACT(A_["sz"], A_["sz"], AF.Exp, [A_["SZ"]], [A_["SZ"]], scale=-1.0)
                        yield
                        TT(A_["sz"], psb(7), A_["sz"], ALU.mult, [PSB[7], A_["SZ"]], [A_["SZ"]])
                        yield
                        MM(psb(7), ones_bf, R_["sq"], True, True, [CB, R_["SQ"]], [PSB[7]])
                        yield
                        ACT(R_["sd"], psb(7), AF.Ln, [PSB[7]], [R_["SD"]], bias=NORM_EPS, scale=1.0 / 128.0)
                        RSQ(R_["rs"], R_["sd"], [R_["SD"]], [R_["RS"]])
                        yield
                        TT(R_["rs"], R_["rs"], A_["sz"], ALU.mult, [R_["RS"], A_["SZ"]], [R_["RS"]], eng="pool")
                        yield
                        STT(yA[:, h, blk(n)], A_["o0"], dcol(l, SUBS), R_["rs"], ALU.mult, ALU.mult,
                            [A_["O0"], R_["RS"], CB], [YB[0][h][n]])
                    return gen

                hn = 0
                for h in range(4):
                    run_pipe([qk_item(h, qi, n) for qi in range(2) for n in range(NB)], 2)
                    items = []
                    for n in range(NB):
                        for comp in range(2):
                            for kt in range(4 * (n + 1)):
                                items.append(att_item(h, n, comp, kt, hn % 2))
                        hn += 1
                    run_pipe(items, 3)
                P.end_phase(scratch[:, 4:5], SCB)

                st["off"] = persistent_end
                merged = abv(8 * S).rearrange("p (k s) -> p k s", k=8)
                MGB = [[P.buf("mg") for n in range(NB)] for dc in range(8)]
                wg = [[abv(8 * 128).rearrange("p (k n) -> p k n", k=8) for nb_ in range(3)] for _ in range(2)]
                WG = [[P.buf("wg") for nb_ in range(3)] for _ in range(2)]
                wbr = [[abv(4 * 128).rearrange("p (k n) -> p k n", k=4) for nb_ in range(3)] for _ in range(2)]
                WBR = [[P.buf("wbr") for nb_ in range(3)] for _ in range(2)]
                sgs = [afv(512) for _ in range(3)]
                SGS = [P.buf("sg") for _ in range(3)]
                macc = afv(512)
                MACC = P.buf("macc")
                tmpms = [afv(512), afv(512)]
                TMPMS = [P.buf("tmpm0"), P.buf("tmpm1")]
                mcnt = 0
                def load_merge_w(dc2):
                    par2 = dc2 % 2
                    for nb2 in range(3):
                        c0 = C_G + nb2 * 1024 + dc2 * 128
                        P.dma("pool", wg[par2][nb2], w_in[l][:, c0:c0 + 128].rearrange("(k p) n -> p k n", p=128),
                              writes=[WG[par2][nb2]])
                        P.dma("pool", wbr[par2][nb2],
                              w_br[l][nb2][:, dc2 * 128:(dc2 + 1) * 128].rearrange("(k p) n -> p k n", p=128),
                              writes=[WBR[par2][nb2]])

                load_merge_w(0)
                for dc in range(8):
                    par = dc % 2
                    if dc + 1 < 8:
                        load_merge_w(dc + 1)
                    for n in range(NB):
                        for nb_ in range(3):
                            pr_ = mcnt % 3
                            mcnt += 1
                            bg, bb_ = 2 * pr_, 2 * pr_ + 1
                            sg, SG = sgs[pr_], SGS[pr_]
                            tmpm, TMPM = tmpms[mcnt % 2], TMPMS[mcnt % 2]
                            for kc in range(8):
                                MM(psb(bg), wg[par][nb_][:, kc, :], hT[:, kc, blk(n)], kc == 0, kc == 7,
                                   [WG[par][nb_], HB[n]], [PSB[bg]])
                            for kc in range(4):
                                MM(psb(bb_), wbr[par][nb_][:, kc, :], yBr[nb_][:, kc, blk(n)], kc == 0, kc == 3,
                                   [WBR[par][nb_], YB[nb_][kc][n]], [PSB[bb_]])
                            ACT(sg, psb(bg), AF.Sigmoid, [PSB[bg]], [SG])
                            if nb_ == 0:
                                TT(macc, sg, psb(bb_), ALU.mult, [SG, PSB[bb_]], [MACC])
                            else:
                                TT(tmpm, sg, psb(bb_), ALU.mult, [SG, PSB[bb_]], [TMPM])
                                if nb_ == 1:
                                    TT(macc, macc, tmpm, ALU.add, [MACC, TMPM], [MACC])
                                else:
                                    TT(merged[:, dc, blk(n)], macc, tmpm, ALU.add, [MACC, TMPM], [MGB[dc][n]])
                wo = [abv(8 * 128).rearrange("p (k n) -> p k n", k=8) for _ in range(2)]
                WO = [P.buf("wo0"), P.buf("wo1")]
                xr = [afv(512) for _ in range(4)]
                XR = [P.buf("xr%d" % i_) for i_ in range(4)]
                ot = [afv(512) for _ in range(2)]
                OT = [P.buf("ot0"), P.buf("ot1")]
                its_ = [(dc, n) for dc in range(8) for n in range(NB)]

                def load_x(i_):
                    dc_, n_ = its_[i_]
                    rd_ = [ybuf(b, dc_, n_)] if l > 0 else []
                    P.dma("sp", xr[i_ % 4], xsrc[b][:, dc_, blk(n_)], reads=rd_, writes=[XR[i_ % 4]])

                P.dma("pool", wo[0], w_out[l][:, 0:128].rearrange("(k p) n -> p k n", p=128), writes=[WO[0]])
                load_x(0)
                load_x(1)
                for i_, (dc, n) in enumerate(its_):
                    par = dc % 2
                    if n == 0 and dc + 1 < 8:
                        P.dma("pool", wo[1 - par], w_out[l][:, (dc + 1) * 128:(dc + 2) * 128].rearrange("(k p) n -> p k n", p=128),
                              writes=[WO[1 - par]])
                    if i_ + 2 < len(its_):
                        load_x(i_ + 2)
                    i2 = i_ % 2
                    bo = 6 + i2
                    for kc in range(8):
                        MM(psb(bo), wo[par][:, kc, :], merged[:, kc, blk(n)], kc == 0, kc == 7,
                           [WO[par], MGB[kc][n]], [PSB[bo]])
                    TT(ot[i2], psb(bo), xr[i_ % 4], ALU.add, [PSB[bo], XR[i_ % 4]], [OT[i2]])
                    P.dma("sp", yT[b][:, dc, blk(n)], ot[i2], reads=[OT[i2]], writes=[ybuf(b, dc, n)],
                          final=(l == NL - 1))
                if dbg and b == 0 and l == 0:
                    dtmp = afv(4 * S).rearrange("p (k s) -> p k s", k=4)
                    DT = P.buf("dtmp")
                    for br, nm in enumerate(("dA", "dB", "dC")):
                        allb = [YB[br][c][n] for c in range(4) for n in range(NB)]
                        CP("dve", dtmp, yBr[br], allb, [DT])
                        P.dma("sp", dbgt[nm], dtmp, reads=[DT], final=True)
                P.end_phase(scratch[:, 4:5], SCB)
        counts = P.emit()
    return nc, counts


def _fm(a):
    T = a.shape[0]
    return np.ascontiguousarray(a.T.reshape(8, 128, T).transpose(1, 0, 2))


def _cols(v):
    v = np.asarray(v, np.float32).reshape(-1)
    return v.reshape(-1, 128).T


def _consts(S):
    P_ = 128
    ident = np.eye(P_, dtype=np.float32)
    ones = np.ones((P_, P_), np.float32)
    bd = np.zeros((P_, P_), np.float32)
    bd[:64, :64] = 1
    bd[64:, 64:] = 1
    perm = np.zeros((P_, P_), np.float32)
    for p in range(P_):
        d = p % 64
        if d < 8:
            perm[p + 8, p] = 1
        elif d < 16:
            perm[p - 8, p] = 1
    tt = np.arange(P_)
    same = np.ones((P_, P_), bool)
    strictT = (same & (tt[None, :] > tt[:, None])).astype(np.float32)
    inclT = (same & (tt[None, :] >= tt[:, None])).astype(np.float32)
    strict = strictT.T.copy()
    csq = np.concatenate([ident, ones, bd, perm, strictT, inclT, strict], axis=1)
    id2 = np.concatenate([np.eye(64, dtype=np.float32)] * 2, axis=0)
    rot = 16
    inv = (1.0 / (500000.0 ** (np.arange(0, rot, 2, dtype=np.float32) / np.float32(rot)))).astype(np.float32)
    ang = np.arange(S, dtype=np.float32)[:, None] * inv[None, :]
    cos, sin = np.cos(ang).astype(np.float32), np.sin(ang).astype(np.float32)
    C = np.ones((P_, S), np.float32)
    Sg = np.zeros((P_, S), np.float32)
    for p in range(P_):
        d = p % 64
        if d < 8:
            C[p] = cos[:, d]
            Sg[p] = -sin[:, d]
        elif d < 16:
            C[p] = cos[:, d - 8]
            Sg[p] = sin[:, d - 8]
    rope = np.concatenate([C, Sg], axis=1)
    key = np.arange(P_)[:, None]
    q = np.arange(512)[None, :]
    cmask = np.concatenate([(q >= key + 128 * j).astype(np.float32) for j in range(4)], axis=1)
    rmask = np.ones((P_, 512), np.float32)
    rmask[:, ::128] = 0
    return dict(csq=csq, id2=id2, rope=rope, cmask=cmask, rmask=rmask)


def _cp_table(inp, NL):
    out = []
    for l in range(NL):
        t = np.zeros((128, NCP), np.float32)
        t[:, NG:NG + 8] = _cols(inp["norm_g"][l])
        t[:, MG:MG + 8] = _cols(inp["mem_norm_g"][l])
        t[:, DQ] = np.tile(inp["da_q_norm"][l], 2)
        t[:, DK] = np.tile(inp["da_k_norm"][l], 2)
        t[:, DS] = inp["da_subln"][l]
        t[:64, LAMC:LAMC + 4] = np.asarray(inp["da_lambda"][l]).T
        mu = np.asarray(inp["rw_mu"][l])
        t[:, MU_R:MU_R + 4] = _cols(mu[0:512])
        t[:, MU_K:MU_K + 4] = _cols(mu[512:1024])
        t[:, MU_V:MU_V + 4] = _cols(mu[1024:1536])
        t[:, MU_WA] = mu[1536:1664]
        t[:, W0:W0 + 4] = _cols(inp["rw_w0"][l])
        t[:, A0:A0 + 4] = _cols(inp["rw_a0"][l])
        t[:, KKc:KKc + 4] = _cols(inp["rw_k_k"][l])
        t[:, KAc:KAc + 4] = _cols(inp["rw_k_a"][l])
        t[:, RKc:RKc + 4] = _cols(np.asarray(inp["rw_r_k"][l]).reshape(-1))
        t[:, LGc:LGc + 4] = _cols(inp["rw_ln_g"][l])
        t[:, LBc:LBc + 4] = _cols(inp["rw_ln_b"][l])
        t[:, CQc] = inp["ca_q_norm"][l]
        t[:, CKc] = inp["ca_k_norm"][l]
        out.append(t)
    return np.ascontiguousarray(np.concatenate(out, axis=1))


def prep_inputs(inp, S, NSEQ, NL, ncores):
    inp = {k: np.asarray(v, dtype=np.float32) for k, v in inp.items()}
    cst = _consts(S)
    shared = dict(
        w_in=np.ascontiguousarray(inp["w_in"][:NL]),
        w_mkv=np.ascontiguousarray(inp["w_mem_kv"][:NL]),
        w_br=np.ascontiguousarray(inp["w_branch"][:NL]),
        w_out=np.ascontiguousarray(inp["w_out"][:NL]),
        wa_up=np.ascontiguousarray(np.concatenate([inp["rw_w_up"][:NL], inp["rw_a_up"][:NL]], axis=1)),
        cp=_cp_table(inp, NL),
        **cst,
    )
    maps = []
    for c in range(ncores):
        m = dict(shared)
        m["xT"] = np.stack([_fm(inp["x"][c * NSEQ + j][:S]) for j in range(NSEQ)])
        m["memT"] = np.stack([_fm(inp["mem"][c * NSEQ + j]) for j in range(NSEQ)])
        maps.append(m)
    return maps


def unprep_output(res, S, NSEQ, ncores):
    outs = []
    for c in range(ncores):
        y = res[c]["yT"]
        for j in range(NSEQ):
            outs.append(y[j].transpose(1, 0, 2).reshape(1024, S).T)
    return np.ascontiguousarray(np.stack(outs)).astype(np.float32)


_CACHE = {}


def kernel(**inputs):
    S, NSEQ, NL, ncores = 2048, 2, 2, 8
    if "nc" not in _CACHE:
        _CACHE["nc"] = build(S, NSEQ, NL)[0]
    nc = _CACHE["nc"]
    maps = prep_inputs(inputs, S, NSEQ, NL, ncores)
    res = run_bass_kernel_spmd(nc, maps, core_ids=list(range(ncores)))
    return unprep_output(res.results, S, NSEQ, ncores)
```

```python
import math
from contextlib import ExitStack
import numpy as np
import concourse.bass as bass
import concourse.mybir as mybir
from concourse.bass_utils import run_bass_kernel_spmd

F32 = mybir.dt.float32
BF16 = mybir.dt.bfloat16
AF = mybir.ActivationFunctionType
ALU = mybir.AluOpType

ENGS = ("pe", "act", "dve", "pool", "sp")
ATTACH_WAIT = True


class Buf:
    __slots__ = ("name", "last_w", "readers", "excl")

    def __init__(self, name="", last_w=None):
        self.name = name
        self.last_w = last_w
        self.readers = []
        self.excl = False


class Op:
    __slots__ = ("eng", "fn", "deps", "needed", "val", "is_dma", "sem", "idx", "dmak")

    def __init__(self, eng, fn, is_dma=False):
        self.eng = eng
        self.fn = fn
        self.deps = []
        self.needed = False
        self.val = None
        self.is_dma = is_dma
        self.sem = None
        self.idx = None
        self.dmak = None


class Prog:
    def __init__(self, nc, same_engine_sync=True, n_dma_sems=16):
        self.nc = nc
        self.ops = []
        self.same_engine_sync = same_engine_sync
        self.n_dma_sems = n_dma_sems
        self.final_waits = []
        self.fence = None
        self.phase_bufs = []

    def buf(self, name="", local=True):
        b = Buf(name, self.fence if local else None)
        if local:
            self.phase_bufs.append(b)
        return b

    def op(self, eng, fn, reads=(), writes=(), is_dma=False):
        o = Op(eng, fn, is_dma)
        o.idx = len(self.ops)
        deps = []
        for b in reads:
            if b.last_w is not None:
                deps.append(b.last_w)
            if b.excl:
                deps.extend(r for r in b.readers if r.eng != eng)
        for b in writes:
            if b.last_w is not None:
                deps.append(b.last_w)
            deps.extend(b.readers)
        seen = set()
        for d in deps:
            if d is o or id(d) in seen:
                continue
            seen.add(id(d))
            if (not is_dma) and (not d.is_dma) and d.eng == eng:
                if eng == "pe" or not self.same_engine_sync:
                    continue
            o.deps.append(d)
        for b in writes:
            b.last_w = o
            b.readers = []
        for b in reads:
            if b not in writes:
                b.readers.append(o)
        self.ops.append(o)
        return o

    def dma(self, q, out_ap, in_ap, reads=(), writes=(), final=False):
        def fn(e):
            return e.dma_start(out=out_ap, in_=in_ap)
        o = self.op(q, fn, reads, writes, is_dma=True)
        if final:
            self.final_waits.append(o)
        return o

    def end_phase(self, scratch_ap, scratch_buf):
        bufs = self.phase_bufs + [scratch_buf]
        self.fence = self.op("dve", lambda e: e.memset(scratch_ap, 0.0), reads=(), writes=bufs)
        self.phase_bufs = []

    def emit(self):
        nc = self.nc
        ops = self.ops
        for o in ops:
            for d in o.deps:
                d.needed = True
        cnt = {e: 0 for e in ENGS}
        dcount = {e: 0 for e in ENGS}
        for o in ops:
            if o.is_dma:
                o.dmak = dcount[o.eng]
                dcount[o.eng] += 1
            elif o.needed:
                cnt[o.eng] += 1
                o.val = cnt[o.eng]
        with ExitStack() as es:
            csem = {e: es.enter_context(nc.semaphore("cs_" + e)) for e in ENGS}
            dsem = {}
            for e in ENGS:
                if dcount[e] > 0:
                    dsem[e] = [es.enter_context(nc.semaphore("ds_%s_%d" % (e, i)))
                               for i in range(min(self.n_dma_sems, dcount[e]))]
            for o in ops:
                if o.is_dma:
                    pool = dsem[o.eng]
                    o.sem = pool[o.dmak % len(pool)]
                    o.val = 16 * (o.dmak // len(pool) + 1)
            per_eng = {e: [o for o in ops if o.eng == e] for e in ENGS}
            seen = {e: {} for e in ENGS}
            dma_lists = {e: [o for o in per_eng[e] if o.is_dma] for e in ENGS}
            waits_of = {}
            vc_of = {}

            def semkey(d):
                return ("d", d.sem.num) if d.is_dma else ("c", d.eng)

            def semobj(d):
                return d.sem if d.is_dma else csem[d.eng]

            for o in ops:
                sn = seen[o.eng]
                deps = list(o.deps)
                if o.is_dma:
                    pool_n = len(dsem[o.eng])
                    if o.dmak >= pool_n:
                        deps.append(dma_lists[o.eng][o.dmak - pool_n])
                deps.sort(key=lambda d: -d.idx)
                need = []
                for d in deps:
                    k = semkey(d)
                    if sn.get(k, 0) >= d.val:
                        continue
                    need.append((semobj(d), d.val))
                    sn[k] = d.val
                    for k2, v2 in vc_of.get(d.idx, {}).items():
                        if sn.get(k2, 0) < v2:
                            sn[k2] = v2
                best = {}
                for s_, v_ in need:
                    if best.get(s_.num, (None, 0))[1] < v_:
                        best[s_.num] = (s_, v_)
                waits_of[o.idx] = list(best.values())
                if o.needed or o.is_dma:
                    vc_of[o.idx] = dict(sn)
            block = es.enter_context(nc.Block())

            def run(ename, eng):
                seen_c = {e: 0 for e in ENGS}
                seen_d = {}
                my = per_eng[ename]
                mydmas = [o for o in my if o.is_dma]
                for o in my:
                    waits = list(waits_of[o.idx])
                    attach = None
                    if ATTACH_WAIT and waits and not o.is_dma:
                        attach = waits.pop()
                    for sem_, val_ in waits:
                        eng.wait_ge(sem_, val_)
                    if o.is_dma:
                        ins = o.fn(eng)
                        ins.then_inc(o.sem, 16)
                    else:
                        ins = o.fn(eng)
                        if attach is not None:
                            ins._wait_ge(attach[0], attach[1])
                        if o.needed:
                            ins.then_inc(csem[ename], 1)
                for o in self.final_waits:
                    if o.eng == ename:
                        eng.wait_ge(o.sem, o.val)

            if per_eng["sp"]:
                @block.sync
                def _(e):
                    run("sp", e)
            if per_eng["pool"]:
                @block.gpsimd
                def _(e):
                    run("pool", e)
            if per_eng["act"]:
                @block.scalar
                def _(e):
                    run("act", e)
            if per_eng["dve"]:
                @block.vector
                def _(e):
                    run("dve", e)
            if per_eng["pe"]:
                @block.tensor
                def _(e):
                    run("pe", e)
        return {e: len(per_eng[e]) for e in ENGS}


def run_pipe(items, depth):
    norm = []
    for it in items:
        if isinstance(it, tuple):
            norm.append(it)
        else:
            norm.append((it, []))
    done = [False] * len(norm)
    nxt_i = 0
    free = list(range(depth))
    active = []
    while nxt_i < len(norm) or active:
        while nxt_i < len(norm) and free and all(done[d] for d in norm[nxt_i][1]):
            slot = free.pop(0)
            active.append((norm[nxt_i][0](slot), slot, nxt_i))
            nxt_i += 1
        assert active, "pipeline deadlock"
        nxt = []
        for g, slot, idx in active:
            try:
                next(g)
                nxt.append((g, slot, idx))
            except StopIteration:
                free.append(slot)
                free.sort()
                done[idx] = True
        active = nxt


D = 1024
KC = 8
IN_W = 8320
C_DAQ, C_DAK, C_DAV, C_DAZ = 0, 512, 1024, 1536
C_RR, C_RK, C_RV, C_RWA, C_RZ = 2048, 2560, 3072, 3584, 3712
C_CQ, C_CZ = 4224, 4736
C_G = 5248
NORM_EPS = 1e-6
GN_EPS = 64e-5
DECAY_C = math.exp(-0.5)
NG, MG, DQ, DK, DS, LAMC, MU_R, MU_K, MU_V, MU_WA = 0, 8, 16, 17, 18, 19, 23, 27, 31, 35
W0, A0, KKc, KAc, RKc, LGc, LBc, CQc, CKc = 36, 40, 44, 48, 52, 56, 60, 64, 65
NCP = 66
OM_R, OM_K, OM_V, OM_WA, OM_KA, LAM, NLAM, SUBS = 0, 4, 8, 12, 13, 17, 18, 19
NDER = 20


def build(S, NSEQ, NL, dbg=False, with_rwkv=True):
    nc = bass.Bass("TRN2", target_bir_lowering=False)
    NB = S // 512
    NT = S // 128
    NCH = S // 128

    def din(name, shape):
        return nc.dram_tensor(name, list(shape), F32, kind="ExternalInput").ap()

    xT = din("xT", [NSEQ, 128, 8, S])
    memT = din("memT", [NSEQ, 128, 8, 256])
    w_in = din("w_in", [NL, 1024, IN_W])
    w_mkv = din("w_mkv", [NL, 1024, 1024])
    w_br = din("w_br", [NL, 3, 512, 1024])
    w_out = din("w_out", [NL, 1024, 1024])
    wa_up = din("wa_up", [NL, 128, 512])
    cpd = din("cp", [128, NL * NCP])
    csq = din("csq", [128, 7 * 128])
    id2d = din("id2", [128, 64])
    roped = din("rope", [128, 2 * S])
    cmaskd = din("cmask", [128, 4 * 512])
    rmaskd = din("rmask", [128, 512])
    yT = nc.dram_tensor("yT", [NSEQ, 128, 8, S], F32, kind="ExternalOutput").ap()
    dbgt = {}
    if dbg:
        for nm in ("dA", "dB", "dC"):
            dbgt[nm] = nc.dram_tensor(nm, [128, 4, S], F32, kind="ExternalOutput").ap()

    with ExitStack() as es:
        LIMIT = 53000
        arena = es.enter_context(nc.sbuf_tensor("arena", [128, LIMIT], F32))
        ps = es.enter_context(nc.psum_tensor("ps", [128, 4096], F32))
        P = Prog(nc)
        st = {"off": 0}

        def alloc(words):
            o = st["off"]
            st["off"] += int(words)
            assert st["off"] <= LIMIT, ("SBUF overflow", st["off"])
            return o

        def fv(off, n):
            return arena[:, off:off + n]

        def bv(off, n):
            return arena[:, off:off + n // 2].bitcast(BF16)

        def afv(n):
            return fv(alloc(n), n)

        def abv(n):
            return bv(alloc(n // 2), n)

        PSB = [Buf("psb%d" % i) for i in range(8)]
        for b_ in PSB:
            b_.excl = True

        def psb(i):
            return ps[:, i * 512:(i + 1) * 512]

        def psb_bf(i):
            return ps[:, i * 512:(i + 1) * 512].bitcast(BF16)

        def MM(out, lhsT, rhs, start, stop, r, w):
            P.op("pe", lambda e: e.matmul(out, lhsT=lhsT, rhs=rhs, start=start, stop=stop), reads=r, writes=w)

        def ACT(out, in_, func, r, w, bias=0.0, scale=1.0):
            P.op("act", lambda e: e.activation(out=out, in_=in_, func=func, bias=bias, scale=scale), reads=r, writes=w)

        def TT(out, in0, in1, op, r, w, eng="dve"):
            P.op(eng, lambda e: e.tensor_tensor(out=out, in0=in0, in1=in1, op=op), reads=r, writes=w)

        def TS(out, in0, s1, s2, op0, op1, r, w, eng="dve"):
            P.op(eng, lambda e: e.tensor_scalar(out=out, in0=in0, scalar1=s1, scalar2=s2, op0=op0, op1=op1), reads=r, writes=w)

        def STT(out, in0, scalar, in1, op0, op1, r, w):
            P.op("dve", lambda e: e.scalar_tensor_tensor(out=out, in0=in0, scalar=scalar, in1=in1, op0=op0, op1=op1), reads=r, writes=w)

        def CP(eng, out, in_, r, w):
            if eng == "act":
                P.op("act", lambda e: e.copy(out=out, in_=in_), reads=r, writes=w)
            else:
                P.op(eng, lambda e: e.tensor_copy(out=out, in_=in_), reads=r, writes=w)

        def RECIP(out, in_, r, w):
            P.op("dve", lambda e: e.reciprocal(out=out, in_=in_), reads=r, writes=w)

        def RSQ(out, in_, r, w):
            ACT(out, in_, AF.Exp, r, w, scale=-0.5)

        def RINV(out, in_, r, w):
            ACT(out, in_, AF.Ln, r, w)
            ACT(out, out, AF.Exp, w, w, scale=-1.0)

        cp_rr = {"i": 0}

        def CPRR(out, in_, r, w):
            cp_rr["i"] += 1
            CP("act" if cp_rr["i"] % 2 else "dve", out, in_, r, w)

        CB = P.buf("consts", local=False)
        csq_bf = abv(7 * 128)
        P.dma("pool", csq_bf, csq, writes=[CB])
        ident_bf = csq_bf[:, 0:128]
        ones_bf = csq_bf[:, 128:256]
        bd_bf = csq_bf[:, 256:384]
        perm_bf = csq_bf[:, 384:512]
        mstrT = csq_bf[:, 512:640]
        minclT = csq_bf[:, 640:768]
        mstr = csq_bf[:, 768:896]
        ones_f = afv(128)
        P.dma("sp", ones_f, csq[:, 128:256], writes=[CB])
        id2_bf = abv(64)
        P.dma("pool", id2_bf, id2d, writes=[CB])
        rmask_bf = abv(512)
        P.dma("pool", rmask_bf, rmaskd, writes=[CB])
        NCOL = NCP + NDER
        cpt = afv(NL * NCOL)
        for l in range(NL):
            P.dma("sp", cpt[:, l * NCOL:l * NCOL + NCP], cpd[:, l * NCP:(l + 1) * NCP], writes=[CB])
        waup_bf = abv(NL * 512)
        P.dma("pool", waup_bf, wa_up.rearrange("l p n -> p l n"), writes=[CB])
        scratch = afv(8)
        SCB = P.buf("scratch", local=False)

        def col(l, c, n=1):
            return cpt[:, l * NCOL + c:l * NCOL + c + n]

        def dcol(l, c, n=1):
            return cpt[:, l * NCOL + NCP + c:l * NCOL + NCP + c + n]

        for l in range(NL):
            lam_init = 0.8 - 0.6 * math.exp(-0.3 * l)
            for (src, dst, n) in ((MU_R, OM_R, 4), (MU_K, OM_K, 4), (MU_V, OM_V, 4), (MU_WA, OM_WA, 1), (KAc, OM_KA, 4)):
                TS(dcol(l, dst, n), col(l, src, n), -1.0, 1.0, ALU.mult, ALU.add, [CB], [CB])
            pr2 = scratch[:, 0:2]
            TT(pr2[:, 0:1], col(l, LAMC), col(l, LAMC + 1), ALU.mult, [CB], [SCB])
            TT(pr2[:, 1:2], col(l, LAMC + 2), col(l, LAMC + 3), ALU.mult, [CB, SCB], [SCB])
            MM(psb(0)[:, 0:2], ones_f, pr2, True, True, [CB, SCB], [PSB[0]])
            ex2 = scratch[:, 2:4]
            ACT(ex2, psb(0)[:, 0:2], AF.Exp, [PSB[0]], [SCB])
            TS(dcol(l, LAM), ex2[:, 0:1], ex2[:, 1:2], lam_init, ALU.subtract, ALU.add, [SCB], [CB])
            TS(dcol(l, NLAM), dcol(l, LAM), -1.0, None, ALU.mult, ALU.bypass, [CB], [CB])
            TS(dcol(l, SUBS), col(l, DS), 1.0 - lam_init, None, ALU.mult, ALU.bypass, [CB], [CB])

        hT = abv(8 * S).rearrange("p (k s) -> p k s", k=8)
        HB = [P.buf("hT%d" % n, local=False) for n in range(NB)]
        yBr = [None, None, None]
        YB = [None, None, None]
        NSLOT = 4
        wslot = [abv(8 * 512).rearrange("p (k n) -> p k n", k=8) for _ in range(NSLOT)]
        yac_start = None
        for br in (1, 0, 2):
            if br == 0:
                yac_start = st["off"]
            yBr[br] = abv(4 * S).rearrange("p (k s) -> p k s", k=4)
            YB[br] = [[P.buf("y%d_%d_%d" % (br, c, n), local=False) for n in range(NB)] for c in range(4)]
        WSB = [P.buf("wslot%d" % i, local=False) for i in range(NSLOT)]
        wst = {"i": 0}

        def load_w(src2d, ncols=512):
            i = wst["i"] % NSLOT
            wst["i"] += 1
            dst = wslot[i][:, :, 0:ncols]
            P.dma("pool", dst, src2d.rearrange("(k p) n -> p k n", p=128), writes=[WSB[i]])
            return wslot[i], WSB[i]

        persistent_end = st["off"]
        DR = {}

        def ybuf(b, dc, n):
            k = (b, dc, n)
            if k not in DR:
                DR[k] = P.buf("yd", local=False)
            return DR[k]

        def blk(n):
            return slice(n * 512, (n + 1) * 512)

        def proj_fm(wv, wb, c, n, bank):
            for kc in range(8):
                MM(psb(bank), wv[:, kc, c * 128:(c + 1) * 128], hT[:, kc, blk(n)], kc == 0, kc == 7,
                   [wb, HB[n]], [PSB[bank]])

        def mk_rms():
            d_ = dict(sq=abv(512), SQ=P.buf("sq"), sd=afv(512), SD=P.buf("sd"))
            d_["rs"] = d_["sd"]
            d_["RS"] = d_["SD"]
            return d_

        def rms_stat(RT, src_ap, src_bufs, ones_m, nelem, eps, bank, n512=512):
            sq, SQ = RT["sq"][:, 0:n512], RT["SQ"]
            ACT(sq, src_ap, AF.Square, src_bufs, [SQ])
            MM(psb(bank)[:, 0:n512], ones_m, sq, True, True, [CB, SQ], [PSB[bank]])
            sd, SD = RT["sd"][:, 0:n512], RT["SD"]
            ACT(sd, psb(bank)[:, 0:n512], AF.Ln, [PSB[bank]], [SD], bias=eps, scale=1.0 / nelem)
            rs, RS = RT["rs"][:, 0:n512], RT["RS"]
            RSQ(rs, sd, [SD], [RS])
            return rs, RS

        def v3(ap, a):
            return ap.rearrange("p (a b) -> p a b", a=a)

        def TRN(out, in_, r, w):
            P.op("pe", lambda e: e.transpose(out=out, in_=in_, identity=ident_bf), reads=r, writes=w)

        def rwkv(b, l):
            c_ = DECAY_C
            yB = yBr[1]
            names = ["rp", "kp", "vp", "e", "a", "kk", "m", "ka", "Lp", "Lx", "ex0", "ex1", "sd"]
            T = {nm: afv(512) for nm in names}
            TB = {nm: P.buf(nm) for nm in names}
            T["rs"] = T["sd"]
            TB["rs"] = TB["sd"]
            wla = T["ex0"].bitcast(BF16).rearrange("p (k n) -> p k n", k=8)
            WLA = TB["ex0"]
            P.dma("pool", wla, w_in[l][:, C_RWA:C_RWA + 128].rearrange("(k p) n -> p k n", p=128), writes=[WLA])
            wr, WR = load_w(w_in[l][:, C_RR:C_RR + 512])
            wk, WK = load_w(w_in[l][:, C_RK:C_RK + 512])
            wv, WV = load_w(w_in[l][:, C_RV:C_RV + 512])
            wz, WZ = load_w(w_in[l][:, C_RZ:C_RZ + 512])
            tw = abv(S)
            TWB = [P.buf("tw") for _ in range(NB)]
            carry = afv(4)
            CARB = [P.buf("car") for _ in range(4)]
            tmps = [afv(512), afv(512)]
            TMPS = [P.buf("tmp0"), P.buf("tmp1")]
            tcnt = {"i": 0}

            def shift(bk, ci, mu_ap, om_ap, out, OUT, n):
                i = tcnt["i"] % 2
                tcnt["i"] += 1
                tmp, TMP = tmps[i], TMPS[i]
                TS(tmp[:, 1:512], psb(bk)[:, 0:511], mu_ap, None, ALU.mult, ALU.bypass, [PSB[bk], CB], [TMP])
                if n == 0:
                    P.op("dve", lambda e: e.memset(tmp[:, 0:1], 0.0), writes=[TMP])
                else:
                    TS(tmp[:, 0:1], carry[:, ci:ci + 1], mu_ap, None, ALU.mult, ALU.bypass, [CARB[ci], CB], [TMP])
                STT(out, psb(bk), om_ap, tmp, ALU.mult, ALU.add, [PSB[bk], TMP, CB], [OUT])
                CP("dve", carry[:, ci:ci + 1], psb(bk)[:, 511:512], [PSB[bk]], [CARB[ci]])

            sh = T["Lx"]
            SH = TB["Lx"]
            for n in range(NB):
                for kc in range(8):
                    MM(psb(0), wla[:, kc, :], hT[:, kc, blk(n)], kc == 0, kc == 7, [WLA, HB[n]], [PSB[0]])
                shift(0, 0, col(l, MU_WA), dcol(l, OM_WA), sh, SH, n)
                ACT(tw[0:64, blk(n)], sh[0:64, :], AF.Tanh, [SH], [TWB[n]])
                CP("act", tw[64:128, blk(n)], sh[64:128, :], [SH], [TWB[n]])

            NGR = S // 256
            RpT = abv(S)
            RPB = [P.buf("rp") for _ in range(NGR)]
            Y0T = abv(S)
            Y0B = [P.buf("y0") for _ in range(NGR)]
            bon = abv(S)
            BONB = [P.buf("bon") for _ in range(NB)]
            szb = abv(S)
            SZB = [P.buf("szb") for _ in range(NB)]
            McT = abv(NCH * 64).rearrange("p (c k) -> p c k", c=NCH)
            MCB = [P.buf("mc") for _ in range(NGR)]
            Ncs = abv(NCH * 64).rearrange("p (c k) -> p c k", c=NCH)
            NCB = [P.buf("nc") for _ in range(NGR)]
            ST = abv((NCH + 1) * 64).rearrange("p (c k) -> p c k", c=NCH + 1)
            STB = [P.buf("st") for _ in range(NCH + 1)]
            PC = afv(NCH)
            PCB = [P.buf("pc") for _ in range(NB)]
            sqb = abv(512)
            SQB = P.buf("sqb")
            PO = []
            for par in range(2):
                d_ = dict(ARt=abv(4 * 256).rearrange("p (u x) -> p u x", u=4), ARB=P.buf("ARt"))
                for nm in ("BtT0", "BtT1", "KtT0", "KtT1", "BhT", "KhT", "vTb"):
                    d_[nm] = abv(512)
                    d_[nm + "_B"] = P.buf(nm)
                for nm in ("BtT0", "BtT1", "KtT0", "KtT1"):
                    P.op("pool", (lambda e, ap=d_[nm]: e.memset(ap, 0.0)), writes=[d_[nm + "_B"]])
                PO.append(d_)
            GS = []
            for sl in range(2):
                GS.append(dict(tokm=abv(1024).rearrange("p (j k x) -> p j k x", j=2, k=4), TK=P.buf("tok"),
                               Am=[abv(512), abv(512)], AMB=[P.buf("am0"), P.buf("am1")],
                               AmT=[abv(512), abv(512)], AMTB=[P.buf("amt0"), P.buf("amt1")],
                               ArbT=abv(512), ARBT=P.buf("arbt"), AakT=abv(512), AAKT=P.buf("aakt"),
                               ArkT=abv(512), ARKT=P.buf("arkt"),
                               X=[abv(512).rearrange("p (s x) -> p s x", s=4) for _ in range(2)],
                               XB=[P.buf("x0"), P.buf("x1")], tmpM=afv(256), TMPM=P.buf("tmpM"),
                               banks=(3 * sl, 3 * sl + 1, 3 * sl + 2)))
            ytile = T["e"]
            YT_ = TB["e"]
            ybf = sqb
            YBF = SQB

            def bmask(m):
                return m.unsqueeze(1).to_broadcast([128, 4, 128])

            def prep_item(f, n):
                O_ = PO[n % 2]

                def gen(slot):
                    ba, bb = 6, 7
                    proj_fm(wr, WR, f, n, ba)
                    yield
                    shift(ba, 1, col(l, MU_R + f), dcol(l, OM_R + f), T["rp"], TB["rp"], n)
                    proj_fm(wk, WK, f, n, bb)
                    yield
                    shift(bb, 2, col(l, MU_K + f), dcol(l, OM_K + f), T["kp"], TB["kp"], n)
                    proj_fm(wv, WV, f, n, ba)
                    yield
                    shift(ba, 3, col(l, MU_V + f), dcol(l, OM_V + f), T["vp"], TB["vp"], n)
                    MM(psb(bb), waup_bf[0:64, l * 512 + f * 128:l * 512 + (f + 1) * 128], tw[0:64, blk(n)], True, True,
                       [CB, TWB[n]], [PSB[bb]])
                    yield
                    ACT(T["e"], psb(bb), AF.Sigmoid, [PSB[bb], CB], [TB["e"]], bias=col(l, W0 + f))
                    MM(psb(ba), waup_bf[64:128, l * 512 + f * 128:l * 512 + (f + 1) * 128], tw[64:128, blk(n)], True, True,
                       [CB, TWB[n]], [PSB[ba]])
                    TS(T["kk"], T["kp"], col(l, KKc + f), None, ALU.mult, ALU.bypass, [TB["kp"], CB], [TB["kk"]])
                    yield
                    ACT(T["a"], psb(ba), AF.Sigmoid, [PSB[ba], CB], [TB["a"]], bias=col(l, A0 + f))
                    ACT(sqb, T["kk"], AF.Square, [TB["kk"]], [SQB])
                    P.op("dve", lambda e: e.tensor_tensor_scan(out=T["Lp"], data0=rmask_bf, data1=T["e"], initial=0.0,
                                                               op0=ALU.mult, op1=ALU.add),
                         reads=[CB, TB["e"]], writes=[TB["Lp"]])
                    yield
                    MM(psb(bb), bd_bf, sqb, True, True, [CB, SQB], [PSB[bb]])
                    TT(T["Lx"], T["Lp"], T["e"], ALU.subtract, [TB["Lp"], TB["e"]], [TB["Lx"]], eng="pool")
                    TS(T["m"], T["a"], col(l, KAc + f), dcol(l, OM_KA + f), ALU.mult, ALU.add, [TB["a"], CB], [TB["m"]])
                    yield
                    ACT(T["sd"], psb(bb), AF.Ln, [PSB[bb]], [TB["sd"]], bias=1e-18, scale=1.0)
                    RSQ(T["rs"], T["sd"], [TB["sd"]], [TB["rs"]])
                    ACT(T["ex0"], T["Lx"], AF.Exp, [TB["Lx"]], [TB["ex0"]], scale=-c_)
                    TT(T["m"], T["kp"], T["m"], ALU.mult, [TB["kp"], TB["m"]], [TB["m"]], eng="pool")
                    yield
                    TT(T["kk"], T["kk"], T["rs"], ALU.mult, [TB["kk"], TB["rs"]], [TB["kk"]])
                    ACT(T["ex1"], T["Lp"], AF.Exp, [TB["Lp"]], [TB["ex1"]], scale=-c_)
                    yield
                    STT(O_["ARt"][:, :, 0:128], v3(T["kk"], 4), -1.0, v3(T["ex0"], 4), ALU.mult, ALU.mult,
                        [TB["kk"], TB["ex0"]], [O_["ARB"]])
                    TT(T["ka"], T["kk"], T["a"], ALU.mult, [TB["kk"], TB["a"]], [TB["ka"]], eng="pool")
                    yield
                    TT(O_["ARt"][:, :, 128:256], v3(T["rp"], 4), v3(T["ex1"], 4), ALU.mult, [TB["rp"], TB["ex1"]], [O_["ARB"]])
                    ACT(T["ex0"], T["Lp"], AF.Exp, [TB["Lp"]], [TB["ex0"]], scale=c_)
                    Lp3 = v3(T["Lp"], 4)
                    ACT(PC[:, n * 4:(n + 1) * 4], Lp3[:, :, 127], AF.Exp, [TB["Lp"]], [PCB[n]], scale=-c_)
                    TT(v3(T["Lx"], 4), Lp3[:, :, 127:128].to_broadcast([128, 4, 128]), Lp3, ALU.subtract,
                       [TB["Lp"]], [TB["Lx"]])
                    yield
                    ACT(T["ex1"], T["Lx"], AF.Exp, [TB["Lx"]], [TB["ex1"]], scale=-c_)
                    for hp_ in range(2):
                        rw_ = slice(hp_ * 64, (hp_ + 1) * 64)
                        TT(O_["BtT%d" % hp_][rw_, :], T["ka"][rw_, :], T["ex0"][rw_, :], ALU.mult, [TB["ka"], TB["ex0"]],
                           [O_["BtT%d_B" % hp_]], eng="pool")
                    yield
                    for hp_ in range(2):
                        rw_ = slice(hp_ * 64, (hp_ + 1) * 64)
                        TT(O_["KtT%d" % hp_][rw_, :], T["m"][rw_, :], T["ex0"][rw_, :], ALU.mult, [TB["m"], TB["ex0"]],
                           [O_["KtT%d_B" % hp_]], eng="dve")
                    CP("act", O_["vTb"], T["vp"], [TB["vp"]], [O_["vTb_B"]])
                    yield
                    TT(O_["BhT"], T["ka"], T["ex1"], ALU.mult, [TB["ka"], TB["ex1"]], [O_["BhT_B"]], eng="pool")
                    TT(O_["KhT"], T["m"], T["ex1"], ALU.mult, [TB["m"], TB["ex1"]], [O_["KhT_B"]])
                    yield
                    STT(sqb, T["rp"], col(l, RKc + f), T["m"], ALU.mult, ALU.mult, [TB["rp"], TB["m"], CB], [SQB])
                    proj_fm(wz, WZ, f, n, bb)
                    yield
                    MM(psb(ba), bd_bf, sqb, True, True, [CB, SQB], [PSB[ba]])
                    ACT(szb[:, blk(n)], psb(bb), AF.Silu, [PSB[bb]], [SZB[n]])
                    yield
                    TT(bon[:, blk(n)], psb(ba), T["vp"], ALU.mult, [PSB[ba], TB["vp"]], [BONB[n]])
                return gen

            def grp_item(f, g):
                n = g // 2
                gi = g % 2
                O_ = PO[n % 2]
                G_ = GS[g % 2]
                ba, bb, bc = G_["banks"]
                tokm, TK = G_["tokm"], G_["TK"]
                Am, AMB, AmT, AMTB = G_["Am"], G_["AMB"], G_["AmT"], G_["AMTB"]
                X, XB_ = G_["X"], G_["XB"]

                def sets():
                    for j in range(2):
                        for hp in range(2):
                            yield j, hp, j * 2 + hp, 2 * gi + j

                def gen(slot):
                    pb = psb_bf(ba)
                    for j in range(2):
                        ub = 2 * gi + j
                        tk = slice(ub * 128, (ub + 1) * 128)
                        srcs = ((O_["ARt"][:, ub, 0:128], O_["ARB"]), (O_["BhT"][:, tk], O_["BhT_B"]),
                                (O_["KhT"][:, tk], O_["KhT_B"]), (O_["vTb"][:, tk], O_["vTb_B"]))
                        for kind, (sap, sbuf_) in enumerate(srcs):
                            TRN(pb[:, (j * 4 + kind) * 128:(j * 4 + kind + 1) * 128], sap, [sbuf_, CB], [PSB[ba]])
                    yield
                    CPRR(tokm.rearrange("p j k x -> p (j k x)"), pb, [PSB[ba]], [TK])
                    kinds = (
                        (bb, lambda j, hp, ub, tk: (O_["ARt"][:, ub, 0:128], O_["BtT%d" % hp][:, tk]), "B", Am[0], AMB[0], mstr),
                        (bc, lambda j, hp, ub, tk: (O_["BtT%d" % hp][:, tk], O_["ARt"][:, ub, 0:128]), "B", AmT[0], AMTB[0], mstrT),
                        (ba, lambda j, hp, ub, tk: (O_["BtT%d" % hp][:, tk], O_["ARt"][:, ub, 128:256]), "B", G_["ArbT"], G_["ARBT"], minclT),
                        (bb, lambda j, hp, ub, tk: (O_["KtT%d" % hp][:, tk], O_["ARt"][:, ub, 0:128]), "K", G_["AakT"], G_["AAKT"], mstrT),
                        (bc, lambda j, hp, ub, tk: (O_["KtT%d" % hp][:, tk], O_["ARt"][:, ub, 128:256]), "K", G_["ArkT"], G_["ARKT"], minclT),
                    )
                    pend = None
                    for (bk, opf, which, dst, DST, msk) in kinds:
                        for j, hp, s, ub in sets():
                            tk = slice(ub * 128, (ub + 1) * 128)
                            lhs, rhs = opf(j, hp, ub, tk)
                            wbuf = O_[("BtT%d_B" if which == "B" else "KtT%d_B") % hp]
                            MM(psb(bk)[:, s * 128:(s + 1) * 128], lhs, rhs, True, True, [O_["ARB"], wbuf], [PSB[bk]])
                        if pend is not None:
                            pend()
                        pend = (lambda bk=bk, dst=dst, DST=DST, msk=msk:
                                TT(v3(dst, 4), v3(psb(bk), 4), bmask(msk), ALU.mult, [PSB[bk], CB], [DST]))
                        yield
                    pend()
                    for j, hp, s, ub in sets():
                        MM(psb(ba)[:, s * 64:(s + 1) * 64], G_["AakT"][:, s * 128:(s + 1) * 128],
                           tokm[:, j, 3, hp * 64:(hp + 1) * 64], True, True, [G_["AAKT"], TK], [PSB[ba]])
                    for j in range(2):
                        CP("act", X[0][:, 2 * j:2 * j + 2, 0:64], tokm[:, j, 0, :].rearrange("p (h k) -> p h k", h=2),
                           [TK], [XB_[0]])
                    yield
                    CP("dve", X[0][:, :, 64:128], psb(ba)[:, 0:256].rearrange("p (s v) -> p s v", s=4), [PSB[ba]], [XB_[0]])
                    yield
                    cur = 0
                    for jj in range(7):
                        nxt = 1 - cur
                        xi, xo = jj % 2, (jj + 1) % 2
                        for s in range(4):
                            sc_ = slice(s * 128, (s + 1) * 128)
                            MM(psb(ba)[:, sc_], ident_bf, X[xi][:, s, :], True, False, [CB, XB_[xi]], [PSB[ba]])
                            MM(psb(ba)[:, sc_], AmT[cur][:, sc_], X[xi][:, s, :], False, True, [AMTB[cur], XB_[xi]], [PSB[ba]])
                        if jj < 6:
                            for s in range(4):
                                sc_ = slice(s * 128, (s + 1) * 128)
                                MM(psb(bb)[:, sc_], Am[cur][:, sc_], AmT[cur][:, sc_], True, True, [AMB[cur], AMTB[cur]], [PSB[bb]])
                            if jj < 5:
                                for s in range(4):
                                    sc_ = slice(s * 128, (s + 1) * 128)
                                    MM(psb(bc)[:, sc_], AmT[cur][:, sc_], Am[cur][:, sc_], True, True, [AMB[cur], AMTB[cur]], [PSB[bc]])
                        yield
                        CP("act", X[xo].rearrange("p s x -> p (s x)"), psb(ba), [PSB[ba]], [XB_[xo]])
                        if jj < 6:
                            CP("dve", AmT[nxt], psb(bb), [PSB[bb]], [AMTB[nxt]])
                            if jj < 5:
                                CPRR(Am[nxt], psb(bc), [PSB[bc]], [AMB[nxt]])
                            cur = nxt
                        yield
                    Xf, XFB = X[1], XB_[1]
                    ArbT, ARBT, ArkT, ARKT = G_["ArbT"], G_["ARBT"], G_["ArkT"], G_["ARKT"]
                    for j, hp, s, ub in sets():
                        rows = slice(hp * 64, (hp + 1) * 64)
                        sc_ = slice(s * 128, (s + 1) * 128)
                        jc = slice(j * 128, (j + 1) * 128)
                        MM(psb(ba)[rows, jc], Xf[:, s, 0:64], ArbT[:, sc_], True, True, [XFB, ARBT], [PSB[ba]])
                        MM(psb(bb)[rows, jc], Xf[:, s, 64:128], ArbT[:, sc_], True, False, [XFB, ARBT], [PSB[bb]])
                        MM(psb(bb)[rows, jc], tokm[:, j, 3, hp * 64:(hp + 1) * 64], ArkT[:, sc_], False, True,
                           [TK, ARKT], [PSB[bb]])
                    yield
                    gt = slice(g * 256, (g + 1) * 256)
                    TT(v3(RpT[:, gt], 2), v3(psb(ba)[:, 0:256], 2), O_["ARt"][:, 2 * gi:2 * gi + 2, 128:256], ALU.add,
                       [PSB[ba], O_["ARB"]], [RPB[g]])
                    CP("act", Y0T[:, gt], psb(bb)[:, 0:256], [PSB[bb]], [Y0B[g]])
                    yield
                    for j, hp, s, ub in sets():
                        rows = slice(hp * 64, (hp + 1) * 64)
                        hc = slice(hp * 64, (hp + 1) * 64)
                        oc = slice(j * 64, (j + 1) * 64)
                        MM(psb(ba)[rows, oc], Xf[:, s, 0:64], tokm[:, j, 1, hc], True, True, [XFB, TK], [PSB[ba]])
                        MM(psb(bc)[rows, oc], tokm[:, j, 1, hc], Xf[:, s, 64:128], True, False, [XFB, TK], [PSB[bc]])
                        MM(psb(bc)[rows, oc], tokm[:, j, 2, hc], tokm[:, j, 3, hc], False, True, [TK], [PSB[bc]])
                    gcs = slice(g * 2, (g + 1) * 2)
                    tmpM, TMPM_ = G_["tmpM"][:, 0:128], G_["TMPM"]
                    TT(v3(tmpM, 2), id2_bf.unsqueeze(1).to_broadcast([128, 2, 64]),
                       PC[:, gcs].unsqueeze(2).to_broadcast([128, 2, 64]), ALU.mult, [CB, PCB[n]], [TMPM_], eng="pool")
                    yield
                    TT(McT[:, gcs, :], v3(psb(ba)[:, 0:128], 2), v3(tmpM, 2), ALU.add, [PSB[ba], TMPM_], [MCB[g]])
                    CP("act", Ncs[:, gcs, :], v3(psb(bc)[:, 0:128], 2), [PSB[bc]], [NCB[g]])
                return gen

            for f in range(4):
                items = []
                idx = {}
                for n in range(NB):
                    deps = []
                    if n >= 1:
                        deps.append(idx[("p", n - 1)])
                    if n >= 2:
                        deps += [idx[("g", 2 * n - 4)], idx[("g", 2 * n - 3)]]
                    idx[("p", n)] = len(items)
                    items.append((prep_item(f, n), deps))
                    if n >= 1:
                        pass
                    for gi in range(2):
                        g = 2 * n + gi
                        deps = [idx[("p", n)]]
                        if g >= 2:
                            deps.append(idx[("g", g - 2)])
                        idx[("g", g)] = len(items)
                        items.append((grp_item(f, g), deps))
                order = []
                for n in range(NB):
                    if n == 0:
                        order.append(("p", 0))
                    if n + 1 < NB:
                        order.append(("p", n + 1))
                    order += [("g", 2 * n), ("g", 2 * n + 1)]
                remap = {}
                new_items = []
                for key in order:
                    remap[idx[key]] = len(new_items)
                    new_items.append(items[idx[key]])
                new_items = [(fn, [remap[d] for d in deps]) for fn, deps in new_items]
                run_pipe(new_items, 3)
                P.op("dve", lambda e: e.memset(ST[:, 0, :], 0.0), writes=[STB[0]])
                cprog = {"c": 0}

                def chain_item():
                    def gen(slot):
                        for c in range(NCH):
                            for hp in range(2):
                                rows = slice(hp * 64, (hp + 1) * 64)
                                bk = 1 + 2 * (c % 2) + hp
                                MM(psb(bk)[rows, 0:64], McT[rows, c, :], ST[rows, c, :], True, True, [MCB[c // 2], STB[c]], [PSB[bk]])
                            yield
                            for hp in range(2):
                                rows = slice(hp * 64, (hp + 1) * 64)
                                bk = 1 + 2 * (c % 2) + hp
                                TT(ST[rows, c + 1, :], psb(bk)[rows, 0:64], Ncs[rows, c, :], ALU.add, [PSB[bk], NCB[c // 2]], [STB[c + 1]])
                            cprog["c"] = c + 1
                            yield
                    return gen

                def out_item(n):
                    def gen(slot):
                        while cprog["c"] < 4 * n + 4:
                            yield
                        for ub in range(4):
                            for hp in range(2):
                                rows = slice(hp * 64, (hp + 1) * 64)
                                c = n * 4 + ub
                                oc = slice(ub * 128, (ub + 1) * 128)
                                tc_ = slice(n * 512 + ub * 128, n * 512 + (ub + 1) * 128)
                                MM(psb(7 - hp)[rows, oc], ST[rows, c, :], RpT[rows, tc_], True, True,
                                   [STB[c], RPB[(n * 4 + ub) // 2]], [PSB[7 - hp]])
                        yield
                        for hp in range(2):
                            rows = slice(hp * 64, (hp + 1) * 64)
                            TT(ytile[rows, :], psb(7 - hp)[rows, :], Y0T[rows, blk(n)], ALU.add,
                               [PSB[7 - hp], Y0B[2 * n], Y0B[2 * n + 1]], [YT_])
                        yield
                        CP("act", ybf, ytile, [YT_], [YBF])
                        yield
                        MM(psb(0), bd_bf, ybf, True, True, [CB, YBF], [PSB[0]])
                        yield
                        STT(ytile, psb(0), -1.0 / 64.0, ytile, ALU.mult, ALU.add, [PSB[0], YT_], [YT_])
                        yield
                        ACT(ybf, ytile, AF.Square, [YT_], [YBF])
                        yield
                        MM(psb(0), bd_bf, ybf, True, True, [CB, YBF], [PSB[0]])
                        yield
                        ACT(T["sd"], psb(0), AF.Ln, [PSB[0]], [TB["sd"]], bias=GN_EPS, scale=1.0 / 64.0)
                        RSQ(T["rs"], T["sd"], [TB["sd"]], [TB["rs"]])
                        yield
                        TT(ytile, ytile, T["rs"], ALU.mult, [YT_, TB["rs"]], [YT_])
                        yield
                        ACT(ytile, ytile, AF.Identity, [YT_, CB], [YT_], bias=col(l, LBc + f), scale=col(l, LGc + f))
                        yield
                        TT(ytile, ytile, bon[:, blk(n)], ALU.add, [YT_, BONB[n]], [YT_], eng="pool")
                        yield
                        TT(yB[:, f, blk(n)], ytile, szb[:, blk(n)], ALU.mult, [YT_, SZB[n]], [YB[1][f][n]])
                    return gen

                its = [(chain_item(), [])]
                for n in range(NB):
                    its.append((out_item(n), [len(its) - 1] if n > 0 else []))
                run_pipe(its, 2)

        for b in range(NSEQ):
            for l in range(NL):
                xsrc = xT if l == 0 else yT
                st["off"] = persistent_end
                xb = [afv(8 * 512).rearrange("p (k s) -> p k s", k=8) for _ in range(2)]
                XBB = [P.buf("xb0"), P.buf("xb1")]
                sqx = abv(8 * 512).rearrange("p (k s) -> p k s", k=8)
                SQX = P.buf("sqx")
                sdx = afv(512)
                SDX = P.buf("sdx")
                rsx = afv(512)
                RSX = P.buf("rsx")
                for n in range(NB):
                    xv, XB_ = xb[n % 2], XBB[n % 2]
                    rd = [ybuf(b, dc, n) for dc in range(8)] if l > 0 else []
                    P.dma("sp", xv, xsrc[b][:, :, blk(n)], reads=rd, writes=[XB_])
                    ACT(sqx, xv, AF.Square, [XB_], [SQX])
                    for kc in range(8):
                        MM(psb(0), ones_bf, sqx[:, kc, :], kc == 0, kc == 7, [CB, SQX], [PSB[0]])
                    ACT(sdx, psb(0), AF.Ln, [PSB[0]], [SDX], bias=NORM_EPS, scale=1.0 / D)
                    RSQ(rsx, sdx, [SDX], [RSX])
                    for kc in range(8):
                        STT(hT[:, kc, blk(n)], xv[:, kc, :], col(l, NG + kc), rsx, ALU.mult, ALU.mult,
                            [XB_, RSX, CB], [HB[n]])
                P.end_phase(scratch[:, 4:5], SCB)

                st["off"] = yac_start
                if with_rwkv:
                    rwkv(b, l)
                else:
                    for c_ in range(4):
                        for n_ in range(NB):
                            P.op('dve', (lambda e, c_=c_, n_=n_: e.memset(yBr[1][:, c_, blk(n_)], 0.0)), writes=[YB[1][c_][n_]])
                P.end_phase(scratch[:, 4:5], SCB)

                st["off"] = persistent_end
                RT = mk_rms()
                mx = afv(8 * 256).rearrange("p (k s) -> p k s", k=8)
                MX = P.buf("mx")
                P.dma("sp", mx, memT[b], writes=[MX])
                sqm = abv(8 * 256).rearrange("p (k s) -> p k s", k=8)
                SQM = P.buf("sqm")
                ACT(sqm, mx, AF.Square, [MX], [SQM])
                for kc in range(8):
                    MM(psb(0)[:, 0:256], ones_bf, sqm[:, kc, :], kc == 0, kc == 7, [CB, SQM], [PSB[0]])
                sdm = afv(256)
                SDM = P.buf("sdm")
                ACT(sdm, psb(0)[:, 0:256], AF.Ln, [PSB[0]], [SDM], bias=NORM_EPS, scale=1.0 / D)
                rsm = afv(256)
                RSM = P.buf("rsm")
                RSQ(rsm, sdm, [SDM], [RSM])
                hmT = abv(8 * 256).rearrange("p (k s) -> p k s", k=8)
                HM = P.buf("hm")
                for kc in range(8):
                    STT(hmT[:, kc, :], mx[:, kc, :], col(l, MG + kc), rsm, ALU.mult, ALU.mult, [MX, RSM, CB], [HM])
                wk, WK = load_w(w_mkv[l][:, 0:512])
                wv_, WV = load_w(w_mkv[l][:, 512:1024])
                KmT = abv(4 * 256).rearrange("p (h s) -> p h s", h=4)
                KM = P.buf("KmT")
                kraw = afv(256)
                KR = P.buf("kraw")
                for hd in range(4):
                    for kc in range(8):
                        MM(psb(1)[:, 0:256], wk[:, kc, hd * 128:(hd + 1) * 128], hmT[:, kc, :], kc == 0, kc == 7,
                           [WK, HM], [PSB[1]])
                    CP("act", kraw, psb(1)[:, 0:256], [PSB[1]], [KR])
                    rs, RS = rms_stat(RT, kraw, [KR], ones_bf, 128.0, NORM_EPS, 2, 256)
                    STT(KmT[:, hd, :], kraw, col(l, CKc), rs, ALU.mult, ALU.mult, [KR, RS, CB], [KM])
                Vm = abv(2 * 512).rearrange("p (t n) -> p t n", t=2)
                VM = P.buf("Vm")
                for mt in range(2):
                    for kc in range(8):
                        MM(psb(3), hmT[:, kc, mt * 128:(mt + 1) * 128], wv_[:, kc, :], kc == 0, kc == 7, [WV, HM], [PSB[3]])
                    CP("act", Vm[:, mt, :], psb(3), [PSB[3]], [VM])
                wq, WQ = load_w(w_in[l][:, C_CQ:C_CQ + 512])
                wz, WZ = load_w(w_in[l][:, C_CZ:C_CZ + 512])
                yC = yBr[2]
                cscale = 128.0 ** -0.5
                CS = []
                for sl in range(4):
                    CS.append(dict(qraw=afv(512), QR=P.buf("qraw"), qn=abv(512), QN=P.buf("qn"),
                                   pT=[abv(512), abv(512)], PT=[P.buf("pT0"), P.buf("pT1")],
                                   accs=afv(512), ACS=P.buf("accs"), rsum=afv(512), RSU=P.buf("rsum"),
                                   sz=afv(512), SZ=P.buf("sz"), RT=mk_rms()))

                def ca_item(hd, n):
                    def gen(slot):
                        Q = CS[slot]
                        ba, bd_ = 2 * slot, 2 * slot + 1
                        bb = ba
                        bc = ba
                        proj_fm(wq, WQ, hd, n, ba)
                        yield
                        CP("act", Q["qraw"], psb(ba), [PSB[ba]], [Q["QR"]])
                        R_ = Q["RT"]
                        TT(R_["sq"], Q["qraw"], Q["qraw"], ALU.mult, [Q["QR"]], [R_["SQ"]])
                        proj_fm(wz, WZ, hd, n, bd_)
                        yield
                        MM(psb(ba), ones_bf, R_["sq"], True, True, [CB, R_["SQ"]], [PSB[ba]])
                        ACT(Q["sz"], psb(bd_), AF.Exp, [PSB[bd_]], [Q["SZ"]], scale=-1.0)
                        yield
                        ACT(R_["sd"], psb(ba), AF.Ln, [PSB[ba]], [R_["SD"]], bias=NORM_EPS, scale=1.0 / 128.0)
                        ACT(Q["sz"], Q["sz"], AF.Ln, [Q["SZ"]], [Q["SZ"]], bias=1.0)
                        ACT(Q["sz"], Q["sz"], AF.Exp, [Q["SZ"]], [Q["SZ"]], scale=-1.0)
                        yield
                        RSQ(R_["rs"], R_["sd"], [R_["SD"]], [R_["RS"]])
                        TT(Q["sz"], psb(bd_), Q["sz"], ALU.mult, [PSB[bd_], Q["SZ"]], [Q["SZ"]])
                        STT(Q["qn"], Q["qraw"], col(l, CQc), R_["rs"], ALU.mult, ALU.mult, [Q["QR"], R_["RS"], CB], [Q["QN"]])
                        yield
                        MM(psb(ba), KmT[:, hd, 0:128], Q["qn"], True, True, [KM, Q["QN"]], [PSB[ba]])
                        MM(psb(bd_), KmT[:, hd, 128:256], Q["qn"], True, True, [KM, Q["QN"]], [PSB[bd_]])
                        yield
                        ACT(Q["pT"][0], psb(ba), AF.Exp, [PSB[ba]], [Q["PT"][0]], scale=cscale)
                        ACT(Q["pT"][1], psb(bd_), AF.Exp, [PSB[bd_]], [Q["PT"][1]], scale=cscale)
                        yield
                        for mt in range(2):
                            MM(psb(bd_), Vm[:, mt, hd * 128:(hd + 1) * 128], Q["pT"][mt], mt == 0, mt == 1,
                               [VM, Q["PT"][mt]], [PSB[bd_]])
                        for mt in range(2):
                            MM(psb(ba), ones_bf, Q["pT"][mt], mt == 0, mt == 1, [CB, Q["PT"][mt]], [PSB[ba]])
                        yield
                        RINV(Q["rsum"], psb(ba), [PSB[ba]], [Q["RSU"]])
                        yield
                        TT(Q["rsum"], Q["rsum"], Q["sz"], ALU.mult, [Q["RSU"], Q["SZ"]], [Q["RSU"]], eng="pool")
                        yield
                        TT(yC[:, hd, blk(n)], psb(bd_), Q["rsum"], ALU.mult, [PSB[bd_], Q["RSU"]], [YB[2][hd][n]])
                    return gen

                run_pipe([ca_item(hd, n) for hd in range(4) for n in range(NB)], 4)
                P.end_phase(scratch[:, 4:5], SCB)

                st["off"] = persistent_end
                RCB = P.buf("ropecm")
                rope_bf = abv(2 * S)
                P.dma("pool", rope_bf, roped, writes=[RCB])
                ropeC = rope_bf[:, 0:S]
                ropeS = rope_bf[:, S:2 * S]
                cmask_bf = abv(4 * 512)
                P.dma("pool", cmask_bf, cmaskd, writes=[RCB])
                qz = [abv(S), abv(S)]
                QZB = [P.buf("qz0"), P.buf("qz1")]
                for c2 in range(2):
                    P.op("pool", (lambda e, ap=qz[c2]: e.memset(ap, 0.0)), writes=[QZB[c2]])
                qT = None
                kT = abv(S)
                QKB = [[P.buf("qk") for n in range(NB)] for _ in range(2)]
                Vtok = abv(NT * 512).rearrange("p (t n) -> p t n", t=NT)
                VTB = [P.buf("vt") for _ in range(NT)]
                wq_, WQ_ = load_w(w_in[l][:, C_DAQ:C_DAQ + 512])
                wk_, WK_ = load_w(w_in[l][:, C_DAK:C_DAK + 512])
                wv, WB_ = load_w(w_in[l][:, C_DAV:C_DAV + 512])
                wz_, WZ_ = load_w(w_in[l][:, C_DAZ:C_DAZ + 512])
                for t in range(NT):
                    bkv = 6 + (t % 2)
                    for kc in range(8):
                        MM(psb(bkv), hT[:, kc, t * 128:(t + 1) * 128], wv[:, kc, :], kc == 0, kc == 7,
                           [WB_, HB[t // 4]], [PSB[bkv]])
                    CPRR(Vtok[:, t, :], psb(bkv), [PSB[bkv]], [VTB[t]])
                QS = []
                for sl in range(2):
                    QS.append(dict(raw=afv(512), RAW=P.buf("raw"), qn=abv(512), QN=P.buf("qn"), t1=abv(512), T1=P.buf("t1"),
                                   t2=afv(512), T2=P.buf("t2"), RT=mk_rms()))

                def qk_item(h, qi, n):
                    wv2, WB2, dst, gcol = ((wq_, WQ_, qT, DQ), (wk_, WK_, kT, DK))[qi]

                    def gen(slot):
                        Q = QS[slot]
                        R_ = Q["RT"]
                        b0, b1, b2 = 3 * slot, 3 * slot + 1, 3 * slot + 2
                        proj_fm(wv2, WB2, h, n, b0)
                        yield
                        CP("act", Q["raw"], psb(b0), [PSB[b0]], [Q["RAW"]])
                        TT(R_["sq"], Q["raw"], Q["raw"], ALU.mult, [Q["RAW"]], [R_["SQ"]])
                        yield
                        MM(psb(b1), bd_bf, R_["sq"], True, True, [CB, R_["SQ"]], [PSB[b1]])
                        yield
                        ACT(R_["sd"], psb(b1), AF.Ln, [PSB[b1]], [R_["SD"]], bias=NORM_EPS, scale=1.0 / 64.0)
                        yield
                        RSQ(R_["rs"], R_["sd"], [R_["SD"]], [R_["RS"]])
                        STT(Q["qn"], Q["raw"], col(l, gcol), R_["rs"], ALU.mult, ALU.mult, [Q["RAW"], R_["RS"], CB], [Q["QN"]])
                        yield
                        MM(psb(b2), perm_bf, Q["qn"], True, True, [CB, Q["QN"]], [PSB[b2]])
                        TT(Q["t1"], Q["qn"], ropeC[:, blk(n)], ALU.mult, [Q["QN"], RCB], [Q["T1"]], eng="pool")
                        yield
                        TT(Q["t2"], psb(b2), ropeS[:, blk(n)], ALU.mult, [PSB[b2], RCB], [Q["T2"]])
                        yield
                        if qi == 1:
                            TT(dst[:, blk(n)], Q["t1"], Q["t2"], ALU.add, [Q["T1"], Q["T2"]], [QKB[qi][n]], eng="pool")
                        else:
                            for c2 in range(2):
                                r2 = slice(c2 * 64, (c2 + 1) * 64)
                                TT(qz[c2][r2, blk(n)], Q["t1"][r2, :], Q["t2"][r2, :], ALU.add, [Q["T1"], Q["T2"], QZB[c2]],
                                   [QKB[qi][n]], eng="pool")
                    return gen

                pTs = [abv(512) for _ in range(3)]
                PTS = [P.buf("pT") for _ in range(3)]
                SBK = (0, 1, 2)
                A_ = dict(rinv=afv(512), RINV=P.buf("rinv"),
                          o0=afv(512), O0=P.buf("o0"), o1=afv(512), O1=P.buf("o1"),
                          sz=afv(512), SZ=P.buf("sz"), RT=mk_rms())
                yA = yBr[0]

                def att_item(h, n, comp, kt, par):
                    nkt = 4 * (n + 1)
                    ob = 3 + comp
                    sbk = 5 + comp
                    rows = slice(comp * 64, (comp + 1) * 64)

                    def gen(slot):
                        sb_ = SBK[slot]
                        q0 = 128 * max(0, kt - 4 * n)
                        cs_ = slice(q0, 512)
                        qs_ = slice(n * 512 + q0, (n + 1) * 512)
                        MM(psb(sb_)[:, cs_], kT[:, kt * 128:(kt + 1) * 128], qz[comp][:, qs_], True, True,
                           [QKB[1][kt // 4], QKB[0][n], QZB[comp]], [PSB[sb_]])
                        yield
                        ACT(pTs[slot][:, cs_], psb(sb_)[:, cs_], AF.Exp, [PSB[sb_]], [PTS[slot]], scale=0.125)
                        yield
                        if kt >= 4 * n:
                            j = kt - 4 * n
                            TT(pTs[slot][:, q0:q0 + 128], pTs[slot][:, q0:q0 + 128], cmask_bf[:, j * 512 + q0:j * 512 + q0 + 128],
                               ALU.mult, [PTS[slot], RCB], [PTS[slot]])
                        MM(psb(ob)[:, cs_], Vtok[:, kt, h * 128:(h + 1) * 128], pTs[slot][:, cs_], kt == 0, kt == nkt - 1,
                           [VTB[kt], PTS[slot]], [PSB[ob]])
                        MM(psb(sbk)[:, cs_], ones_bf, pTs[slot][:, cs_], kt == 0, kt == nkt - 1, [CB, PTS[slot]], [PSB[sbk]])
                        if kt < nkt - 1:
                            return
                        yield
                        RINV(A_["rinv"], psb(sbk), [PSB[sbk]], [A_["RINV"]])
                        yield
                        if comp == 0:
                            TT(A_["o0"], psb(ob), A_["rinv"], ALU.mult, [PSB[ob], A_["RINV"]], [A_["O0"]])
                            return
                        TT(A_["o1"], psb(ob), A_["rinv"], ALU.mult, [PSB[ob], A_["RINV"]], [A_["O1"]])
                        for kc in range(8):
                            MM(psb(7), wz_[:, kc, h * 128:(h + 1) * 128], hT[:, kc, blk(n)], kc == 0, kc == 7,
                               [WZ_, HB[n]], [PSB[7]])
                        yield
                        STT(A_["o0"], A_["o1"], dcol(l, NLAM), A_["o0"], ALU.mult, ALU.add, [A_["O0"], A_["O1"], CB], [A_["O0"]])
                        R_ = A_["RT"]
                        TT(R_["sq"], A_["o0"], A_["o0"], ALU.mult, [A_["O0"]], [R_["SQ"]])
                        ACT(A_["sz"], psb(7), AF.Exp, [PSB[7]], [A_["SZ"]], scale=-1.0)
                        yield
                        ACT(A_["sz"], A_["sz"], AF.Ln, [A_["SZ"]], [A_["SZ"]], bias=1.0)
                        ACT(A_["sz"], A_["sz"], AF.Exp, [A_["SZ"]], [A_["SZ"]], scale=-1.0)
                        yield
                        TT(A_["sz"], psb(7), A_["sz"], ALU.mult, [PSB[7], A_["SZ"]], [A_["SZ"]])
                        yield
                        MM(psb(7), ones_bf, R_["sq"], True, True, [CB, R_["SQ"]], [PSB[7]])
                        yield
                        ACT(R_["sd"], psb(7), AF.Ln, [PSB[7]], [R_["SD"]], bias=NORM_EPS, scale=1.0 / 128.0)
                        RSQ(R_["rs"], R_["sd"], [R_["SD"]], [R_["RS"]])
                        yield
                        TT(R_["rs"], R_["rs"], A_["sz"], ALU.mult, [R_["RS"], A_["SZ"]], [R_["RS"]], eng="pool")
                        yield
                        STT(yA[:, h, blk(n)], A_["o0"], dcol(l, SUBS), R_["rs"], ALU.mult, ALU.mult,
                            [A_["O0"], R_["RS"], CB], [YB[0][h][n]])
                    return gen

                hn = 0
                for h in range(4):
                    run_pipe([qk_item(h, qi, n) for qi in range(2) for n in range(NB)], 2)
                    items = []
                    for n in range(NB):
                        for comp in range(2):
                            for kt in range(4 * (n + 1)):
                                items.append(att_item(h, n, comp, kt, hn % 2))
                        hn += 1
                    run_pipe(items, 3)
                P.end_phase(scratch[:, 4:5], SCB)

                st["off"] = persistent_end
                merged = abv(8 * S).rearrange("p (k s) -> p k s", k=8)
                MGB = [[P.buf("mg") for n in range(NB)] for dc in range(8)]
                wg = [[abv(8 * 128).rearrange("p (k n) -> p k n", k=8) for nb_ in range(3)] for _ in range(2)]
                WG = [[P.buf("wg") for nb_ in range(3)] for _ in range(2)]
                wbr = [[abv(4 * 128).rearrange("p (k n) -> p k n", k=4) for nb_ in range(3)] for _ in range(2)]
                WBR = [[P.buf("wbr") for nb_ in range(3)] for _ in range(2)]
                sgs = [afv(512) for _ in range(3)]
                SGS = [P.buf("sg") for _ in range(3)]
                macc = afv(512)
                MACC = P.buf("macc")
                tmpms = [afv(512), afv(512)]
                TMPMS = [P.buf("tmpm0"), P.buf("tmpm1")]
                mcnt = 0
                def load_merge_w(dc2):
                    par2 = dc2 % 2
                    for nb2 in range(3):
                        c0 = C_G + nb2 * 1024 + dc2 * 128
                        P.dma("pool", wg[par2][nb2], w_in[l][:, c0:c0 + 128].rearrange("(k p) n -> p k n", p=128),
                              writes=[WG[par2][nb2]])
                        P.dma("pool", wbr[par2][nb2],
                              w_br[l][nb2][:, dc2 * 128:(dc2 + 1) * 128].rearrange("(k p) n -> p k n", p=128),
                              writes=[WBR[par2][nb2]])

                load_merge_w(0)
                for dc in range(8):
                    par = dc % 2
                    if dc + 1 < 8:
                        load_merge_w(dc + 1)
                    for n in range(NB):
                        for nb_ in range(3):
                            pr_ = mcnt % 3
                            mcnt += 1
                            bg, bb_ = 2 * pr_, 2 * pr_ + 1
                            sg, SG = sgs[pr_], SGS[pr_]
                            tmpm, TMPM = tmpms[mcnt % 2], TMPMS[mcnt % 2]
                            for kc in range(8):
                                MM(psb(bg), wg[par][nb_][:, kc, :], hT[:, kc, blk(n)], kc == 0, kc == 7,
                                   [WG[par][nb_], HB[n]], [PSB[bg]])
                            for kc in range(4):
                                MM(psb(bb_), wbr[par][nb_][:, kc, :], yBr[nb_][:, kc, blk(n)], kc == 0, kc == 3,
                                   [WBR[par][nb_], YB[nb_][kc][n]], [PSB[bb_]])
                            ACT(sg, psb(bg), AF.Sigmoid, [PSB[bg]], [SG])
                            if nb_ == 0:
                                TT(macc, sg, psb(bb_), ALU.mult, [SG, PSB[bb_]], [MACC])
                            else:
                                TT(tmpm, sg, psb(bb_), ALU.mult, [SG, PSB[bb_]], [TMPM])
                                if nb_ == 1:
                                    TT(macc, macc, tmpm, ALU.add, [MACC, TMPM], [MACC])
                                else:
                                    TT(merged[:, dc, blk(n)], macc, tmpm, ALU.add, [MACC, TMPM], [MGB[dc][n]])
                wo = [abv(8 * 128).rearrange("p (k n) -> p k n", k=8) for _ in range(2)]
                WO = [P.buf("wo0"), P.buf("wo1")]
                xr = [afv(512) for _ in range(4)]
                XR = [P.buf("xr%d" % i_) for i_ in range(4)]
                ot = [afv(512) for _ in range(2)]
                OT = [P.buf("ot0"), P.buf("ot1")]
                its_ = [(dc, n) for dc in range(8) for n in range(NB)]

                def load_x(i_):
                    dc_, n_ = its_[i_]
                    rd_ = [ybuf(b, dc_, n_)] if l > 0 else []
                    P.dma("sp", xr[i_ % 4], xsrc[b][:, dc_, blk(n_)], reads=rd_, writes=[XR[i_ % 4]])

                P.dma("pool", wo[0], w_out[l][:, 0:128].rearrange("(k p) n -> p k n", p=128), writes=[WO[0]])
                load_x(0)
                load_x(1)
                for i_, (dc, n) in enumerate(its_):
                    par = dc % 2
                    if n == 0 and dc + 1 < 8:
                        P.dma("pool", wo[1 - par], w_out[l][:, (dc + 1) * 128:(dc + 2) * 128].rearrange("(k p) n -> p k n", p=128),
                              writes=[WO[1 - par]])
                    if i_ + 2 < len(its_):
                        load_x(i_ + 2)
                    i2 = i_ % 2
                    bo = 6 + i2
                    for kc in range(8):
                        MM(psb(bo), wo[par][:, kc, :], merged[:, kc, blk(n)], kc == 0, kc == 7,
                           [WO[par], MGB[kc][n]], [PSB[bo]])
                    TT(ot[i2], psb(bo), xr[i_ % 4], ALU.add, [PSB[bo], XR[i_ % 4]], [OT[i2]])
                    P.dma("sp", yT[b][:, dc, blk(n)], ot[i2], reads=[OT[i2]], writes=[ybuf(b, dc, n)],
                          final=(l == NL - 1))
                if dbg and b == 0 and l == 0:
                    dtmp = afv(4 * S).rearrange("p (k s) -> p k s", k=4)
                    DT = P.buf("dtmp")
                    for br, nm in enumerate(("dA", "dB", "dC")):
                        allb = [YB[br][c][n] for c in range(4) for n in range(NB)]
                        CP("dve", dtmp, yBr[br], allb, [DT])
                        P.dma("sp", dbgt[nm], dtmp, reads=[DT], final=True)
                P.end_phase(scratch[:, 4:5], SCB)
        counts = P.emit()
    return nc, counts


def _fm(a):
    T = a.shape[0]
    return np.ascontiguousarray(a.T.reshape(8, 128, T).transpose(1, 0, 2))


def _cols(v):
    v = np.asarray(v, np.float32).reshape(-1)
    return v.reshape(-1, 128).T


def _consts(S):
    P_ = 128
    ident = np.eye(P_, dtype=np.float32)
    ones = np.ones((P_, P_), np.float32)
    bd = np.zeros((P_, P_), np.float32)
    bd[:64, :64] = 1
    bd[64:, 64:] = 1
    perm = np.zeros((P_, P_), np.float32)
    for p in range(P_):
        d = p % 64
        if d < 8:
            perm[p + 8, p] = 1
        elif d < 16:
            perm[p - 8, p] = 1
    tt = np.arange(P_)
    same = np.ones((P_, P_), bool)
    strictT = (same & (tt[None, :] > tt[:, None])).astype(np.float32)
    inclT = (same & (tt[None, :] >= tt[:, None])).astype(np.float32)
    strict = strictT.T.copy()
    csq = np.concatenate([ident, ones, bd, perm, strictT, inclT, strict], axis=1)
    id2 = np.concatenate([np.eye(64, dtype=np.float32)] * 2, axis=0)
    rot = 16
    inv = (1.0 / (500000.0 ** (np.arange(0, rot, 2, dtype=np.float32) / np.float32(rot)))).astype(np.float32)
    ang = np.arange(S, dtype=np.float32)[:, None] * inv[None, :]
    cos, sin = np.cos(ang).astype(np.float32), np.sin(ang).astype(np.float32)
    C = np.ones((P_, S), np.float32)
    Sg = np.zeros((P_, S), np.float32)
    for p in range(P_):
        d = p % 64
        if d < 8:
            C[p] = cos[:, d]
            Sg[p] = -sin[:, d]
        elif d < 16:
            C[p] = cos[:, d - 8]
            Sg[p] = sin[:, d - 8]
    rope = np.concatenate([C, Sg], axis=1)
    key = np.arange(P_)[:, None]
    q = np.arange(512)[None, :]
    cmask = np.concatenate([(q >= key + 128 * j).astype(np.float32) for j in range(4)], axis=1)
    rmask = np.ones((P_, 512), np.float32)
    rmask[:, ::128] = 0
    return dict(csq=csq, id2=id2, rope=rope, cmask=cmask, rmask=rmask)


def _cp_table(inp, NL):
    out = []
    for l in range(NL):
        t = np.zeros((128, NCP), np.float32)
        t[:, NG:NG + 8] = _cols(inp["norm_g"][l])
        t[:, MG:MG + 8] = _cols(inp["mem_norm_g"][l])
        t[:, DQ] = np.tile(inp["da_q_norm"][l], 2)
        t[:, DK] = np.tile(inp["da_k_norm"][l], 2)
        t[:, DS] = inp["da_subln"][l]
        t[:64, LAMC:LAMC + 4] = np.asarray(inp["da_lambda"][l]).T
        mu = np.asarray(inp["rw_mu"][l])
        t[:, MU_R:MU_R + 4] = _cols(mu[0:512])
        t[:, MU_K:MU_K + 4] = _cols(mu[512:1024])
        t[:, MU_V:MU_V + 4] = _cols(mu[1024:1536])
        t[:, MU_WA] = mu[1536:1664]
        t[:, W0:W0 + 4] = _cols(inp["rw_w0"][l])
        t[:, A0:A0 + 4] = _cols(inp["rw_a0"][l])
        t[:, KKc:KKc + 4] = _cols(inp["rw_k_k"][l])
        t[:, KAc:KAc + 4] = _cols(inp["rw_k_a"][l])
        t[:, RKc:RKc + 4] = _cols(np.asarray(inp["rw_r_k"][l]).reshape(-1))
        t[:, LGc:LGc + 4] = _cols(inp["rw_ln_g"][l])
        t[:, LBc:LBc + 4] = _cols(inp["rw_ln_b"][l])
        t[:, CQc] = inp["ca_q_norm"][l]
        t[:, CKc] = inp["ca_k_norm"][l]
        out.append(t)
    return np.ascontiguousarray(np.concatenate(out, axis=1))


def prep_inputs(inp, S, NSEQ, NL, ncores):
    inp = {k: np.asarray(v, dtype=np.float32) for k, v in inp.items()}
    cst = _consts(S)
    shared = dict(
        w_in=np.ascontiguousarray(inp["w_in"][:NL]),
        w_mkv=np.ascontiguousarray(inp["w_mem_kv"][:NL]),
        w_br=np.ascontiguousarray(inp["w_branch"][:NL]),
        w_out=np.ascontiguousarray(inp["w_out"][:NL]),
        wa_up=np.ascontiguousarray(np.concatenate([inp["rw_w_up"][:NL], inp["rw_a_up"][:NL]], axis=1)),
        cp=_cp_table(inp, NL),
        **cst,
    )
    maps = []
    for c in range(ncores):
        m = dict(shared)
        m["xT"] = np.stack([_fm(inp["x"][c * NSEQ + j][:S]) for j in range(NSEQ)])
        m["memT"] = np.stack([_fm(inp["mem"][c * NSEQ + j]) for j in range(NSEQ)])
        maps.append(m)
    return maps


def unprep_output(res, S, NSEQ, ncores):
    outs = []
    for c in range(ncores):
        y = res[c]["yT"]
        for j in range(NSEQ):
            outs.append(y[j].transpose(1, 0, 2).reshape(1024, S).T)
    return np.ascontiguousarray(np.stack(outs)).astype(np.float32)


_CACHE = {}


def kernel(**inputs):
    S, NSEQ, NL, ncores = 2048, 2, 2, 8
    if "nc" not in _CACHE:
        _CACHE["nc"] = build(S, NSEQ, NL)[0]
    nc = _CACHE["nc"]
    maps = prep_inputs(inputs, S, NSEQ, NL, ncores)
    res = run_bass_kernel_spmd(nc, maps, core_ids=list(range(ncores)))
    return unprep_output(res.results, S, NSEQ, ncores)
```
